# Optimizing a Trainium2 kernel written in Bass

```python
import math
import jax, jax.numpy as jnp
from jax import lax
import numpy as np

D_MODEL = 2048
BATCH = 1
SEQ = 8192
DEPTH = 2
DEC_BATCH = 16
DEC_SEQ = 32
PAST_LEN = 4096

CHUNK = 64
N_META = 16
N_MIXERS = 2
EXPAND = 2
D_INNER = EXPAND * D_MODEL
CONV_WIDTH = 3
SSM_GROUP = 16
N_SSM_GROUPS = D_INNER // SSM_GROUP
SSM_STATE = 64
N_CONV_LAYERS = (DEPTH + 1) // 2
N_SSM_LAYERS = DEPTH // 2
RMS_EPS = 1e-6
DT_MIN = 1e-3
DT_MAX = 1e-1

kernel_name = "hybrid_shortconv_s5_stream_step"


def rmsnorm(x, w):
    x32 = x.astype(jnp.float32)
    y = x32 * lax.rsqrt(jnp.mean(x32 * x32, axis=-1, keepdims=True) + RMS_EPS)
    return (y * w.astype(jnp.float32)).astype(x.dtype)


def conv_mixer(x, conv_state, w_in, conv_w, conv_b, w_out):
    T = x.shape[1]
    proj = x @ w_in
    b_gate, c_gate, v, z = jnp.split(proj, 4, axis=-1)
    cv = c_gate * v
    padded = jnp.concatenate([conv_state.astype(cv.dtype), cv], axis=1)
    conv = conv_b
    for k in range(CONV_WIDTH):
        conv = conv + conv_w[k] * padded[:, k:k + T]
    y = b_gate * conv * jax.nn.silu(z)
    return y @ w_out, padded[:, -(CONV_WIDTH - 1):]


def s5_discretize(a_re, a_im, b_re, b_im, log_dt):
    lam = lax.complex(a_re.astype(jnp.float32), a_im.astype(jnp.float32))
    dt = jnp.exp(log_dt.astype(jnp.float32))[:, None]
    a_bar = jnp.exp(lam * dt)
    b_bar = ((a_bar - 1.0) / lam)[..., None] * lax.complex(
        b_re.astype(jnp.float32), b_im.astype(jnp.float32))
    return a_bar, b_bar


def _lin_combine(left, right):
    a_l, b_l = left
    a_r, b_r = right
    return a_r * a_l, a_r * b_l + b_r


def s5_block(h0, u_blk, a_bar, b_bar, c_mat, d_vec):
    bu = lax.complex(jnp.einsum('btgc,gpc->btgp', u_blk, jnp.real(b_bar)),
                     jnp.einsum('btgc,gpc->btgp', u_blk, jnp.imag(b_bar)))
    bu = bu.at[:, 0].add(a_bar * h0)
    a = jnp.broadcast_to(a_bar, bu.shape)
    _, hs = lax.associative_scan(_lin_combine, (a, bu), axis=1)
    y = jnp.real(jnp.einsum('gcp,btgp->btgc', c_mat, hs)) + d_vec * u_blk
    return hs[:, -1], y


def s5_mixer(x, h0, fresh, w_in, a_bar, b_bar, c_mat, d_vec, w_glu, b_glu, w_out):
    bsz, T, _ = x.shape
    proj = x @ w_in
    u, z = jnp.split(proj, 2, axis=-1)
    u32 = u.astype(jnp.float32).reshape(bsz, T, N_SSM_GROUPS, SSM_GROUP)

    def body(h, u_blk):
        return s5_block(h, u_blk, a_bar, b_bar, c_mat, d_vec)

    if fresh:
        pad = (-T) % CHUNK
        u_pad = jnp.pad(u32, ((0, 0), (pad, 0), (0, 0), (0, 0)))
        nb = (T + pad) // CHUNK
        xs = jnp.swapaxes(u_pad.reshape(bsz, nb, CHUNK, N_SSM_GROUPS, SSM_GROUP), 0, 1)
        h_last, ys = lax.scan(body, h0, xs)
        y = jnp.swapaxes(ys, 0, 1).reshape(bsz, nb * CHUNK, N_SSM_GROUPS, SSM_GROUP)[:, pad:]
    else:
        h_last, y = body(h0, u32)
    y = y.reshape(bsz, T, D_INNER).astype(x.dtype)
    g = jax.nn.gelu(y)
    y = g * jax.nn.sigmoid(g @ w_glu + b_glu)
    out = (y * jax.nn.silu(z)) @ w_out
    return out, h_last


def setup_inputs(seed: int = 0) -> dict:
    key = jax.random.key(seed)
    ks = jax.random.split(key, 24)
    f32 = jnp.float32
    nrm = lambda k, shape, s: jax.random.normal(k, shape, f32) * s
    G, P, C, E, D = N_SSM_GROUPS, SSM_STATE, SSM_GROUP, D_INNER, D_MODEL
    a_im = jnp.broadcast_to(jnp.pi * jnp.arange(P, dtype=f32), (N_SSM_LAYERS, G, P))
    return {
        'x_prompt': nrm(ks[0], (BATCH, SEQ, D), 1.0),
        'x_sample': nrm(ks[1], (DEC_BATCH, DEC_SEQ, D), 1.0),
        'cache_conv': nrm(ks[2], (N_CONV_LAYERS, DEC_BATCH, CONV_WIDTH - 1, E), 1.0),
        'state_ssm_re': nrm(ks[3], (N_SSM_LAYERS, DEC_BATCH, G, P), 0.5),
        'state_ssm_im': nrm(ks[4], (N_SSM_LAYERS, DEC_BATCH, G, P), 0.5),
        'meta_tokens': nrm(ks[5], (N_META, D), 1.0),
        'norm_w': 1.0 + nrm(ks[6], (DEPTH, D), 0.02),
        'final_norm_w': 1.0 + nrm(ks[7], (D,), 0.02),
        'conv_w_in': nrm(ks[8], (N_CONV_LAYERS, D, 4 * E), D ** -0.5),
        'conv_w': nrm(ks[9], (N_CONV_LAYERS, CONV_WIDTH, E), CONV_WIDTH ** -0.5),
        'conv_b': nrm(ks[10], (N_CONV_LAYERS, E), 0.01),
        'conv_w_out': nrm(ks[11], (N_CONV_LAYERS, E, D), E ** -0.5),
        'ssm_w_in': nrm(ks[12], (N_SSM_LAYERS, D, 2 * E), D ** -0.5),
        'ssm_a_re': -0.5 * jnp.exp(nrm(ks[13], (N_SSM_LAYERS, G, P), 0.05)),
        'ssm_a_im': a_im + nrm(ks[14], (N_SSM_LAYERS, G, P), 0.01),
        'ssm_b_re': nrm(ks[15], (N_SSM_LAYERS, G, P, C), (2 * C) ** -0.5),
        'ssm_b_im': nrm(ks[16], (N_SSM_LAYERS, G, P, C), (2 * C) ** -0.5),
        'ssm_c_re': nrm(ks[17], (N_SSM_LAYERS, G, C, P), (2 * P) ** -0.5),
        'ssm_c_im': nrm(ks[18], (N_SSM_LAYERS, G, C, P), (2 * P) ** -0.5),
        'ssm_d': nrm(ks[19], (N_SSM_LAYERS, E), 0.5),
        'ssm_log_dt': jax.random.uniform(ks[20], (N_SSM_LAYERS, G), f32,
                                         math.log(DT_MIN), math.log(DT_MAX)),
        'ssm_w_glu': nrm(ks[21], (N_SSM_LAYERS, E, E), E ** -0.5),
        'ssm_b_glu': nrm(ks[22], (N_SSM_LAYERS, E), 0.01),
        'ssm_w_out': nrm(ks[23], (N_SSM_LAYERS, E, D), E ** -0.5),
    }


def reference(x_prompt, x_sample, cache_conv, state_ssm_re, state_ssm_im, meta_tokens,
              norm_w, final_norm_w, conv_w_in, conv_w, conv_b, conv_w_out, ssm_w_in,
              ssm_a_re, ssm_a_im, ssm_b_re, ssm_b_im, ssm_c_re, ssm_c_im, ssm_d,
              ssm_log_dt, ssm_w_glu, ssm_b_glu, ssm_w_out):
    n_prompt = x_prompt.shape[0]
    meta = jnp.broadcast_to(meta_tokens.astype(x_prompt.dtype)[None], (n_prompt, N_META, D_MODEL))
    hp = jnp.concatenate([meta, x_prompt], axis=1)
    hs = x_sample
    conv_p, conv_s, ssm_p, ssm_s = [], [], [], []
    for i in range(DEPTH):
        j = i // N_MIXERS
        xp_n = rmsnorm(hp, norm_w[i])
        xs_n = rmsnorm(hs, norm_w[i])
        if i % N_MIXERS == 0:
            zero_state = jnp.zeros((n_prompt, CONV_WIDTH - 1, D_INNER), hp.dtype)
            op, st_p = conv_mixer(xp_n, zero_state, conv_w_in[j], conv_w[j], conv_b[j], conv_w_out[j])
            os_, st_s = conv_mixer(xs_n, cache_conv[j], conv_w_in[j], conv_w[j], conv_b[j], conv_w_out[j])
            conv_p.append(st_p)
            conv_s.append(st_s)
        else:
            a_bar, b_bar = s5_discretize(ssm_a_re[j], ssm_a_im[j], ssm_b_re[j], ssm_b_im[j], ssm_log_dt[j])
            c_mat = lax.complex(ssm_c_re[j].astype(jnp.float32), ssm_c_im[j].astype(jnp.float32))
            d_vec = ssm_d[j].astype(jnp.float32).reshape(N_SSM_GROUPS, SSM_GROUP)
            h0p = jnp.zeros((n_prompt, N_SSM_GROUPS, SSM_STATE), jnp.complex64)
            h0s = lax.complex(state_ssm_re[j].astype(jnp.float32), state_ssm_im[j].astype(jnp.float32))
            op, st_p = s5_mixer(xp_n, h0p, True, ssm_w_in[j], a_bar, b_bar, c_mat, d_vec,
                                ssm_w_glu[j], ssm_b_glu[j], ssm_w_out[j])
            os_, st_s = s5_mixer(xs_n, h0s, False, ssm_w_in[j], a_bar, b_bar, c_mat, d_vec,
                                 ssm_w_glu[j], ssm_b_glu[j], ssm_w_out[j])
            ssm_p.append(st_p)
            ssm_s.append(st_s)
        hp = hp + op
        hs = hs + os_
    y_prompt = rmsnorm(hp, final_norm_w)[:, N_META:]
    y_sample = rmsnorm(hs, final_norm_w)
    new_conv_prompt = jnp.stack(conv_p)
    new_conv_sample = jnp.stack(conv_s)
    h_p = jnp.stack(ssm_p)
    h_s = jnp.stack(ssm_s)
    return (y_prompt, y_sample, new_conv_prompt, new_conv_sample,
            jnp.real(h_p), jnp.imag(h_p), jnp.real(h_s), jnp.imag(h_s))
```

```python
import contextlib
import numpy as np
import concourse.bass as bass
import concourse.mybir as mybir
from concourse.bass_utils import run_bass_kernel_spmd

F32 = mybir.dt.float32
BF16 = mybir.dt.bfloat16
ACT = mybir.ActivationFunctionType
ALU = mybir.AluOpType

D = 2048
E = 4096
NCORE = 8
SEG = 1028
T = 1098
C_P0, C_A0, C_B0 = 2, 1032, 1066
RANGES = [(0, 366), (366, 732), (732, 1098)]
NTT = 9
EPS = 1e-6
MAGIC = 12582912.0
TWO_PI = float(2.0 * np.pi)
NPAIR = 128
STRICT_SAME_ENGINE = True


class DSem:
    def __init__(self, h):
        self.h = h
        self.val = 0


class K:
    def __init__(self, nc, es):
        self.nc = nc
        self.es = es
        self.eng = {"pe": nc.tensor, "act": nc.scalar, "dve": nc.vector, "pool": nc.gpsimd, "sp": nc.sync}
        self.sem = {k: es.enter_context(nc.semaphore("s_" + k)) for k in ("pe", "act", "dve", "pool")}
        self.cnt = {k: 0 for k in self.sem}
        self.seen = {k: {} for k in self.eng}
        self.nd = 0
        self.selfseen = {k_: 0 for k_ in self.sem}

    def dsem(self):
        self.nd += 1
        return DSem(self.es.enter_context(self.nc.semaphore("d%d" % self.nd)))

    def wait(self, en, toks):
        for t in toks:
            if t is None:
                continue
            if t[0] == "c":
                _, src, val = t
                if src == en:
                    continue
                if self.seen[en].get(src, 0) >= val:
                    continue
                self.eng[en].wait_ge(self.sem[src], val)
                self.seen[en][src] = val
            else:
                _, ds, val = t
                key = id(ds)
                if self.seen[en].get(key, 0) >= val:
                    continue
                self.eng[en].wait_ge(ds.h, val)
                self.seen[en][key] = val

    def wait_self(self, en, tok):
        if tok is None:
            return
        _, src, val = tok
        assert src == en
        self.eng[en].wait_ge(self.sem[src], val)

    def op(self, en, fn, waits=(), inc=True):
        self.wait(en, waits)
        if STRICT_SAME_ENGINE and en in ("act", "dve", "pool") and self.cnt[en] > self.selfseen[en]:
            self.eng[en].wait_ge(self.sem[en], self.cnt[en])
            self.selfseen[en] = self.cnt[en]
        ins = fn(self.eng[en])
        if inc:
            self.cnt[en] += 1
            ins.then_inc(self.sem[en], 1)
            return ("c", en, self.cnt[en])
        return None

    def dma(self, en, out, in_, ds, waits=(), **kw):
        self.wait(en, waits)
        ins = self.eng[en].dma_start(out=out, in_=in_, **kw)
        ds.val += 16
        ins.then_inc(ds.h, 16)
        return ("d", ds, ds.val)


def tmax(*toks):
    out = []
    for t in toks:
        if t is None:
            continue
        if isinstance(t, list):
            out.extend(t)
        else:
            out.append(t)
    return out


class Arena:
    def __init__(self, ap, nwords):
        self.ap = ap
        self.n = nwords
        self.top = 0

    def alloc(self, shape, dt=F32):
        n = 1
        for s_ in shape[1:]:
            n *= s_
        words = (n + 1) // 2 if dt == BF16 else n
        words = (words + 1) // 2 * 2
        off = self.top
        self.top += words
        assert self.top <= self.n, ("arena overflow", self.top, self.n)
        v = self.ap[:, off:off + words]
        if dt == BF16:
            v = v.bitcast(BF16)
        v = v[:, 0:n]
        if len(shape) == 3:
            v = v.rearrange("p (a b) -> p a b", a=shape[1])
        elif len(shape) == 4:
            v = v.rearrange("p (a b c) -> p a b c", a=shape[1], b=shape[2])
        return v


def emit_ssm_consts(k, ar, ps, ident, are, aim, ldt, bre, bim, waits, bank_free, nsteps_list, npow=1, dram_bft=None):
    A = lambda: ar.alloc([128, NPAIR])
    dt_, xr, rho, kap, fr, t0, t1, sn, cs, fre, fim, den, t2 = [A() for _ in range(13)]
    a = k.op("act", lambda e: e.activation(out=dt_, in_=ldt, func=ACT.Exp), waits)
    v = k.op("dve", lambda e: e.tensor_tensor(out=xr, in0=are, in1=dt_, op=ALU.mult), tmax(a, waits))
    a = k.op("act", lambda e: e.activation(out=rho, in_=xr, func=ACT.Exp), [v])
    v = k.op("dve", lambda e: e.scalar_tensor_tensor(out=kap, in0=aim, scalar=1.0 / TWO_PI, in1=dt_, op0=ALU.mult, op1=ALU.mult))

    def cossin(mult, c_out, s_out):
        v_ = k.op("dve", lambda e: e.tensor_scalar(out=t0, in0=kap, scalar1=float(mult), scalar2=None, op0=ALU.mult))
        v_ = k.op("dve", lambda e: e.tensor_scalar(out=t1, in0=t0, scalar1=MAGIC, scalar2=MAGIC, op0=ALU.add, op1=ALU.subtract))
        v_ = k.op("dve", lambda e: e.tensor_tensor(out=fr, in0=t0, in1=t1, op=ALU.subtract))
        a_ = k.op("act", lambda e: e.activation(out=s_out, in_=fr, func=ACT.Sin, scale=TWO_PI), [v_])
        a_ = k.op("act", lambda e: e.activation(out=t1, in_=fr, func=ACT.Abs))
        k.wait_self("act", a_)
        a_ = k.op("act", lambda e: e.activation(out=c_out, in_=t1, func=ACT.Sin, scale=-TWO_PI, bias=float(np.pi / 2)))
        k.wait("dve", [a_])
        return a_

    a = cossin(1.0, cs, sn)
    abr, abi = t0, t1
    ab_re = A() if npow > 1 else abr
    ab_im = A() if npow > 1 else None
    v = k.op("dve", lambda e: e.tensor_tensor(out=ab_re, in0=rho, in1=cs, op=ALU.mult), [a])
    v = k.op("dve", lambda e: e.tensor_scalar(out=abr, in0=ab_re, scalar1=-1.0, scalar2=None, op0=ALU.add))
    v = k.op("dve", lambda e: e.tensor_tensor(out=abi, in0=rho, in1=sn, op=ALU.mult))
    if npow > 1:
        v = k.op("dve", lambda e: e.tensor_copy(out=ab_im, in_=abi))
    v = k.op("dve", lambda e: e.tensor_tensor(out=den, in0=are, in1=are, op=ALU.mult))
    v = k.op("dve", lambda e: e.tensor_tensor(out=t2, in0=aim, in1=aim, op=ALU.mult))
    v = k.op("dve", lambda e: e.tensor_tensor(out=den, in0=den, in1=t2, op=ALU.add))
    v = k.op("dve", lambda e: e.reciprocal(out=den, in_=den))
    v = k.op("dve", lambda e: e.tensor_tensor(out=fre, in0=abr, in1=are, op=ALU.mult))
    v = k.op("dve", lambda e: e.tensor_tensor(out=t2, in0=abi, in1=aim, op=ALU.mult))
    v = k.op("dve", lambda e: e.tensor_tensor(out=fre, in0=fre, in1=t2, op=ALU.add))
    v = k.op("dve", lambda e: e.tensor_tensor(out=fre, in0=fre, in1=den, op=ALU.mult))
    v = k.op("dve", lambda e: e.tensor_tensor(out=fim, in0=abi, in1=are, op=ALU.mult))
    v = k.op("dve", lambda e: e.tensor_tensor(out=t2, in0=abr, in1=aim, op=ALU.mult))
    v = k.op("dve", lambda e: e.tensor_tensor(out=fim, in0=fim, in1=t2, op=ALU.subtract))
    v = k.op("dve", lambda e: e.tensor_tensor(out=fim, in0=fim, in1=den, op=ALU.mult))
    rots = {}
    for nst in nsteps_list:
        cN, sN, rN = A(), A(), A()
        a = cossin(float(nst), cN, sN)
        a = k.op("act", lambda e: e.activation(out=rN, in_=xr, func=ACT.Exp, scale=float(nst)))
        rots[nst] = (cN, sN, rN, a)
    if dram_bft is None:
        bft_re = [ar.alloc([128, 32, 128], BF16) for _ in range(npow)]
        bft_im = [ar.alloc([128, 32, 128], BF16) for _ in range(npow)]
    else:
        bft_re = bft_im = None
    pw_re, pw_im, pa, pb_ = sn, cs, den, t2
    keep_top = ar.top
    bd_re = ar.alloc([128, NPAIR, 32], BF16)
    bd_im = ar.alloc([128, NPAIR, 32], BF16)
    u1 = ar.alloc([128, NPAIR, 16])
    u2 = ar.alloc([128, NPAIR, 16])
    ident_bf = ar.alloc([128, 128], BF16)
    k.op("dve", lambda e: e.tensor_copy(out=ident_bf, in_=ident))
    k.op("dve", lambda e: e.memset(bd_re, 0.0))
    k.op("dve", lambda e: e.memset(bd_im, 0.0))
    b3r = bre.rearrange("p (j c) -> p j c", c=16)
    b3i = bim.rearrange("p (j c) -> p j c", c=16)
    k.op("dve", lambda e: e.tensor_copy(out=pw_re, in_=fre))
    k.op("dve", lambda e: e.tensor_copy(out=pw_im, in_=fim))
    if dram_bft is not None:
        stage = [ar.alloc([128, 32, 128], BF16) for _ in range(2)]
        stage_sem = [k.dsem(), k.dsem()]
        stage_tok = [None, None]
        sink_toks = [None, None]
    else:
        sink_toks = [None, None]
    last = None
    vlast = None
    i = 0
    for pw in range(npow):
        if pw > 0:
            k.op("dve", lambda e: e.tensor_tensor(out=pa, in0=pw_re, in1=ab_re, op=ALU.mult), tmax(last))
            k.op("dve", lambda e: e.tensor_tensor(out=pb_, in0=pw_im, in1=ab_im, op=ALU.mult))
            k.op("dve", lambda e: e.tensor_tensor(out=pa, in0=pa, in1=pb_, op=ALU.subtract))
            k.op("dve", lambda e: e.tensor_tensor(out=pb_, in0=pw_re, in1=ab_im, op=ALU.mult))
            k.op("dve", lambda e: e.tensor_tensor(out=pw_im, in0=pw_im, in1=ab_re, op=ALU.mult))
            k.op("dve", lambda e: e.tensor_tensor(out=pw_im, in0=pw_im, in1=pb_, op=ALU.add))
            k.op("dve", lambda e: e.tensor_copy(out=pw_re, in_=pa))
        frb = pw_re.unsqueeze(2).to_broadcast([128, NPAIR, 16])
        fib = pw_im.unsqueeze(2).to_broadcast([128, NPAIR, 16])
        k.op("dve", lambda e: e.tensor_tensor(out=u1, in0=b3r, in1=frb, op=ALU.mult), tmax(last))
        k.op("dve", lambda e: e.tensor_tensor(out=u2, in0=b3i, in1=fib, op=ALU.mult))
        for (lo, hi, c0) in ((0, 64, 0), (64, 128, 16)):
            k.op("dve", lambda e: e.tensor_tensor(out=bd_re[lo:hi, :, c0:c0 + 16], in0=u1[lo:hi], in1=u2[lo:hi], op=ALU.subtract))
        k.op("dve", lambda e: e.tensor_tensor(out=u1, in0=b3i, in1=frb, op=ALU.mult))
        k.op("dve", lambda e: e.tensor_tensor(out=u2, in0=b3r, in1=fib, op=ALU.mult))
        for (lo, hi, c0) in ((0, 64, 0), (64, 128, 16)):
            vlast = k.op("dve", lambda e: e.tensor_tensor(out=bd_im[lo:hi, :, c0:c0 + 16], in0=u1[lo:hi], in1=u2[lo:hi], op=ALU.add))
        for ri_, src in enumerate((bd_re, bd_im)):
            if dram_bft is None:
                dst = (bft_re, bft_im)[ri_][pw]
                dst_wait = None
            else:
                dst = stage[ri_]
                dst_wait = stage_tok[ri_]
            for b in range(32):
                bk = i % 2
                pv = ps[:, bk, 0:64].bitcast(BF16)
                p = k.op("pe", lambda e: e.transpose(out=pv, in_=src[:, 4 * b:4 * b + 4, :].rearrange("p a c -> p (a c)"),
                                                     identity=ident_bf), tmax(vlast, bank_free[bk]))
                bank_free[bk] = k.op("act", lambda e: e.copy(out=dst[:, b, :], in_=pv), tmax(p, dst_wait))
                last = bank_free[bk]
                i += 1
            if dram_bft is not None:
                stage_tok[ri_] = k.dma("sp", dram_bft[2 * pw + ri_], dst.rearrange("p a b -> p (a b)"), stage_sem[ri_], [last])
                sink_toks[ri_] = stage_tok[ri_]
    return dict(rho=rho, kap=kap, xr=xr, bft_re=bft_re, bft_im=bft_im, rots=rots, tok=tmax(last, vlast, sink_toks[0], sink_toks[1]),
                keep_top=keep_top, ab_re=ab_re, ab_im=ab_im)


class PairBufs:
    def __init__(self, ar, ncols):
        A = lambda: ar.alloc([128, ncols])
        self.bu_re, self.bu_im = A(), A()
        self.ph, self.rn, self.fr = A(), A(), A()
        self.ab = self.rn
        self.sn, self.cs, self.rrow = A(), A(), A()
        self.zre, self.zim, self.gre, self.gim = A(), A(), A(), A()
        self.t_prerot = None
        self.t_tabact = None
        self.t_scan = None
        self.t_gread = None


AR_WORDS = 51500


def build_a(stop_after=99):
    nc = bass.Bass("TRN2", target_bir_lowering=False)
    dr = lambda name, shape, kind="ExternalInput": nc.dram_tensor(name, shape, F32, kind=kind).ap()
    xin = dr("xin", [T, D])
    ident_d = dr("ident", [128, 128])
    nw_d = dr("nw", [128, 32])
    wci = dr("wci", [32, 128, 16 * 512])
    cw_d = dr("cw", [128, 32 * 4])
    cc_d = dr("cc", [128, 32 * 4])
    wco = dr("wco", [16, 128, 32 * 128])
    are_d = dr("are", [128, NPAIR])
    aim_d = dr("aim", [128, NPAIR])
    ldt_d = dr("ldt", [128, NPAIR])
    bre_d = dr("bre", [128, NPAIR * 16])
    bim_d = dr("bim", [128, NPAIR * 16])
    wsu = dr("wsu", [32, 128, 16 * 128])
    iota_d = dr("iota", [128, T])
    h1T_o = dr("h1T", [128, 16 * T], "ExternalOutput")
    ncv_o = dr("ncv", [128, 32 * 6], "ExternalOutput")
    f0_o = dr("f0", [128, NPAIR * 2], "ExternalOutput")

    with contextlib.ExitStack() as es:
        k = K(nc, es)
        arena_t = es.enter_context(nc.sbuf_tensor("arena", [128, AR_WORDS], F32))
        ar = Arena(arena_t[:], AR_WORDS)
        ps = es.enter_context(nc.psum_tensor("ps", [128, 8, 512], F32))
        small_t = es.enter_context(nc.sbuf_tensor("small", [128, 800], F32))
        sm = Arena(small_t[:], 800)
        ident = sm.alloc([128, 128])
        nw = sm.alloc([128, 32])
        cw = sm.alloc([128, 32, 4])
        cc = sm.alloc([128, 32, 2, 2])
        ncv = sm.alloc([128, 32, 6])
        csem = k.dsem()
        k.dma("sp", ident, ident_d, csem)
        k.dma("sp", nw, nw_d, csem)
        k.dma("sp", cw.rearrange("p a b -> p (a b)"), cw_d, csem)
        cst = k.dma("sp", cc.rearrange("p a b c -> p (a b c)"), cc_d, csem)
        xsem = [k.dsem(), k.dsem()]
        wsem = [k.dsem(), k.dsem()]
        bank_free = [None] * 8

        ar.top = 0
        y0T = ar.alloc([128, 32, T], BF16)
        top_h1 = ar.top
        xn = ar.alloc([128, 16, T], BF16)
        wc = [ar.alloc([128, 16, 512], BF16) for _ in range(2)]
        top_s2 = ar.top
        xs = [ar.alloc([128, D]) for _ in range(2)]
        ss = ar.alloc([128, 16])
        rstd = ar.alloc([128, 16])
        ar.top = AR_WORDS - 2 * D
        xbuf = [ar.alloc([128, D]) for _ in range(2)]

        k.op("dve", lambda e: e.memset(ss, 0.0))
        ss_zero = k.op("dve", lambda e: e.memset(rstd, 0.0))
        xbuf_free = [None, None]
        xs_free = [None, None]
        last_xn = None
        for i in range(NTT):
            r0 = 128 * i
            nr = min(128, T - r0)
            s = i % 2
            ld = k.dma("sp", xbuf[s][0:nr, :], xin[r0:r0 + nr, :], xsem[s], tmax(xbuf_free[s]))
            a1 = k.op("act", lambda e: e.activation(out=xs[s][0:nr, :], in_=xbuf[s][0:nr, :], func=ACT.Square,
                                                    accum_out=ss[0:nr, i:i + 1]), tmax(ld, xs_free[s], cst, ss_zero))
            v1 = k.op("dve", lambda e: e.tensor_scalar(out=rstd[0:nr, i:i + 1], in0=ss[0:nr, i:i + 1], scalar1=1.0 / D,
                                                       scalar2=EPS, op0=ALU.mult, op1=ALU.add), [a1])
            a2 = k.op("act", lambda e: e.activation(out=rstd[0:nr, i:i + 1], in_=rstd[0:nr, i:i + 1], func=ACT.Sqrt), [v1])
            v2 = k.op("dve", lambda e: e.reciprocal(out=rstd[0:nr, i:i + 1], in_=rstd[0:nr, i:i + 1]), [a2])
            a3 = k.op("act", lambda e: e.activation(out=xs[s][0:nr, :], in_=xbuf[s][0:nr, :], func=ACT.Copy,
                                                    scale=rstd[0:nr, i:i + 1]), [v2])
            xbuf_free[s] = a3
            p = None
            for g in range(4):
                bk = g % 2
                for q in range(4):
                    kc = 4 * g + q
                    p = k.op("pe", lambda e: e.transpose(out=ps[:, bk, q * 128:q * 128 + nr],
                                                         in_=xs[s][0:nr, kc * 128:(kc + 1) * 128], identity=ident[0:nr, 0:nr]),
                             tmax(a3, bank_free[bk]), inc=(q == 3))
                src = ps[:, bk, :].rearrange("p (a c) -> p a c", a=4)[:, :, 0:nr]
                bank_free[bk] = k.op("dve", lambda e: e.tensor_tensor(
                    out=xn[:, 4 * g:4 * g + 4, r0:r0 + nr], in0=src,
                    in1=nw[:, 4 * g:4 * g + 4].unsqueeze(2).to_broadcast([128, 4, nr]), op=ALU.mult), [p])
                last_xn = bank_free[bk]
            xs_free[s] = p

        ar.top = top_s2
        cvb = [ar.alloc([128, T]) for _ in range(2)]
        bsb = [ar.alloc([128, T]) for _ in range(2)]
        szb = [ar.alloc([128, T]) for _ in range(2)]
        acc = [ar.alloc([128, T]) for _ in range(2)]
        csb = [ar.alloc([128, 366]) for _ in range(2)]
        assert ar.top <= AR_WORDS - 2 * D
        k.op("dve", lambda e: e.memset(y0T[:, :, 0:2], 0.0))
        wc_free = [None, None]
        blk_free = [last_xn, last_xn]
        csb_free = [last_xn, last_xn]
        step = 0
        last_pe = None
        y0_done = None
        for eb in range(32):
            s = eb % 2
            wl = k.dma("pool", wc[s].rearrange("p a b -> p (a b)"), wci[eb], wsem[s], tmax(wc_free[s]), max_dma_last_dim=8192)
            evs = []
            for (c0, c1) in RANGES:
                n = c1 - c0
                st_ = step % 2
                step += 1
                p = None
                for q in range(4):
                    bk = st_ * 4 + q
                    for kc in range(16):
                        p = k.op("pe", lambda e: e.matmul(ps[:, bk, 0:n], lhsT=wc[s][:, kc, q * 128:(q + 1) * 128],
                                                          rhs=xn[:, kc, c0:c1], start=(kc == 0), stop=(kc == 15)),
                                 tmax(wl, last_xn, bank_free[bk]), inc=(q == 3 and kc == 15))
                last_pe = p
                b0 = st_ * 4
                a_c = k.op("act", lambda e: e.copy(out=csb[st_][:, 0:n], in_=ps[:, b0 + 1, 0:n]), tmax(p, csb_free[st_]))
                v_cv = k.op("dve", lambda e: e.tensor_tensor(out=cvb[s][:, c0:c1], in0=ps[:, b0 + 2, 0:n], in1=csb[st_][:, 0:n],
                                                            op=ALU.mult), tmax(p, a_c, blk_free[s]))
                csb_free[st_] = v_cv
                a_b = k.op("act", lambda e: e.copy(out=bsb[s][:, c0:c1], in_=ps[:, b0 + 0, 0:n]), tmax(blk_free[s]))
                a_z = k.op("act", lambda e: e.activation(out=szb[s][:, c0:c1], in_=ps[:, b0 + 3, 0:n], func=ACT.Silu))
                for q in range(4):
                    bank_free[b0 + q] = tmax(a_z, v_cv)
                evs = tmax(a_z, v_cv)
            wc_free[s] = last_pe
            k.op("dve", lambda e: e.tensor_copy(out=cvb[s][:, C_A0 - 2:C_A0], in_=cc[:, eb, 0, :]), evs)
            v = k.op("dve", lambda e: e.tensor_copy(out=cvb[s][:, C_B0 - 2:C_B0], in_=cc[:, eb, 1, :]))
            a = k.op("act", lambda e: e.activation(out=acc[s][:, 2:T], in_=cvb[s][:, 2:T], func=ACT.Identity,
                                                   scale=cw[:, eb, 2:3], bias=cw[:, eb, 3:4]), tmax(v, evs))
            v = k.op("dve", lambda e: e.scalar_tensor_tensor(out=acc[s][:, 2:T], in0=cvb[s][:, 1:T - 1], scalar=cw[:, eb, 1:2],
                                                            in1=acc[s][:, 2:T], op0=ALU.mult, op1=ALU.add), [a])
            v = k.op("dve", lambda e: e.scalar_tensor_tensor(out=acc[s][:, 2:T], in0=cvb[s][:, 0:T - 2], scalar=cw[:, eb, 0:1],
                                                            in1=acc[s][:, 2:T], op0=ALU.mult, op1=ALU.add))
            v = k.op("dve", lambda e: e.tensor_tensor(out=acc[s][:, 2:T], in0=acc[s][:, 2:T], in1=bsb[s][:, 2:T], op=ALU.mult), evs)
            v = k.op("dve", lambda e: e.tensor_tensor(out=y0T[:, eb, 2:T], in0=acc[s][:, 2:T], in1=szb[s][:, 2:T], op=ALU.mult))
            src = cvb[s][:, T - 102:T].rearrange("p (a b) -> p a b", b=34)[:, :, 32:34]
            v = k.op("dve", lambda e: e.tensor_copy(out=ncv[:, eb, :].rearrange("p (a b) -> p a b", b=2), in_=src))
            blk_free[s] = v
            y0_done = v
        s2_pe_done = last_pe
        st_ncv = k.dsem()
        out_ncv = k.dma("sp", ncv_o, ncv.rearrange("p a b -> p (a b)"), st_ncv, [y0_done])

        ar.top = top_h1
        h1T = ar.alloc([128, 16, T])
        wo = [ar.alloc([128, 32, 128], BF16) for _ in range(2)]
        assert ar.top <= AR_WORDS - 2 * D
        guard = tmax(s2_pe_done, y0_done)
        xbuf_free = [guard, guard]
        last_fill = None
        for i in range(NTT):
            r0 = 128 * i
            nr = min(128, T - r0)
            s = i % 2
            ld = k.dma("sp", xbuf[s][0:nr, :], xin[r0:r0 + nr, :], xsem[s], tmax(xbuf_free[s]))
            p = None
            for g in range(4):
                bk = g % 2
                for q in range(4):
                    kc = 4 * g + q
                    p = k.op("pe", lambda e: e.transpose(out=ps[:, bk, q * 128:q * 128 + nr],
                                                         in_=xbuf[s][0:nr, kc * 128:(kc + 1) * 128], identity=ident[0:nr, 0:nr]),
                             tmax(ld, bank_free[bk], guard), inc=(q == 3))
                src = ps[:, bk, :].rearrange("p (a c) -> p a c", a=4)[:, :, 0:nr]
                bank_free[bk] = k.op("act", lambda e: e.copy(out=h1T[:, 4 * g:4 * g + 4, r0:r0 + nr], in_=src), tmax(p, guard))
                last_fill = bank_free[bk]
            xbuf_free[s] = p
        wo_free = [guard, guard]
        hsem = k.dsem()
        last_h = None
        for db in range(16):
            s = db % 2
            wl = k.dma("pool", wo[s].rearrange("p a b -> p (a b)"), wco[db], wsem[s], tmax(wo_free[s]), max_dma_last_dim=8192)
            p = None
            for ri, (c0, c1) in enumerate(RANGES):
                n = c1 - c0
                bk = 2 + (db % 2) * 3 + ri
                for eb in range(32):
                    p = k.op("pe", lambda e: e.matmul(ps[:, bk, 0:n], lhsT=wo[s][:, eb, :], rhs=y0T[:, eb, c0:c1],
                                                      start=(eb == 0), stop=(eb == 31)),
                             tmax(wl, y0_done, bank_free[bk]), inc=(eb == 31))
                bank_free[bk] = k.op("dve", lambda e: e.tensor_tensor(out=h1T[:, db, c0:c1], in0=ps[:, bk, 0:n], in1=h1T[:, db, c0:c1],
                                                                     op=ALU.add), tmax(p, last_fill))
                last_h = bank_free[bk]
            wo_free[s] = p
            k.dma("sp", h1T_o[:, db * T:(db + 1) * T], h1T[:, db, :], hsem, [last_h])
        h_out_tok = ("d", hsem, hsem.val)
        if stop_after <= 3:
            k.wait("sp", [h_out_tok, out_ncv])
            return nc

        ar.top = 0
        xn1 = ar.alloc([128, 16, T], BF16)
        sq = [ar.alloc([128, T]) for _ in range(2)]
        rb = ar.alloc([128, T])
        ones_d = ar.alloc([128, 128])
        assert ar.top <= top_h1
        k.op("dve", lambda e: e.memset(ones_d, 1.0 / D), [last_h])
        vm = k.op("dve", lambda e: e.memset(rb, 0.0))
        sq_free = [last_h, last_h]
        p = None
        for db in range(16):
            s = db % 2
            a = k.op("act", lambda e: e.activation(out=sq[s], in_=h1T[:, db, :], func=ACT.Square), tmax(last_h, sq_free[s]))
            for ri, (c0, c1) in enumerate(RANGES):
                n = c1 - c0
                p = k.op("pe", lambda e: e.matmul(ps[:, 2 + ri, 0:n], lhsT=ones_d, rhs=sq[s][:, c0:c1], start=(db == 0), stop=(db == 15)),
                         tmax(a, vm, bank_free[2 + ri]), inc=(ri == 2))
            sq_free[s] = p
        v = None
        for ri, (c0, c1) in enumerate(RANGES):
            n = c1 - c0
            v = k.op("dve", lambda e: e.tensor_scalar(out=rb[:, c0:c1], in0=ps[:, 2 + ri, 0:n], scalar1=EPS, scalar2=None, op0=ALU.add), [p])
            bank_free[2 + ri] = v
        a = k.op("act", lambda e: e.activation(out=rb, in_=rb, func=ACT.Sqrt), [v])
        v = k.op("dve", lambda e: e.reciprocal(out=rb, in_=rb), [a])
        for db in range(16):
            v = k.op("dve", lambda e: e.scalar_tensor_tensor(out=xn1[:, db, :], in0=h1T[:, db, :], scalar=nw[:, 16 + db:17 + db],
                                                            in1=rb, op0=ALU.mult, op1=ALU.mult))
        xn1_done = v

        ar.top = 8784
        are, aim, ldt = [ar.alloc([128, NPAIR]) for _ in range(3)]
        bre = ar.alloc([128, NPAIR * 16])
        bim = ar.alloc([128, NPAIR * 16])
        iot = ar.alloc([128, T])
        onesr = ar.alloc([128, T])
        ge_re = ar.alloc([128, NPAIR])
        ge_im = ar.alloc([128, NPAIR])
        f0 = ar.alloc([128, NPAIR, 2])
        c2 = k.dsem()
        gw = tmax(xn1_done, h_out_tok)
        k.dma("sp", are, are_d, c2, gw)
        k.dma("sp", aim, aim_d, c2)
        k.dma("sp", ldt, ldt_d, c2)
        k.dma("sp", bre, bre_d, c2)
        k.dma("sp", iot, iota_d, c2)
        c2t = k.dma("sp", bim, bim_d, c2)
        k.op("dve", lambda e: e.memset(onesr, 1.0), gw)
        cons = emit_ssm_consts(k, ar, ps, ident, are, aim, ldt, bre, bim, tmax(c2t, gw), bank_free, [SEG, 4], npow=4)
        ar.top = cons["keep_top"]
        cN, sN, rN, rot_tok = cons["rots"][SEG]
        _, _, rho4, rot4_tok = cons["rots"][4]
        kap4 = ar.alloc([128, NPAIR])
        k4_tok = k.op("dve", lambda e: e.tensor_scalar(out=kap4, in0=cons["kap"], scalar1=4.0, scalar2=None, op0=ALU.mult), tmax(cons["tok"]))

        NCH = SEG // 4
        wu = [ar.alloc([128, 16, 128], BF16) for _ in range(2)]
        uT = [ar.alloc([128, T], BF16) for _ in range(2)]
        pbs = [PairBufs(ar, NCH) for _ in range(2)]
        PC0, PC1 = C_P0, C_P0 + SEG
        pre_tok = tmax(cons["tok"], k4_tok, rot4_tok)
        wu_free = [pre_tok, pre_tok]
        uT_free = [pre_tok, pre_tok]
        step = 0
        last_copy = None
        for b in range(32):
            s = b % 2
            wl = k.dma("pool", wu[s].rearrange("p a b -> p (a b)"), wsu[b], wsem[s], tmax(wu_free[s], gw), max_dma_last_dim=8192)
            p = None
            a_u = None
            for ri, (c0, c1) in enumerate(RANGES):
                n = c1 - c0
                bk = 5 + ri
                for kc in range(16):
                    p = k.op("pe", lambda e: e.matmul(ps[:, bk, 0:n], lhsT=wu[s][:, kc, :], rhs=xn1[:, kc, c0:c1],
                                                      start=(kc == 0), stop=(kc == 15)),
                             tmax(wl, xn1_done, bank_free[bk]), inc=(kc == 15))
                a_u = k.op("act", lambda e: e.copy(out=uT[s][:, c0:c1], in_=ps[:, bk, 0:n]), tmax(p, uT_free[s]))
                bank_free[bk] = a_u
            wu_free[s] = p
            last_bu_pe = None
            for j4 in range(4):
                pair = 4 * b + j4
                pb = pbs[pair % 2]
                bkr = (step % 2) * 2
                step += 1
                pi_ = None
                for (bft, bk_) in ((cons["bft_re"], bkr), (cons["bft_im"], bkr + 1)):
                    for r in range(4):
                        rhs = uT[s][32 * j4:32 * j4 + 32, PC0 + r:PC0 + r + 4 * (NCH - 1) + 1:4]
                        pi_ = k.op("pe", lambda e: e.matmul(ps[:, bk_, 0:NCH], lhsT=bft[3 - r][32 * j4:32 * j4 + 32, b, :], rhs=rhs,
                                                            start=(r == 0), stop=(r == 3), tile_position=(32 * j4, 0)),
                                   tmax(a_u, bank_free[bk_], pre_tok), inc=(r == 3))
                last_bu_pe = pi_
                k.op("act", lambda e: e.copy(out=pb.bu_re, in_=ps[:, bkr, 0:NCH]), tmax(pi_, pb.t_prerot))
                a_e = k.op("act", lambda e: e.copy(out=pb.bu_im, in_=ps[:, bkr + 1, 0:NCH]))
                bank_free[bkr] = a_e
                bank_free[bkr + 1] = a_e
                kcol = kap4[:, pair:pair + 1]
                v = k.op("dve", lambda e: e.tensor_scalar(out=pb.ph, in0=iot[:, PC0:PC0 + NCH], scalar1=kcol, scalar2=None, op0=ALU.mult),
                         tmax(pb.t_tabact, pre_tok))
                v = k.op("dve", lambda e: e.tensor_scalar(out=pb.rn, in0=pb.ph, scalar1=MAGIC, scalar2=MAGIC, op0=ALU.add, op1=ALU.subtract))
                v = k.op("dve", lambda e: e.tensor_tensor(out=pb.fr, in0=pb.ph, in1=pb.rn, op=ALU.subtract))
                a = k.op("act", lambda e: e.activation(out=pb.sn, in_=pb.fr, func=ACT.Sin, scale=TWO_PI), tmax(v, pb.t_prerot))
                a = k.op("act", lambda e: e.activation(out=pb.ab, in_=pb.fr, func=ACT.Abs))
                a_t = k.op("act", lambda e: e.activation(out=pb.cs, in_=pb.ab, func=ACT.Sin, scale=-TWO_PI, bias=float(np.pi / 2)))
                pb.t_tabact = a_t
                a_r = k.op("act", lambda e: e.activation(out=pb.rrow, in_=onesr[:, 0:NCH], func=ACT.Copy,
                                                         scale=rho4[:, pair:pair + 1]), tmax(pb.t_scan))
                t1, t2 = pb.ph, pb.rn
                v = k.op("dve", lambda e: e.tensor_tensor(out=t1, in0=pb.cs, in1=pb.bu_re, op=ALU.mult), tmax(a_t, a_e))
                v = k.op("dve", lambda e: e.tensor_tensor(out=t2, in0=pb.sn, in1=pb.bu_im, op=ALU.mult))
                v = k.op("dve", lambda e: e.tensor_tensor(out=pb.zre, in0=t1, in1=t2, op=ALU.add))
                v = k.op("dve", lambda e: e.tensor_tensor(out=t1, in0=pb.cs, in1=pb.bu_im, op=ALU.mult))
                v = k.op("dve", lambda e: e.tensor_tensor(out=t2, in0=pb.sn, in1=pb.bu_re, op=ALU.mult))
                v = k.op("dve", lambda e: e.tensor_tensor(out=pb.zim, in0=t1, in1=t2, op=ALU.subtract))
                pb.t_prerot = v
                v = k.op("dve", lambda e: e.tensor_tensor_scan(out=pb.gre, data0=pb.rrow, data1=pb.zre, initial=0.0,
                                                               op0=ALU.mult, op1=ALU.add), tmax(a_r, pb.t_gread))
                v = k.op("dve", lambda e: e.tensor_tensor_scan(out=pb.gim, data0=pb.rrow, data1=pb.zim, initial=0.0,
                                                               op0=ALU.mult, op1=ALU.add))
                pb.t_scan = v
                k.op("act", lambda e: e.copy(out=ge_re[:, pair:pair + 1], in_=pb.gre[:, NCH - 1:NCH]), [v])
                a = k.op("act", lambda e: e.copy(out=ge_im[:, pair:pair + 1], in_=pb.gim[:, NCH - 1:NCH]))
                pb.t_gread = a
                last_copy = a
            uT_free[s] = last_bu_pe
        ta, tb_ = ar.alloc([128, NPAIR]), ar.alloc([128, NPAIR])
        v = k.op("dve", lambda e: e.tensor_tensor(out=ta, in0=cN, in1=ge_re, op=ALU.mult), tmax(last_copy, rot_tok))
        v = k.op("dve", lambda e: e.tensor_tensor(out=tb_, in0=sN, in1=ge_im, op=ALU.mult))
        v = k.op("dve", lambda e: e.tensor_tensor(out=f0[:, :, 0], in0=ta, in1=tb_, op=ALU.subtract))
        v = k.op("dve", lambda e: e.tensor_tensor(out=ta, in0=sN, in1=ge_re, op=ALU.mult))
        v = k.op("dve", lambda e: e.tensor_tensor(out=tb_, in0=cN, in1=ge_im, op=ALU.mult))
        v = k.op("dve", lambda e: e.tensor_tensor(out=f0[:, :, 1], in0=ta, in1=tb_, op=ALU.add))
        fsem = k.dsem()
        ft = k.dma("sp", f0_o, f0.rearrange("p a b -> p (a b)"), fsem, [v])
        k.wait("sp", [h_out_tok, out_ncv, ft])
    return nc


AR_B = 52700
GV = lambda g: g[:, T - 102:T].rearrange("p (a b) -> p a b", b=34)[:, :, 33]


def build_b():
    nc = bass.Bass("TRN2", target_bir_lowering=False)
    dr = lambda name, shape, kind="ExternalInput": nc.dram_tensor(name, shape, F32, kind=kind).ap()
    h1T_i = dr("h1T_i", [128, 16 * T])
    f0p_d = dr("f0prev", [128, 7 * NPAIR * 2])
    ident_d = dr("ident", [128, 128])
    nw_d = dr("nw", [128, 48])
    are_d = dr("are", [128, NPAIR])
    aim_d = dr("aim", [128, NPAIR])
    ldt_d = dr("ldt", [128, NPAIR])
    bre_d = dr("bre", [128, NPAIR * 16])
    bim_d = dr("bim", [128, NPAIR * 16])
    iotc_d = dr("iotc", [128, 273])
    cbr_d = dr("cbr", [128, NPAIR * 32])
    cbi_d = dr("cbi", [128, NPAIR * 32])
    dv_d = dr("dvec", [128, 32])
    bg_d = dr("bgl", [128, 32])
    h0r_d = dr("h0r", [128, NPAIR * 2])
    h0i_d = dr("h0i", [128, NPAIR * 2])
    wsu = dr("wsu", [32, 128, 16 * 128])
    wsz = dr("wsz", [32, 128, 16 * 128])
    wgl = dr("wgl", [32, 128, 32 * 128])
    wso = dr("wso", [16, 128, 32 * 128])
    y_o = dr("yout", [T, D], "ExternalOutput")
    scr_bft = nc.dram_tensor("scr_bft", [8, 128, 32 * 128], BF16, kind="Internal").ap()
    scr_ck = nc.dram_tensor("scr_ck", [10, 128, NPAIR * 32], BF16, kind="Internal").ap()
    hf_o = dr("hfin", [128, NPAIR * 6], "ExternalOutput")

    with contextlib.ExitStack() as es:
        k = K(nc, es)
        arena_t = es.enter_context(nc.sbuf_tensor("arena", [128, AR_B], F32))
        ar = Arena(arena_t[:], AR_B)
        ps = es.enter_context(nc.psum_tensor("ps", [128, 8, 512], F32))
        small_t = es.enter_context(nc.sbuf_tensor("small", [128, 256], F32))
        sm = Arena(small_t[:], 256)
        ident = sm.alloc([128, 128])
        nw = sm.alloc([128, 48])
        dvec = sm.alloc([128, 32])
        bgl = sm.alloc([128, 32])
        csem = k.dsem()
        k.dma("sp", ident, ident_d, csem)
        k.dma("sp", nw, nw_d, csem)
        k.dma("sp", dvec, dv_d, csem)
        cst = k.dma("sp", bgl, bg_d, csem)
        wsem = [k.dsem(), k.dsem(), k.dsem(), k.dsem()]
        bank_free = [None] * 8

        xn1 = ar.alloc([128, 16, T], BF16)
        g_all = ar.alloc([128, 32, T], BF16)
        BASE = ar.top
        h1T = ar.alloc([128, 16, T])
        sq = [ar.alloc([128, T]) for _ in range(2)]
        rb = ar.alloc([128, T])
        ones_d = ar.alloc([128, 128])
        hsem = k.dsem()
        for db in range(16):
            k.dma("sp", h1T[:, db, :], h1T_i[:, db * T:(db + 1) * T], hsem)
        h_in = ("d", hsem, hsem.val)
        vm = k.op("dve", lambda e: e.memset(ones_d, 1.0 / D))
        sq_free = [None, None]
        p = None
        for db in range(16):
            s = db % 2
            a = k.op("act", lambda e: e.activation(out=sq[s], in_=h1T[:, db, :], func=ACT.Square), tmax(h_in, sq_free[s]))
            for ri, (c0, c1) in enumerate(RANGES):
                n = c1 - c0
                p = k.op("pe", lambda e: e.matmul(ps[:, 2 + ri, 0:n], lhsT=ones_d, rhs=sq[s][:, c0:c1], start=(db == 0), stop=(db == 15)),
                         tmax(a, vm), inc=(ri == 2))
            sq_free[s] = p
        v = None
        for ri, (c0, c1) in enumerate(RANGES):
            n = c1 - c0
            v = k.op("dve", lambda e: e.tensor_scalar(out=rb[:, c0:c1], in0=ps[:, 2 + ri, 0:n], scalar1=EPS, scalar2=None, op0=ALU.add), [p])
            bank_free[2 + ri] = v
        a = k.op("act", lambda e: e.activation(out=rb, in_=rb, func=ACT.Sqrt), [v])
        v = k.op("dve", lambda e: e.reciprocal(out=rb, in_=rb), [a])
        for db in range(16):
            v = k.op("dve", lambda e: e.scalar_tensor_tensor(out=xn1[:, db, :], in0=h1T[:, db, :], scalar=nw[:, 16 + db:17 + db],
                                                            in1=rb, op0=ALU.mult, op1=ALU.mult), [cst, h_in])
        xn1_done = v

        NC = 273
        SEGS = [(0, 8, C_A0), (8, 8, C_B0), (16, 257, C_P0)]
        ar.top = AR_B - (3 * NPAIR + 2 * NPAIR * 16 + 7 * NPAIR * 2)
        tr_base = ar.top
        are, aim, ldt = [ar.alloc([128, NPAIR]) for _ in range(3)]
        bre = ar.alloc([128, NPAIR * 16])
        bim = ar.alloc([128, NPAIR * 16])
        f0p = ar.alloc([128, 7, NPAIR, 2])
        ar.top = BASE
        c2 = k.dsem()
        gw = tmax(xn1_done)
        k.dma("sp", are, are_d, c2, gw)
        k.dma("sp", aim, aim_d, c2)
        k.dma("sp", ldt, ldt_d, c2)
        k.dma("sp", bre, bre_d, c2)
        k.dma("sp", bim, bim_d, c2)
        c2t = k.dma("sp", f0p.rearrange("p a b c -> p (a b c)"), f0p_d, c2)
        cons = emit_ssm_consts(k, ar, ps, ident, are, aim, ldt, bre, bim, tmax(c2t, gw, cst), bank_free, [SEG, 32, 4], npow=4,
                               dram_bft=scr_bft)
        assert ar.top <= tr_base, (ar.top, tr_base)
        ar.top = cons["keep_top"]
        cN, sN, rN, rtokN = cons["rots"][SEG]
        c32, s32, _, rtok32 = cons["rots"][32]
        _, _, rho4, rtok4 = cons["rots"][4]
        iotc = ar.alloc([128, NC])
        h0r = ar.alloc([128, NPAIR, 2])
        h0i = ar.alloc([128, NPAIR, 2])
        ge_re = ar.alloc([128, NPAIR, 3])
        ge_im = ar.alloc([128, NPAIR, 3])
        hi_re = ar.alloc([128, NPAIR, 3])
        hi_im = ar.alloc([128, NPAIR, 3])
        hfin = ar.alloc([128, NPAIR, 3, 2])
        hin_re, hin_im, anr, ani, w1_, w2_, kap4 = [ar.alloc([128, NPAIR]) for _ in range(7)]
        p3_top = ar.top
        c4s = k.dsem()
        k.dma("sp", iotc, iotc_d, c4s, tmax(cons["tok"]))
        k.dma("sp", h0r.rearrange("p a b -> p (a b)"), h0r_d, c4s)
        c4t = k.dma("sp", h0i.rearrange("p a b -> p (a b)"), h0i_d, c4s)
        v = k.op("dve", lambda e: e.tensor_scalar(out=kap4, in0=cons["kap"], scalar1=4.0, scalar2=None, op0=ALU.mult), tmax(cons["tok"]))
        v = k.op("dve", lambda e: e.tensor_tensor(out=anr, in0=rN, in1=cN, op=ALU.mult), tmax(rtokN, rtok32, rtok4, c2t, cons["tok"]))
        v = k.op("dve", lambda e: e.tensor_tensor(out=ani, in0=rN, in1=sN, op=ALU.mult))
        v = k.op("dve", lambda e: e.tensor_copy(out=hin_re, in_=f0p[:, 6, :, 0]))
        v = k.op("dve", lambda e: e.tensor_copy(out=hin_im, in_=f0p[:, 6, :, 1]))
        for j in range(5, -1, -1):
            k.op("dve", lambda e: e.tensor_tensor(out=w1_, in0=anr, in1=hin_re, op=ALU.mult))
            k.op("dve", lambda e: e.tensor_tensor(out=w2_, in0=ani, in1=hin_im, op=ALU.mult))
            k.op("dve", lambda e: e.tensor_tensor(out=w1_, in0=w1_, in1=w2_, op=ALU.subtract))
            k.op("dve", lambda e: e.tensor_tensor(out=w2_, in0=ani, in1=hin_re, op=ALU.mult))
            k.op("dve", lambda e: e.tensor_tensor(out=hin_re, in0=w1_, in1=f0p[:, j, :, 0], op=ALU.add))
            k.op("dve", lambda e: e.tensor_tensor(out=w1_, in0=anr, in1=hin_im, op=ALU.mult))
            k.op("dve", lambda e: e.tensor_tensor(out=w1_, in0=w1_, in1=w2_, op=ALU.add))
            v = k.op("dve", lambda e: e.tensor_tensor(out=hin_im, in0=w1_, in1=f0p[:, j, :, 1], op=ALU.add))
        k.op("dve", lambda e: e.tensor_copy(out=hi_re[:, :, 0:2], in_=h0r), [c4t])
        k.op("dve", lambda e: e.tensor_copy(out=hi_im[:, :, 0:2], in_=h0i))
        k.op("dve", lambda e: e.tensor_copy(out=hi_re[:, :, 2], in_=hin_re))
        carry_tok = k.op("dve", lambda e: e.tensor_copy(out=hi_im[:, :, 2], in_=hin_im))
        pws = [(None, None)]
        for kk in range(1, 5):
            pr_, pi_2 = ar.alloc([128, NPAIR]), ar.alloc([128, NPAIR])
            if kk == 1:
                k.op("dve", lambda e: e.tensor_copy(out=pr_, in_=cons["ab_re"]))
                k.op("dve", lambda e: e.tensor_copy(out=pi_2, in_=cons["ab_im"]))
            else:
                qr, qi = pws[kk - 1]
                k.op("dve", lambda e: e.tensor_tensor(out=w1_, in0=qr, in1=cons["ab_re"], op=ALU.mult))
                k.op("dve", lambda e: e.tensor_tensor(out=w2_, in0=qi, in1=cons["ab_im"], op=ALU.mult))
                k.op("dve", lambda e: e.tensor_tensor(out=pr_, in0=w1_, in1=w2_, op=ALU.subtract))
                k.op("dve", lambda e: e.tensor_tensor(out=w1_, in0=qr, in1=cons["ab_im"], op=ALU.mult))
                k.op("dve", lambda e: e.tensor_tensor(out=w2_, in0=qi, in1=cons["ab_re"], op=ALU.mult))
                k.op("dve", lambda e: e.tensor_tensor(out=pi_2, in0=w1_, in1=w2_, op=ALU.add))
            pws.append((pr_, pi_2))
        HP = NPAIR // 2
        csm_re = ar.alloc([128, HP, 32])
        csm_im = ar.alloc([128, HP, 32])
        ct1 = ar.alloc([128, HP, 32])
        ct2 = ar.alloc([128, HP, 32])
        cko = [ar.alloc([128, HP, 32], BF16) for _ in range(2)]
        assert ar.top <= tr_base, (ar.top, tr_base)
        c5 = k.dsem()
        cks = [k.dsem(), k.dsem()]
        ck_tok = [None, None]
        v = None
        for half in range(2):
            hs_ = slice(half * HP * 32, (half + 1) * HP * 32)
            k.dma("sp", csm_re.rearrange("p a b -> p (a b)"), cbr_d[:, hs_], c5, tmax(cons["tok"], v))
            c5t = k.dma("sp", csm_im.rearrange("p a b -> p (a b)"), cbi_d[:, hs_], c5)
            for kk in range(5):
                if kk == 0:
                    k.op("dve", lambda e: e.tensor_copy(out=cko[0], in_=csm_re), tmax(c5t, ck_tok[0]))
                    v = k.op("dve", lambda e: e.tensor_copy(out=cko[1], in_=csm_im), tmax(ck_tok[1]))
                else:
                    pr_, pi_2 = pws[kk]
                    prb = pr_[:, half * HP:(half + 1) * HP].unsqueeze(2).to_broadcast([128, HP, 32])
                    pib = pi_2[:, half * HP:(half + 1) * HP].unsqueeze(2).to_broadcast([128, HP, 32])
                    k.op("dve", lambda e: e.tensor_tensor(out=ct1, in0=csm_re, in1=prb, op=ALU.mult), [c5t])
                    k.op("dve", lambda e: e.tensor_tensor(out=ct2, in0=csm_im, in1=pib, op=ALU.mult))
                    k.op("dve", lambda e: e.tensor_tensor(out=cko[0], in0=ct1, in1=ct2, op=ALU.subtract), tmax(ck_tok[0]))
                    k.op("dve", lambda e: e.tensor_tensor(out=ct1, in0=csm_re, in1=pib, op=ALU.mult))
                    k.op("dve", lambda e: e.tensor_tensor(out=ct2, in0=csm_im, in1=prb, op=ALU.mult))
                    v = k.op("dve", lambda e: e.tensor_tensor(out=cko[1], in0=ct1, in1=ct2, op=ALU.add), tmax(ck_tok[1]))
                ck_tok[0] = k.dma("sp", scr_ck[2 * kk][:, hs_], cko[0].rearrange("p a b -> p (a b)"), cks[0], [v])
                ck_tok[1] = k.dma("sp", scr_ck[2 * kk + 1][:, hs_], cko[1].rearrange("p a b -> p (a b)"), cks[1])
        scr_done = tmax(ck_tok[0], ck_tok[1], cons["tok"], v)

        ar.top = p3_top
        wus = [ar.alloc([128, 16, 128], BF16) for _ in range(2)]
        uTs = [ar.alloc([128, T], BF16) for _ in range(2)]
        bfb = [ar.alloc([128, 8, 128], BF16) for _ in range(2)]
        ckb = [ar.alloc([128, 10, 128], BF16) for _ in range(2)]
        A_ = lambda: ar.alloc([128, NC])
        all_tok = tmax(scr_done, carry_tok, c4t)

        class PB:
            pass
        psets = []
        for _ in range(2):
            q = PB()
            q.bu_re, q.bu_im, q.ph, q.rn, q.fr, q.sn, q.cs, q.rrow = [A_() for _ in range(8)]
            q.Lb_re = [ar.alloc([128, NC], BF16) for _ in range(4)]
            q.Lb_n = [ar.alloc([128, NC], BF16) for _ in range(4)]
            q.He_re = ar.alloc([128, NC + 1], BF16)
            q.He_n = ar.alloc([128, NC + 1], BF16)
            q.t_free = all_tok
            q.t_gcopy = all_tok
            q.t_yread = all_tok
            q.t_lread = all_tok
            psets.append(q)
        tmpy = ar.alloc([128, NC])
        wu_free = [all_tok, all_tok]
        uT_free = [all_tok, all_tok]
        blk_sem = [k.dsem(), k.dsem(), k.dsem(), k.dsem()]
        blk_free = [all_tok, all_tok]
        yb_free = [bank_free[0], bank_free[1], bank_free[2], bank_free[3]]
        lset = [(5, 6), (7, 4)]
        lstep = 0
        g_done = None
        BS = {}
        LST = {"lstep": 0}

        def uproj(b):
            sb_ = b % 2
            st = {}
            st["lb1"] = k.dma("sp", bfb[sb_], scr_bft[:, :, b * 128:(b + 1) * 128].rearrange("k p c -> p k c"), blk_sem[sb_], tmax(blk_free[sb_]))
            st["lb2"] = k.dma("sp", ckb[sb_], scr_ck[:, :, b * 128:(b + 1) * 128].rearrange("k p c -> p k c"), blk_sem[2 + sb_])
            wu = wus[sb_]
            uT = uTs[sb_]
            wl = k.dma("pool", wu.rearrange("p a b -> p (a b)"), wsu[b], wsem[sb_], tmax(wu_free[sb_]), max_dma_last_dim=8192)
            p = None
            a_u = None
            for ri, (c0, c1) in enumerate(RANGES):
                n = c1 - c0
                bk = 5 + ri
                for kc in range(16):
                    p = k.op("pe", lambda e: e.matmul(ps[:, bk, 0:n], lhsT=wu[:, kc, :], rhs=xn1[:, kc, c0:c1],
                                                      start=(kc == 0), stop=(kc == 15)),
                             tmax(wl, xn1_done, bank_free[bk]), inc=(kc == 15))
                a_u = k.op("act", lambda e: e.copy(out=uT[:, c0:c1], in_=ps[:, bk, 0:n]), tmax(p, uT_free[sb_]))
                bank_free[bk] = a_u
            wu_free[sb_] = p
            st["a_u"] = a_u
            st["last_y"] = None
            st["a_e"] = {}
            BS[b] = st

        def st1(b, j4):
            sb_ = b % 2
            st = BS[b]
            uT = uTs[sb_]
            pair = 4 * b + j4
            rows = slice(32 * j4, 32 * j4 + 32)
            q = psets[pair % 2]
            a_e = None
            for r in range(4):
                bre_, bim_ = lset[LST["lstep"] % 2]
                LST["lstep"] += 1
                pl = None
                for (ri_, bk_) in ((0, bre_), (1, bim_)):
                    for (cc0, ln, tk0) in SEGS:
                        for rp in range(r + 1):
                            rhs = uT[rows, tk0 + rp:tk0 + rp + 4 * (ln - 1) + 1:4]
                            pl = k.op("pe", lambda e: e.matmul(ps[:, bk_, cc0:cc0 + ln], lhsT=bfb[sb_][rows, 2 * (r - rp) + ri_, :], rhs=rhs,
                                                               start=(rp == 0), stop=(rp == r), tile_position=(32 * j4, 0)),
                                      tmax(st["a_u"], st["lb1"], bank_free[bk_]), inc=(ri_ == 1 and cc0 == 16 and rp == r))
                k.op("act", lambda e: e.copy(out=q.Lb_re[r], in_=ps[:, bre_, 0:NC]), tmax(pl, q.t_lread))
                a_e = k.op("act", lambda e: e.activation(out=q.Lb_n[r], in_=ps[:, bim_, 0:NC], func=ACT.Copy, scale=-1.0))
                if r == 3:
                    k.op("act", lambda e: e.copy(out=q.bu_re, in_=ps[:, bre_, 0:NC]), tmax(q.t_free))
                    a_e = k.op("act", lambda e: e.copy(out=q.bu_im, in_=ps[:, bim_, 0:NC]))
                bank_free[bre_] = a_e
                bank_free[bim_] = a_e
            st["a_e"][j4] = a_e

        def st2(b, j4):
            sb_ = b % 2
            st = BS[b]
            pair = 4 * b + j4
            rows = slice(32 * j4, 32 * j4 + 32)
            q = psets[pair % 2]
            bu_re, bu_im, ph, rn, fr, sn, cs, rrow = q.bu_re, q.bu_im, q.ph, q.rn, q.fr, q.sn, q.cs, q.rrow
            Lb_re, Lb_n, He_re, He_n = q.Lb_re, q.Lb_n, q.He_re, q.He_n
            a_e = st["a_e"][j4]
            kcol = kap4[:, pair:pair + 1]
            v = k.op("dve", lambda e: e.tensor_scalar(out=ph, in0=iotc, scalar1=kcol, scalar2=None, op0=ALU.mult), tmax(q.t_gcopy))
            v = k.op("dve", lambda e: e.tensor_scalar(out=rn, in0=ph, scalar1=MAGIC, scalar2=MAGIC, op0=ALU.add, op1=ALU.subtract))
            v = k.op("dve", lambda e: e.tensor_tensor(out=fr, in0=ph, in1=rn, op=ALU.subtract))
            a = k.op("act", lambda e: e.activation(out=sn, in_=fr, func=ACT.Sin, scale=TWO_PI), tmax(v, q.t_free))
            a = k.op("act", lambda e: e.activation(out=rn, in_=fr, func=ACT.Abs))
            a_t = k.op("act", lambda e: e.activation(out=cs, in_=rn, func=ACT.Sin, scale=-TWO_PI, bias=float(np.pi / 2)))
            a_r = k.op("act", lambda e: e.activation(out=rrow, in_=iotc, func=ACT.Identity, scale=0.0, bias=rho4[:, pair:pair + 1]))
            v = k.op("dve", lambda e: e.tensor_tensor(out=ph, in0=cs, in1=bu_re, op=ALU.mult), tmax(a_t, a_e, a_r))
            v = k.op("dve", lambda e: e.tensor_tensor(out=rn, in0=sn, in1=bu_im, op=ALU.mult))
            v = k.op("dve", lambda e: e.tensor_tensor(out=fr, in0=ph, in1=rn, op=ALU.add))
            v = k.op("dve", lambda e: e.tensor_tensor(out=ph, in0=cs, in1=bu_im, op=ALU.mult))
            v = k.op("dve", lambda e: e.tensor_tensor(out=rn, in0=sn, in1=bu_re, op=ALU.mult))
            v = k.op("dve", lambda e: e.tensor_tensor(out=bu_re, in0=ph, in1=rn, op=ALU.subtract))
            for si, (cc0, ln, tk0) in enumerate(SEGS):
                k.op("dve", lambda e: e.tensor_tensor_scan(out=bu_im[:, cc0:cc0 + ln], data0=rrow[:, cc0:cc0 + ln], data1=fr[:, cc0:cc0 + ln],
                                                           initial=hi_re[:, pair, si:si + 1], op0=ALU.mult, op1=ALU.add))
                v = k.op("dve", lambda e: e.tensor_tensor_scan(out=ph[:, cc0:cc0 + ln], data0=rrow[:, cc0:cc0 + ln], data1=bu_re[:, cc0:cc0 + ln],
                                                               initial=hi_im[:, pair, si:si + 1], op0=ALU.mult, op1=ALU.add))
            gre, gim = bu_im, ph
            k.op("act", lambda e: e.copy(out=ge_re[:, pair, 0:2], in_=gre[:, 7:16:8]), [v])
            k.op("act", lambda e: e.copy(out=ge_re[:, pair, 2:3], in_=gre[:, NC - 1:NC]))
            k.op("act", lambda e: e.copy(out=ge_im[:, pair, 0:2], in_=gim[:, 7:16:8]))
            q.t_gcopy = k.op("act", lambda e: e.copy(out=ge_im[:, pair, 2:3], in_=gim[:, NC - 1:NC]))
            v = k.op("dve", lambda e: e.tensor_tensor(out=rn, in0=cs, in1=gre, op=ALU.mult))
            v = k.op("dve", lambda e: e.tensor_tensor(out=fr, in0=sn, in1=gim, op=ALU.mult))
            v = k.op("dve", lambda e: e.tensor_tensor(out=He_re[:, 1:NC + 1], in0=rn, in1=fr, op=ALU.subtract), tmax(q.t_yread))
            v = k.op("dve", lambda e: e.tensor_tensor(out=rn, in0=sn, in1=gre, op=ALU.mult))
            v = k.op("dve", lambda e: e.tensor_tensor(out=fr, in0=cs, in1=gim, op=ALU.mult))
            v = k.op("dve", lambda e: e.scalar_tensor_tensor(out=He_n[:, 1:NC + 1], in0=rn, scalar=-1.0, in1=fr, op0=ALU.mult, op1=ALU.subtract))
            v = k.op("dve", lambda e: e.tensor_copy(out=He_re[:, 0:17:8], in_=hi_re[:, pair, :]))
            v = k.op("dve", lambda e: e.tensor_scalar(out=He_n[:, 0:17:8], in0=hi_im[:, pair, :], scalar1=-1.0, scalar2=None, op0=ALU.mult))
            q.t_free = v
            py = None
            for r in range(4):
                yo = ps[rows, r, 0:NC]
                k.op("pe", lambda e: e.matmul(yo, lhsT=ckb[sb_][:, 0, rows], rhs=Lb_re[r], start=True, stop=False,
                                              tile_position=(0, 32 * j4)), tmax(v, a_e, st["lb2"], yb_free[r]), inc=False)
                k.op("pe", lambda e: e.matmul(yo, lhsT=ckb[sb_][:, 1, rows], rhs=Lb_n[r], start=False, stop=False,
                                              tile_position=(0, 32 * j4)), inc=False)
                k.op("pe", lambda e: e.matmul(yo, lhsT=ckb[sb_][:, 2 * (r + 1), rows], rhs=He_re[:, 0:NC], start=False, stop=False,
                                              tile_position=(0, 32 * j4)), inc=False)
                py = k.op("pe", lambda e: e.matmul(yo, lhsT=ckb[sb_][:, 2 * (r + 1) + 1, rows], rhs=He_n[:, 0:NC], start=False, stop=True,
                                                   tile_position=(0, 32 * j4)))
            q.t_yread = py
            q.t_lread = py
            st["last_y"] = py

        def epilogue(b):
            sb_ = b % 2
            st = BS[b]
            uT = uTs[sb_]
            last_y = st["last_y"]
            blk_free[sb_] = last_y
            v = None
            a = None
            for r in range(4):
                for (cc0, ln, tk0) in SEGS:
                    tsl = slice(tk0 + r, tk0 + r + 4 * (ln - 1) + 1, 4)
                    v = k.op("dve", lambda e: e.scalar_tensor_tensor(out=tmpy[:, cc0:cc0 + ln], in0=uT[:, tsl], scalar=dvec[:, b:b + 1],
                                                                    in1=ps[:, r, cc0:cc0 + ln], op0=ALU.mult, op1=ALU.add), tmax(last_y, a))
                    a = k.op("act", lambda e: e.activation(out=g_all[:, b, tsl], in_=tmpy[:, cc0:cc0 + ln], func=ACT.Gelu_apprx_tanh), [v])
                yb_free[r] = v
            uT_free[sb_] = tmax(last_y, v)
            return a

        order = [(b_, j_) for b_ in range(32) for j_ in range(4)]
        uproj(0)
        st1(0, 0)
        for idx, (b_, j_) in enumerate(order):
            if idx + 1 < len(order):
                nb, nj = order[idx + 1]
                if nj == 0:
                    uproj(nb)
                st1(nb, nj)
            st2(b_, j_)
            if j_ == 3:
                g_done = epilogue(b_)
        for hc in (0, C_A0 - 2, C_B0 - 2):
            g_done = k.op("act", lambda e: e.activation(out=g_all[:, :, hc:hc + 2], in_=g_all[:, :, 2:4], func=ACT.Copy, scale=0.0), [g_done])
        v = None
        for si, (cc_, ss_, gi) in enumerate(((cN, sN, 2), (c32, s32, 0), (c32, s32, 1))):
            k.op("dve", lambda e: e.tensor_tensor(out=w1_, in0=cc_, in1=ge_re[:, :, gi], op=ALU.mult), tmax(psets[0].t_gcopy, psets[1].t_gcopy, g_done))
            k.op("dve", lambda e: e.tensor_tensor(out=w2_, in0=ss_, in1=ge_im[:, :, gi], op=ALU.mult))
            k.op("dve", lambda e: e.tensor_tensor(out=hfin[:, :, si, 0], in0=w1_, in1=w2_, op=ALU.subtract))
            k.op("dve", lambda e: e.tensor_tensor(out=w1_, in0=ss_, in1=ge_re[:, :, gi], op=ALU.mult))
            k.op("dve", lambda e: e.tensor_tensor(out=w2_, in0=cc_, in1=ge_im[:, :, gi], op=ALU.mult))
            v = k.op("dve", lambda e: e.tensor_tensor(out=hfin[:, :, si, 1], in0=w1_, in1=w2_, op=ALU.add))
        fsem = k.dsem()
        hf_tok = k.dma("sp", hf_o, hfin.rearrange("p a b c -> p (a b c)"), fsem, [v])

        ar.top = BASE
        y2 = ar.alloc([128, 32, T], BF16)
        wg = [ar.alloc([128, 32, 128], BF16) for _ in range(2)]
        wz = [ar.alloc([128, 16, 128], BF16) for _ in range(2)]
        sg = [ar.alloc([128, 366]) for _ in range(2)]
        zs = [ar.alloc([128, 366]) for _ in range(2)]
        guard = tmax(hf_tok, g_done, psets[0].t_yread, psets[1].t_yread, v)
        wg_free = [guard, guard]
        tmp_free = [guard, guard]
        step = 0
        y2_done = None
        for ob in range(32):
            s = ob % 2
            wlg = k.dma("pool", wg[s].rearrange("p a b -> p (a b)"), wgl[ob], wsem[s], tmax(wg_free[s]), max_dma_last_dim=8192)
            wlz = k.dma("pool", wz[s].rearrange("p a b -> p (a b)"), wsz[ob], wsem[2 + s], max_dma_last_dim=8192)
            p = None
            for (c0, c1) in RANGES:
                n = c1 - c0
                st_ = step % 2
                step += 1
                bg_, bz_ = st_ * 2, st_ * 2 + 1
                for kb in range(32):
                    p = k.op("pe", lambda e: e.matmul(ps[:, bg_, 0:n], lhsT=wg[s][:, kb, :], rhs=g_all[:, kb, c0:c1],
                                                      start=(kb == 0), stop=(kb == 31)), tmax(wlg, g_done, bank_free[bg_]), inc=False)
                for kc in range(16):
                    p = k.op("pe", lambda e: e.matmul(ps[:, bz_, 0:n], lhsT=wz[s][:, kc, :], rhs=xn1[:, kc, c0:c1],
                                                      start=(kc == 0), stop=(kc == 15)), tmax(wlz, bank_free[bz_]), inc=(kc == 15))
                a1 = k.op("act", lambda e: e.activation(out=sg[st_][:, 0:n], in_=ps[:, bg_, 0:n], func=ACT.Sigmoid,
                                                        bias=bgl[:, ob:ob + 1]), tmax(p, tmp_free[st_]))
                a2 = k.op("act", lambda e: e.activation(out=zs[st_][:, 0:n], in_=ps[:, bz_, 0:n], func=ACT.Silu))
                bank_free[bg_] = a2
                bank_free[bz_] = a2
                v = k.op("dve", lambda e: e.tensor_tensor(out=sg[st_][:, 0:n], in0=sg[st_][:, 0:n], in1=g_all[:, ob, c0:c1], op=ALU.mult), [a2])
                v = k.op("dve", lambda e: e.tensor_tensor(out=y2[:, ob, c0:c1], in0=sg[st_][:, 0:n], in1=zs[st_][:, 0:n], op=ALU.mult))
                tmp_free[st_] = v
                y2_done = v
            wg_free[s] = p
        glu_pe_done = p

        ar.top = 0
        h2T = ar.alloc([128, 16, T])
        wo = [ar.alloc([128, 32, 128], BF16) for _ in range(2)]
        h1c = [ar.alloc([128, T]) for _ in range(2)]
        assert ar.top <= BASE
        ar.top = BASE + 17568
        xt = [ar.alloc([128, D]) for _ in range(2)]
        sq2 = [ar.alloc([128, T]) for _ in range(2)]
        rb2 = ar.alloc([128, T])
        ones2 = ar.alloc([128, 128])
        guard = tmax(glu_pe_done, y2_done)
        vm2 = k.op("dve", lambda e: e.memset(ones2, 1.0 / D), guard)
        wo_free = [guard, guard]
        h1c_free = [guard, guard]
        h1sem = [k.dsem(), k.dsem()]
        last_h = None
        for db in range(16):
            s = db % 2
            wl = k.dma("pool", wo[s].rearrange("p a b -> p (a b)"), wso[db], wsem[s], tmax(wo_free[s]), max_dma_last_dim=8192)
            hl = k.dma("sp", h1c[s], h1T_i[:, db * T:(db + 1) * T], h1sem[s], tmax(h1c_free[s]))
            p = None
            for ri, (c0, c1) in enumerate(RANGES):
                n = c1 - c0
                bk = 2 + (db % 2) * 3 + ri
                for eb in range(32):
                    p = k.op("pe", lambda e: e.matmul(ps[:, bk, 0:n], lhsT=wo[s][:, eb, :], rhs=y2[:, eb, c0:c1],
                                                      start=(eb == 0), stop=(eb == 31)), tmax(wl, y2_done, bank_free[bk]), inc=(eb == 31))
                bank_free[bk] = k.op("dve", lambda e: e.tensor_tensor(out=h2T[:, db, c0:c1], in0=ps[:, bk, 0:n], in1=h1c[s][:, c0:c1],
                                                                     op=ALU.add), tmax(p, hl, guard))
                last_h = bank_free[bk]
            wo_free[s] = p
            h1c_free[s] = last_h
        sq_free = [guard, guard]
        p = None
        for db in range(16):
            s = db % 2
            a = k.op("act", lambda e: e.activation(out=sq2[s], in_=h2T[:, db, :], func=ACT.Square), tmax(last_h, sq_free[s]))
            for ri, (c0, c1) in enumerate(RANGES):
                n = c1 - c0
                p = k.op("pe", lambda e: e.matmul(ps[:, 2 + ri, 0:n], lhsT=ones2, rhs=sq2[s][:, c0:c1], start=(db == 0), stop=(db == 15)),
                         tmax(a, vm2, bank_free[2 + ri], last_h), inc=(ri == 2))
            sq_free[s] = p
        v = None
        for ri, (c0, c1) in enumerate(RANGES):
            n = c1 - c0
            v = k.op("dve", lambda e: e.tensor_scalar(out=rb2[:, c0:c1], in0=ps[:, 2 + ri, 0:n], scalar1=EPS, scalar2=None, op0=ALU.add), [p])
            bank_free[2 + ri] = v
        a = k.op("act", lambda e: e.activation(out=rb2, in_=rb2, func=ACT.Sqrt), [v])
        v = k.op("dve", lambda e: e.reciprocal(out=rb2, in_=rb2), [a])
        for db in range(16):
            v = k.op("dve", lambda e: e.scalar_tensor_tensor(out=h2T[:, db, :], in0=h2T[:, db, :], scalar=nw[:, 32 + db:33 + db],
                                                            in1=rb2, op0=ALU.mult, op1=ALU.mult))
        yn_done = v
        osem = [k.dsem(), k.dsem()]
        xt_free = [None, None]
        outs = []
        for i in range(NTT):
            r0 = 128 * i
            nr = min(128, T - r0)
            s = i % 2
            p = None
            a3 = None
            for g in range(4):
                bk = g % 2
                for q in range(4):
                    kc = 4 * g + q
                    p = k.op("pe", lambda e: e.transpose(out=ps[0:nr, bk, q * 128:(q + 1) * 128], in_=h2T[:, kc, r0:r0 + nr],
                                                         identity=ident), tmax(yn_done, bank_free[bk]), inc=(q == 3))
                a3 = k.op("act", lambda e: e.copy(out=xt[s][0:nr, 512 * g:512 * (g + 1)], in_=ps[0:nr, bk, :]), tmax(p, xt_free[s]))
                bank_free[bk] = a3
            od = k.dma("sp", y_o[r0:r0 + nr, :], xt[s][0:nr, :], osem[s], [a3])
            xt_free[s] = od
            outs.append(od)
        k.wait("sp", tmax(hf_tok, outs[-1], outs[-2]))
    return nc

def host_layout_a(inp):
    f = np.float32
    xp = inp["x_prompt"][0]
    P = np.concatenate([np.zeros((16, D), f), inp["meta_tokens"].astype(f), xp], axis=0)
    Ppad = np.concatenate([np.zeros((2, D), f), P], axis=0)
    xs = inp["x_sample"]
    z2 = np.zeros((2, D), f)
    xin = []
    for c in range(NCORE):
        rows = [Ppad[SEG * c:SEG * c + SEG + 2], z2, xs[2 * c], z2, xs[2 * c + 1]]
        xin.append(np.ascontiguousarray(np.concatenate(rows, axis=0)))
    nw = np.concatenate([inp["norm_w"][0].reshape(16, 128).T, inp["norm_w"][1].reshape(16, 128).T], axis=1)
    wci = inp["conv_w_in"][0].reshape(16, 128, 4, 32, 128)
    wci = np.ascontiguousarray(wci.transpose(3, 1, 0, 2, 4)).reshape(32, 128, 16 * 512)
    cwv = np.concatenate([inp["conv_w"][0], inp["conv_b"]], axis=0)
    cw = np.ascontiguousarray(cwv.reshape(4, 32, 128).transpose(2, 1, 0)).reshape(128, 128)
    cache = inp["cache_conv"][0]
    ccs = []
    for c in range(NCORE):
        a = cache[2 * c:2 * c + 2].reshape(2, 2, 32, 128)
        ccs.append(np.ascontiguousarray(a.transpose(3, 2, 0, 1)).reshape(128, 128))
    wco = inp["conv_w_out"][0].reshape(32, 128, 16, 128)
    wco = np.ascontiguousarray(wco.transpose(2, 1, 0, 3)).reshape(16, 128, 32 * 128)
    sm = lambda a: np.ascontiguousarray(a.reshape(NPAIR, 2, 64).transpose(1, 2, 0)).reshape(128, NPAIR)
    are = sm(inp["ssm_a_re"][0])
    aim = sm(inp["ssm_a_im"][0])
    ldt = sm(np.repeat(inp["ssm_log_dt"][0][:, None], 64, axis=1))
    smb = lambda a: np.ascontiguousarray(a.reshape(NPAIR, 2, 64, 16).transpose(1, 2, 0, 3)).reshape(128, NPAIR * 16)
    bre = smb(inp["ssm_b_re"][0])
    bim = smb(inp["ssm_b_im"][0])
    wsu = inp["ssm_w_in"][0][:, :E].reshape(16, 128, 32, 128)
    wsu = np.ascontiguousarray(wsu.transpose(2, 1, 0, 3)).reshape(32, 128, 16 * 128)
    io = np.zeros((T,), f)
    io[C_P0:C_P0 + SEG] = np.arange(1, SEG + 1)
    io[C_A0:C_A0 + 32] = np.arange(1, 33)
    io[C_B0:C_B0 + 32] = np.arange(1, 33)
    iota = np.ascontiguousarray(np.broadcast_to(io[None, :], (128, T)))
    shared = dict(ident=np.eye(128, dtype=f), nw=np.ascontiguousarray(nw.astype(f)), wci=wci, cw=cw, wco=wco, are=are, aim=aim,
                  ldt=ldt, bre=bre, bim=bim, wsu=wsu, iota=iota)
    return [dict(shared, xin=xin[c], cc=ccs[c]) for c in range(NCORE)]


def host_layout_b(inp, res_a):
    f = np.float32
    base = host_layout_a.cache
    f0 = [np.asarray(r["f0"]).reshape(128, NPAIR, 2) for r in res_a]
    nw48 = np.ascontiguousarray(np.concatenate([base["nw"], inp["final_norm_w"].astype(f).reshape(16, 128).T], axis=1))

    def cbd(cm):
        c4 = cm.reshape(NPAIR, 2, 16, 64)
        out = np.zeros((2, 64, NPAIR, 2, 16), f)
        for g2 in range(2):
            out[g2, :, :, g2, :] = c4[:, g2].transpose(2, 0, 1)
        return out.reshape(128, NPAIR * 32)
    cbr = cbd(inp["ssm_c_re"][0])
    cbi = cbd(inp["ssm_c_im"][0])
    dvec = np.ascontiguousarray(inp["ssm_d"][0].reshape(32, 128).T)
    bgl = np.ascontiguousarray(inp["ssm_b_glu"][0].reshape(32, 128).T)
    wsz = inp["ssm_w_in"][0][:, E:].reshape(16, 128, 32, 128)
    wsz = np.ascontiguousarray(wsz.transpose(2, 1, 0, 3)).reshape(32, 128, 16 * 128)
    wgl = inp["ssm_w_glu"][0].reshape(32, 128, 32, 128)
    wgl = np.ascontiguousarray(wgl.transpose(2, 1, 0, 3)).reshape(32, 128, 32 * 128)
    wso = inp["ssm_w_out"][0].reshape(32, 128, 16, 128)
    wso = np.ascontiguousarray(wso.transpose(2, 1, 0, 3)).reshape(16, 128, 32 * 128)
    ioc = np.concatenate([np.arange(1, 9), np.arange(1, 9), np.arange(1, 258)]).astype(f)
    iotc = np.ascontiguousarray(np.broadcast_to(ioc[None, :], (128, 273)))
    shared = dict(ident=base["ident"], nw=nw48, iotc=iotc, are=base["are"], aim=base["aim"], ldt=base["ldt"], bre=base["bre"],
                  bim=base["bim"], cbr=cbr, cbi=cbi, dvec=dvec, bgl=bgl, wsu=base["wsu"], wsz=wsz, wgl=wgl, wso=wso)
    maps = []
    for c in range(NCORE):
        fp = np.zeros((128, 7, NPAIR, 2), f)
        for j in range(7):
            if c - 1 - j >= 0:
                fp[:, j] = f0[c - 1 - j]
        hs = []
        for st in (inp["state_ssm_re"][0], inp["state_ssm_im"][0]):
            a = st[2 * c:2 * c + 2].reshape(2, NPAIR, 2, 64).transpose(2, 3, 1, 0)
            hs.append(np.ascontiguousarray(a).reshape(128, NPAIR * 2))
        maps.append(dict(shared, h1T_i=np.asarray(res_a[c]["h1T"]), f0prev=fp.reshape(128, 7 * NPAIR * 2), h0r=hs[0], h0i=hs[1]))
    return maps


def assemble(inp, res_a, res_b):
    f = np.float32
    y_prompt = np.zeros((1, 8192, D), f)
    y_sample = np.zeros((16, 32, D), f)
    ncp = np.zeros((1, 1, 2, E), f)
    ncs = np.zeros((1, 16, 2, E), f)
    hpr = np.zeros((1, 1, 256, 64), f)
    hpi = np.zeros((1, 1, 256, 64), f)
    hsr = np.zeros((1, 16, 256, 64), f)
    hsi = np.zeros((1, 16, 256, 64), f)
    for c in range(NCORE):
        yo = np.asarray(res_b[c]["yout"])
        lo = SEG * c - 32
        c0 = C_P0 + max(0, -lo)
        y_prompt[0, max(lo, 0):lo + SEG] = yo[c0:C_P0 + SEG]
        y_sample[2 * c] = yo[C_A0:C_A0 + 32]
        y_sample[2 * c + 1] = yo[C_B0:C_B0 + 32]
        ncv = np.asarray(res_a[c]["ncv"]).reshape(128, 32, 3, 2)
        hf = np.asarray(res_b[c]["hfin"]).reshape(2, 64, NPAIR, 3, 2)
        hf = hf.transpose(3, 4, 2, 0, 1).reshape(3, 2, 256, 64)
        if c == NCORE - 1:
            ncp[0, 0] = ncv[:, :, 0, :].transpose(2, 1, 0).reshape(2, E)
            hpr[0, 0] = hf[0, 0]
            hpi[0, 0] = hf[0, 1]
        for s in range(2):
            ncs[0, 2 * c + s] = ncv[:, :, 1 + s, :].transpose(2, 1, 0).reshape(2, E)
            hsr[0, 2 * c + s] = hf[1 + s, 0]
            hsi[0, 2 * c + s] = hf[1 + s, 1]
    return (y_prompt, y_sample, ncp, ncs, hpr, hpi, hsr, hsi)


_PROGS = {}


def kernel(**inp):
    inp = {k_: np.asarray(v) for k_, v in inp.items()}
    maps_a = host_layout_a(inp)
    host_layout_a.cache = maps_a[0]
    if "a" not in _PROGS:
        _PROGS["a"] = build_a()
    res_a = run_bass_kernel_spmd(_PROGS["a"], maps_a, core_ids=list(range(NCORE))).results
    maps_b = host_layout_b(inp, res_a)
    if "b" not in _PROGS:
        _PROGS["b"] = build_b()
    res_b = run_bass_kernel_spmd(_PROGS["b"], maps_b, core_ids=list(range(NCORE))).results
    return assemble(inp, res_a, res_b)
```

```python
import contextlib
import numpy as np
import concourse.bass as bass
import concourse.mybir as mybir
from concourse.bass_utils import run_bass_kernel_spmd

F32 = mybir.dt.float32
BF16 = mybir.dt.bfloat16
ACT = mybir.ActivationFunctionType
ALU = mybir.AluOpType

D = 2048
E = 4096
NCORE = 8
SEG = 1028
T = 1098
C_P0, C_A0, C_B0 = 2, 1032, 1066
RANGES = [(0, 366), (366, 732), (732, 1098)]
NTT = 9
EPS = 1e-6
MAGIC = 12582912.0
TWO_PI = float(2.0 * np.pi)
NPAIR = 128
STRICT_SAME_ENGINE = True


class DSem:
    def __init__(self, h):
        self.h = h
        self.val = 0


class K:
    def __init__(self, nc, es):
        self.nc = nc
        self.es = es
        self.eng = {"pe": nc.tensor, "act": nc.scalar, "dve": nc.vector, "pool": nc.gpsimd, "sp": nc.sync}
        self.sem = {k: es.enter_context(nc.semaphore("s_" + k)) for k in ("pe", "act", "dve", "pool")}
        self.cnt = {k: 0 for k in self.sem}
        self.seen = {k: {} for k in self.eng}
        self.nd = 0
        self.selfseen = {k_: 0 for k_ in self.sem}

    def dsem(self):
        self.nd += 1
        return DSem(self.es.enter_context(self.nc.semaphore("d%d" % self.nd)))

    def wait(self, en, toks):
        for t in toks:
            if t is None:
                continue
            if t[0] == "c":
                _, src, val = t
                if src == en:
                    continue
                if self.seen[en].get(src, 0) >= val:
                    continue
                self.eng[en].wait_ge(self.sem[src], val)
                self.seen[en][src] = val
            else:
                _, ds, val = t
                key = id(ds)
                if self.seen[en].get(key, 0) >= val:
                    continue
                self.eng[en].wait_ge(ds.h, val)
                self.seen[en][key] = val

    def wait_self(self, en, tok):
        if tok is None:
            return
        _, src, val = tok
        assert src == en
        self.eng[en].wait_ge(self.sem[src], val)

    def op(self, en, fn, waits=(), inc=True):
        self.wait(en, waits)
        if STRICT_SAME_ENGINE and en in ("act", "dve", "pool") and self.cnt[en] > self.selfseen[en]:
            self.eng[en].wait_ge(self.sem[en], self.cnt[en])
            self.selfseen[en] = self.cnt[en]
        ins = fn(self.eng[en])
        if inc:
            self.cnt[en] += 1
            ins.then_inc(self.sem[en], 1)
            return ("c", en, self.cnt[en])
        return None

    def dma(self, en, out, in_, ds, waits=(), **kw):
        self.wait(en, waits)
        ins = self.eng[en].dma_start(out=out, in_=in_, **kw)
        ds.val += 16
        ins.then_inc(ds.h, 16)
        return ("d", ds, ds.val)


def tmax(*toks):
    out = []
    for t in toks:
        if t is None:
            continue
        if isinstance(t, list):
            out.extend(t)
        else:
            out.append(t)
    return out


class Arena:
    def __init__(self, ap, nwords):
        self.ap = ap
        self.n = nwords
        self.top = 0

    def alloc(self, shape, dt=F32):
        n = 1
        for s_ in shape[1:]:
            n *= s_
        words = (n + 1) // 2 if dt == BF16 else n
        words = (words + 1) // 2 * 2
        off = self.top
        self.top += words
        assert self.top <= self.n, ("arena overflow", self.top, self.n)
        v = self.ap[:, off:off + words]
        if dt == BF16:
            v = v.bitcast(BF16)
        v = v[:, 0:n]
        if len(shape) == 3:
            v = v.rearrange("p (a b) -> p a b", a=shape[1])
        elif len(shape) == 4:
            v = v.rearrange("p (a b c) -> p a b c", a=shape[1], b=shape[2])
        return v


def emit_ssm_consts(k, ar, ps, ident, are, aim, ldt, bre, bim, waits, bank_free, nsteps_list, npow=1, dram_bft=None):
    A = lambda: ar.alloc([128, NPAIR])
    dt_, xr, rho, kap, fr, t0, t1, sn, cs, fre, fim, den, t2 = [A() for _ in range(13)]
    a = k.op("act", lambda e: e.activation(out=dt_, in_=ldt, func=ACT.Exp), waits)
    v = k.op("dve", lambda e: e.tensor_tensor(out=xr, in0=are, in1=dt_, op=ALU.mult), tmax(a, waits))
    a = k.op("act", lambda e: e.activation(out=rho, in_=xr, func=ACT.Exp), [v])
    v = k.op("dve", lambda e: e.scalar_tensor_tensor(out=kap, in0=aim, scalar=1.0 / TWO_PI, in1=dt_, op0=ALU.mult, op1=ALU.mult))

    def cossin(mult, c_out, s_out):
        v_ = k.op("dve", lambda e: e.tensor_scalar(out=t0, in0=kap, scalar1=float(mult), scalar2=None, op0=ALU.mult))
        v_ = k.op("dve", lambda e: e.tensor_scalar(out=t1, in0=t0, scalar1=MAGIC, scalar2=MAGIC, op0=ALU.add, op1=ALU.subtract))
        v_ = k.op("dve", lambda e: e.tensor_tensor(out=fr, in0=t0, in1=t1, op=ALU.subtract))
        a_ = k.op("act", lambda e: e.activation(out=s_out, in_=fr, func=ACT.Sin, scale=TWO_PI), [v_])
        a_ = k.op("act", lambda e: e.activation(out=t1, in_=fr, func=ACT.Abs))
        k.wait_self("act", a_)
        a_ = k.op("act", lambda e: e.activation(out=c_out, in_=t1, func=ACT.Sin, scale=-TWO_PI, bias=float(np.pi / 2)))
        k.wait("dve", [a_])
        return a_

    a = cossin(1.0, cs, sn)
    abr, abi = t0, t1
    ab_re = A() if npow > 1 else abr
    ab_im = A() if npow > 1 else None
    v = k.op("dve", lambda e: e.tensor_tensor(out=ab_re, in0=rho, in1=cs, op=ALU.mult), [a])
    v = k.op("dve", lambda e: e.tensor_scalar(out=abr, in0=ab_re, scalar1=-1.0, scalar2=None, op0=ALU.add))
    v = k.op("dve", lambda e: e.tensor_tensor(out=abi, in0=rho, in1=sn, op=ALU.mult))
    if npow > 1:
        v = k.op("dve", lambda e: e.tensor_copy(out=ab_im, in_=abi))
    v = k.op("dve", lambda e: e.tensor_tensor(out=den, in0=are, in1=are, op=ALU.mult))
    v = k.op("dve", lambda e: e.tensor_tensor(out=t2, in0=aim, in1=aim, op=ALU.mult))
    v = k.op("dve", lambda e: e.tensor_tensor(out=den, in0=den, in1=t2, op=ALU.add))
    v = k.op("dve", lambda e: e.reciprocal(out=den, in_=den))
    v = k.op("dve", lambda e: e.tensor_tensor(out=fre, in0=abr, in1=are, op=ALU.mult))
    v = k.op("dve", lambda e: e.tensor_tensor(out=t2, in0=abi, in1=aim, op=ALU.mult))
    v = k.op("dve", lambda e: e.tensor_tensor(out=fre, in0=fre, in1=t2, op=ALU.add))
    v = k.op("dve", lambda e: e.tensor_tensor(out=fre, in0=fre, in1=den, op=ALU.mult))
    v = k.op("dve", lambda e: e.tensor_tensor(out=fim, in0=abi, in1=are, op=ALU.mult))
    v = k.op("dve", lambda e: e.tensor_tensor(out=t2, in0=abr, in1=aim, op=ALU.mult))
    v = k.op("dve", lambda e: e.tensor_tensor(out=fim, in0=fim, in1=t2, op=ALU.subtract))
    v = k.op("dve", lambda e: e.tensor_tensor(out=fim, in0=fim, in1=den, op=ALU.mult))
    rots = {}
    for nst in nsteps_list:
        cN, sN, rN = A(), A(), A()
        a = cossin(float(nst), cN, sN)
        a = k.op("act", lambda e: e.activation(out=rN, in_=xr, func=ACT.Exp, scale=float(nst)))
        rots[nst] = (cN, sN, rN, a)
    if dram_bft is None:
        bft_re = [ar.alloc([128, 32, 128], BF16) for _ in range(npow)]
        bft_im = [ar.alloc([128, 32, 128], BF16) for _ in range(npow)]
    else:
        bft_re = bft_im = None
    pw_re, pw_im, pa, pb_ = sn, cs, den, t2
    keep_top = ar.top
    bd_re = ar.alloc([128, NPAIR, 32], BF16)
    bd_im = ar.alloc([128, NPAIR, 32], BF16)
    u1 = ar.alloc([128, NPAIR, 16])
    u2 = ar.alloc([128, NPAIR, 16])
    ident_bf = ar.alloc([128, 128], BF16)
    k.op("dve", lambda e: e.tensor_copy(out=ident_bf, in_=ident))
    k.op("dve", lambda e: e.memset(bd_re, 0.0))
    k.op("dve", lambda e: e.memset(bd_im, 0.0))
    b3r = bre.rearrange("p (j c) -> p j c", c=16)
    b3i = bim.rearrange("p (j c) -> p j c", c=16)
    k.op("dve", lambda e: e.tensor_copy(out=pw_re, in_=fre))
    k.op("dve", lambda e: e.tensor_copy(out=pw_im, in_=fim))
    if dram_bft is not None:
        stage = [ar.alloc([128, 32, 128], BF16) for _ in range(2)]
        stage_sem = [k.dsem(), k.dsem()]
        stage_tok = [None, None]
        sink_toks = [None, None]
    else:
        sink_toks = [None, None]
    last = None
    vlast = None
    i = 0
    for pw in range(npow):
        if pw > 0:
            k.op("dve", lambda e: e.tensor_tensor(out=pa, in0=pw_re, in1=ab_re, op=ALU.mult), tmax(last))
            k.op("dve", lambda e: e.tensor_tensor(out=pb_, in0=pw_im, in1=ab_im, op=ALU.mult))
            k.op("dve", lambda e: e.tensor_tensor(out=pa, in0=pa, in1=pb_, op=ALU.subtract))
            k.op("dve", lambda e: e.tensor_tensor(out=pb_, in0=pw_re, in1=ab_im, op=ALU.mult))
            k.op("dve", lambda e: e.tensor_tensor(out=pw_im, in0=pw_im, in1=ab_re, op=ALU.mult))
            k.op("dve", lambda e: e.tensor_tensor(out=pw_im, in0=pw_im, in1=pb_, op=ALU.add))
            k.op("dve", lambda e: e.tensor_copy(out=pw_re, in_=pa))
        frb = pw_re.unsqueeze(2).to_broadcast([128, NPAIR, 16])
        fib = pw_im.unsqueeze(2).to_broadcast([128, NPAIR, 16])
        k.op("dve", lambda e: e.tensor_tensor(out=u1, in0=b3r, in1=frb, op=ALU.mult), tmax(last))
        k.op("dve", lambda e: e.tensor_tensor(out=u2, in0=b3i, in1=fib, op=ALU.mult))
        for (lo, hi, c0) in ((0, 64, 0), (64, 128, 16)):
            k.op("dve", lambda e: e.tensor_tensor(out=bd_re[lo:hi, :, c0:c0 + 16], in0=u1[lo:hi], in1=u2[lo:hi], op=ALU.subtract))
        k.op("dve", lambda e: e.tensor_tensor(out=u1, in0=b3i, in1=frb, op=ALU.mult))
        k.op("dve", lambda e: e.tensor_tensor(out=u2, in0=b3r, in1=fib, op=ALU.mult))
        for (lo, hi, c0) in ((0, 64, 0), (64, 128, 16)):
            vlast = k.op("dve", lambda e: e.tensor_tensor(out=bd_im[lo:hi, :, c0:c0 + 16], in0=u1[lo:hi], in1=u2[lo:hi], op=ALU.add))
        for ri_, src in enumerate((bd_re, bd_im)):
            if dram_bft is None:
                dst = (bft_re, bft_im)[ri_][pw]
                dst_wait = None
            else:
                dst = stage[ri_]
                dst_wait = stage_tok[ri_]
            for b in range(32):
                bk = i % 2
                pv = ps[:, bk, 0:64].bitcast(BF16)
                p = k.op("pe", lambda e: e.transpose(out=pv, in_=src[:, 4 * b:4 * b + 4, :].rearrange("p a c -> p (a c)"),
                                                     identity=ident_bf), tmax(vlast, bank_free[bk]))
                bank_free[bk] = k.op("act", lambda e: e.copy(out=dst[:, b, :], in_=pv), tmax(p, dst_wait))
                last = bank_free[bk]
                i += 1
            if dram_bft is not None:
                stage_tok[ri_] = k.dma("sp", dram_bft[2 * pw + ri_], dst.rearrange("p a b -> p (a b)"), stage_sem[ri_], [last])
                sink_toks[ri_] = stage_tok[ri_]
    return dict(rho=rho, kap=kap, xr=xr, bft_re=bft_re, bft_im=bft_im, rots=rots, tok=tmax(last, vlast, sink_toks[0], sink_toks[1]),
                keep_top=keep_top, ab_re=ab_re, ab_im=ab_im)


class PairBufs:
    def __init__(self, ar, ncols):
        A = lambda: ar.alloc([128, ncols])
        self.bu_re, self.bu_im = A(), A()
        self.ph, self.rn, self.fr = A(), A(), A()
        self.ab = self.rn
        self.sn, self.cs, self.rrow = A(), A(), A()
        self.zre, self.zim, self.gre, self.gim = A(), A(), A(), A()
        self.t_prerot = None
        self.t_tabact = None
        self.t_scan = None
        self.t_gread = None


AR_WORDS = 51500


def build_a(stop_after=99):
    nc = bass.Bass("TRN2", target_bir_lowering=False)
    dr = lambda name, shape, kind="ExternalInput": nc.dram_tensor(name, shape, F32, kind=kind).ap()
    xin = dr("xin", [T, D])
    ident_d = dr("ident", [128, 128])
    nw_d = dr("nw", [128, 32])
    wci = dr("wci", [32, 128, 16 * 512])
    cw_d = dr("cw", [128, 32 * 4])
    cc_d = dr("cc", [128, 32 * 4])
    wco = dr("wco", [16, 128, 32 * 128])
    are_d = dr("are", [128, NPAIR])
    aim_d = dr("aim", [128, NPAIR])
    ldt_d = dr("ldt", [128, NPAIR])
    bre_d = dr("bre", [128, NPAIR * 16])
    bim_d = dr("bim", [128, NPAIR * 16])
    wsu = dr("wsu", [32, 128, 16 * 128])
    iota_d = dr("iota", [128, T])
    h1T_o = dr("h1T", [128, 16 * T], "ExternalOutput")
    ncv_o = dr("ncv", [128, 32 * 6], "ExternalOutput")
    f0_o = dr("f0", [128, NPAIR * 2], "ExternalOutput")

    with contextlib.ExitStack() as es:
        k = K(nc, es)
        arena_t = es.enter_context(nc.sbuf_tensor("arena", [128, AR_WORDS], F32))
        ar = Arena(arena_t[:], AR_WORDS)
        ps = es.enter_context(nc.psum_tensor("ps", [128, 8, 512], F32))
        small_t = es.enter_context(nc.sbuf_tensor("small", [128, 800], F32))
        sm = Arena(small_t[:], 800)
        ident = sm.alloc([128, 128])
        nw = sm.alloc([128, 32])
        cw = sm.alloc([128, 32, 4])
        cc = sm.alloc([128, 32, 2, 2])
        ncv = sm.alloc([128, 32, 6])
        csem = k.dsem()
        k.dma("sp", ident, ident_d, csem)
        k.dma("sp", nw, nw_d, csem)
        k.dma("sp", cw.rearrange("p a b -> p (a b)"), cw_d, csem)
        cst = k.dma("sp", cc.rearrange("p a b c -> p (a b c)"), cc_d, csem)
        xsem = [k.dsem(), k.dsem()]
        wsem = [k.dsem(), k.dsem()]
        bank_free = [None] * 8

        ar.top = 0
        y0T = ar.alloc([128, 32, T], BF16)
        top_h1 = ar.top
        xn = ar.alloc([128, 16, T], BF16)
        wc = [ar.alloc([128, 16, 512], BF16) for _ in range(2)]
        top_s2 = ar.top
        xs = [ar.alloc([128, D]) for _ in range(2)]
        ss = ar.alloc([128, 16])
        rstd = ar.alloc([128, 16])
        ar.top = AR_WORDS - 2 * D
        xbuf = [ar.alloc([128, D]) for _ in range(2)]

        k.op("dve", lambda e: e.memset(ss, 0.0))
        ss_zero = k.op("dve", lambda e: e.memset(rstd, 0.0))
        xbuf_free = [None, None]
        xs_free = [None, None]
        last_xn = None
        for i in range(NTT):
            r0 = 128 * i
            nr = min(128, T - r0)
            s = i % 2
            ld = k.dma("sp", xbuf[s][0:nr, :], xin[r0:r0 + nr, :], xsem[s], tmax(xbuf_free[s]))
            a1 = k.op("act", lambda e: e.activation(out=xs[s][0:nr, :], in_=xbuf[s][0:nr, :], func=ACT.Square,
                                                    accum_out=ss[0:nr, i:i + 1]), tmax(ld, xs_free[s], cst, ss_zero))
            v1 = k.op("dve", lambda e: e.tensor_scalar(out=rstd[0:nr, i:i + 1], in0=ss[0:nr, i:i + 1], scalar1=1.0 / D,
                                                       scalar2=EPS, op0=ALU.mult, op1=ALU.add), [a1])
            a2 = k.op("act", lambda e: e.activation(out=rstd[0:nr, i:i + 1], in_=rstd[0:nr, i:i + 1], func=ACT.Sqrt), [v1])
            v2 = k.op("dve", lambda e: e.reciprocal(out=rstd[0:nr, i:i + 1], in_=rstd[0:nr, i:i + 1]), [a2])
            a3 = k.op("act", lambda e: e.activation(out=xs[s][0:nr, :], in_=xbuf[s][0:nr, :], func=ACT.Copy,
                                                    scale=rstd[0:nr, i:i + 1]), [v2])
            xbuf_free[s] = a3
            p = None
            for g in range(4):
                bk = g % 2
                for q in range(4):
                    kc = 4 * g + q
                    p = k.op("pe", lambda e: e.transpose(out=ps[:, bk, q * 128:q * 128 + nr],
                                                         in_=xs[s][0:nr, kc * 128:(kc + 1) * 128], identity=ident[0:nr, 0:nr]),
                             tmax(a3, bank_free[bk]), inc=(q == 3))
                src = ps[:, bk, :].rearrange("p (a c) -> p a c", a=4)[:, :, 0:nr]
                bank_free[bk] = k.op("dve", lambda e: e.tensor_tensor(
                    out=xn[:, 4 * g:4 * g + 4, r0:r0 + nr], in0=src,
                    in1=nw[:, 4 * g:4 * g + 4].unsqueeze(2).to_broadcast([128, 4, nr]), op=ALU.mult), [p])
                last_xn = bank_free[bk]
            xs_free[s] = p

        ar.top = top_s2
        cvb = [ar.alloc([128, T]) for _ in range(2)]
        bsb = [ar.alloc([128, T]) for _ in range(2)]
        szb = [ar.alloc([128, T]) for _ in range(2)]
        acc = [ar.alloc([128, T]) for _ in range(2)]
        csb = [ar.alloc([128, 366]) for _ in range(2)]
        assert ar.top <= AR_WORDS - 2 * D
        k.op("dve", lambda e: e.memset(y0T[:, :, 0:2], 0.0))
        wc_free = [None, None]
        blk_free = [last_xn, last_xn]
        csb_free = [last_xn, last_xn]
        step = 0
        last_pe = None
        y0_done = None
        for eb in range(32):
            s = eb % 2
            wl = k.dma("pool", wc[s].rearrange("p a b -> p (a b)"), wci[eb], wsem[s], tmax(wc_free[s]), max_dma_last_dim=8192)
            evs = []
            for (c0, c1) in RANGES:
                n = c1 - c0
                st_ = step % 2
                step += 1
                p = None
                for q in range(4):
                    bk = st_ * 4 + q
                    for kc in range(16):
                        p = k.op("pe", lambda e: e.matmul(ps[:, bk, 0:n], lhsT=wc[s][:, kc, q * 128:(q + 1) * 128],
                                                          rhs=xn[:, kc, c0:c1], start=(kc == 0), stop=(kc == 15)),
                                 tmax(wl, last_xn, bank_free[bk]), inc=(q == 3 and kc == 15))
                last_pe = p
                b0 = st_ * 4
                a_c = k.op("act", lambda e: e.copy(out=csb[st_][:, 0:n], in_=ps[:, b0 + 1, 0:n]), tmax(p, csb_free[st_]))
                v_cv = k.op("dve", lambda e: e.tensor_tensor(out=cvb[s][:, c0:c1], in0=ps[:, b0 + 2, 0:n], in1=csb[st_][:, 0:n],
                                                            op=ALU.mult), tmax(p, a_c, blk_free[s]))
                csb_free[st_] = v_cv
                a_b = k.op("act", lambda e: e.copy(out=bsb[s][:, c0:c1], in_=ps[:, b0 + 0, 0:n]), tmax(blk_free[s]))
                a_z = k.op("act", lambda e: e.activation(out=szb[s][:, c0:c1], in_=ps[:, b0 + 3, 0:n], func=ACT.Silu))
                for q in range(4):
                    bank_free[b0 + q] = tmax(a_z, v_cv)
                evs = tmax(a_z, v_cv)
            wc_free[s] = last_pe
            k.op("dve", lambda e: e.tensor_copy(out=cvb[s][:, C_A0 - 2:C_A0], in_=cc[:, eb, 0, :]), evs)
            v = k.op("dve", lambda e: e.tensor_copy(out=cvb[s][:, C_B0 - 2:C_B0], in_=cc[:, eb, 1, :]))
            a = k.op("act", lambda e: e.activation(out=acc[s][:, 2:T], in_=cvb[s][:, 2:T], func=ACT.Identity,
                                                   scale=cw[:, eb, 2:3], bias=cw[:, eb, 3:4]), tmax(v, evs))
            v = k.op("dve", lambda e: e.scalar_tensor_tensor(out=acc[s][:, 2:T], in0=cvb[s][:, 1:T - 1], scalar=cw[:, eb, 1:2],
                                                            in1=acc[s][:, 2:T], op0=ALU.mult, op1=ALU.add), [a])
            v = k.op("dve", lambda e: e.scalar_tensor_tensor(out=acc[s][:, 2:T], in0=cvb[s][:, 0:T - 2], scalar=cw[:, eb, 0:1],
                                                            in1=acc[s][:, 2:T], op0=ALU.mult, op1=ALU.add))
            v = k.op("dve", lambda e: e.tensor_tensor(out=acc[s][:, 2:T], in0=acc[s][:, 2:T], in1=bsb[s][:, 2:T], op=ALU.mult), evs)
            v = k.op("dve", lambda e: e.tensor_tensor(out=y0T[:, eb, 2:T], in0=acc[s][:, 2:T], in1=szb[s][:, 2:T], op=ALU.mult))
            src = cvb[s][:, T - 102:T].rearrange("p (a b) -> p a b", b=34)[:, :, 32:34]
            v = k.op("dve", lambda e: e.tensor_copy(out=ncv[:, eb, :].rearrange("p (a b) -> p a b", b=2), in_=src))
            blk_free[s] = v
            y0_done = v
        s2_pe_done = last_pe
        st_ncv = k.dsem()
        out_ncv = k.dma("sp", ncv_o, ncv.rearrange("p a b -> p (a b)"), st_ncv, [y0_done])

        ar.top = top_h1
        h1T = ar.alloc([128, 16, T])
        wo = [ar.alloc([128, 32, 128], BF16) for _ in range(2)]
        assert ar.top <= AR_WORDS - 2 * D
        guard = tmax(s2_pe_done, y0_done)
        xbuf_free = [guard, guard]
        last_fill = None
        for i in range(NTT):
            r0 = 128 * i
            nr = min(128, T - r0)
            s = i % 2
            ld = k.dma("sp", xbuf[s][0:nr, :], xin[r0:r0 + nr, :], xsem[s], tmax(xbuf_free[s]))
            p = None
            for g in range(4):
                bk = g % 2
                for q in range(4):
                    kc = 4 * g + q
                    p = k.op("pe", lambda e: e.transpose(out=ps[:, bk, q * 128:q * 128 + nr],
                                                         in_=xbuf[s][0:nr, kc * 128:(kc + 1) * 128], identity=ident[0:nr, 0:nr]),
                             tmax(ld, bank_free[bk], guard), inc=(q == 3))
                src = ps[:, bk, :].rearrange("p (a c) -> p a c", a=4)[:, :, 0:nr]
                bank_free[bk] = k.op("act", lambda e: e.copy(out=h1T[:, 4 * g:4 * g + 4, r0:r0 + nr], in_=src), tmax(p, guard))
                last_fill = bank_free[bk]
            xbuf_free[s] = p
        wo_free = [guard, guard]
        hsem = k.dsem()
        last_h = None
        for db in range(16):
            s = db % 2
            wl = k.dma("pool", wo[s].rearrange("p a b -> p (a b)"), wco[db], wsem[s], tmax(wo_free[s]), max_dma_last_dim=8192)
            p = None
            for ri, (c0, c1) in enumerate(RANGES):
                n = c1 - c0
                bk = 2 + (db % 2) * 3 + ri
                for eb in range(32):
                    p = k.op("pe", lambda e: e.matmul(ps[:, bk, 0:n], lhsT=wo[s][:, eb, :], rhs=y0T[:, eb, c0:c1],
                                                      start=(eb == 0), stop=(eb == 31)),
                             tmax(wl, y0_done, bank_free[bk]), inc=(eb == 31))
                bank_free[bk] = k.op("dve", lambda e: e.tensor_tensor(out=h1T[:, db, c0:c1], in0=ps[:, bk, 0:n], in1=h1T[:, db, c0:c1],
                                                                     op=ALU.add), tmax(p, last_fill))
                last_h = bank_free[bk]
            wo_free[s] = p
            k.dma("sp", h1T_o[:, db * T:(db + 1) * T], h1T[:, db, :], hsem, [last_h])
        h_out_tok = ("d", hsem, hsem.val)
        if stop_after <= 3:
            k.wait("sp", [h_out_tok, out_ncv])
            return nc

        ar.top = 0
        xn1 = ar.alloc([128, 16, T], BF16)
        sq = [ar.alloc([128, T]) for _ in range(2)]
        rb = ar.alloc([128, T])
        ones_d = ar.alloc([128, 128])
        assert ar.top <= top_h1
        k.op("dve", lambda e: e.memset(ones_d, 1.0 / D), [last_h])
        vm = k.op("dve", lambda e: e.memset(rb, 0.0))
        sq_free = [last_h, last_h]
        p = None
        for db in range(16):
            s = db % 2
            a = k.op("act", lambda e: e.activation(out=sq[s], in_=h1T[:, db, :], func=ACT.Square), tmax(last_h, sq_free[s]))
            for ri, (c0, c1) in enumerate(RANGES):
                n = c1 - c0
                p = k.op("pe", lambda e: e.matmul(ps[:, 2 + ri, 0:n], lhsT=ones_d, rhs=sq[s][:, c0:c1], start=(db == 0), stop=(db == 15)),
                         tmax(a, vm, bank_free[2 + ri]), inc=(ri == 2))
            sq_free[s] = p
        v = None
        for ri, (c0, c1) in enumerate(RANGES):
            n = c1 - c0
            v = k.op("dve", lambda e: e.tensor_scalar(out=rb[:, c0:c1], in0=ps[:, 2 + ri, 0:n], scalar1=EPS, scalar2=None, op0=ALU.add), [p])
            bank_free[2 + ri] = v
        a = k.op("act", lambda e: e.activation(out=rb, in_=rb, func=ACT.Sqrt), [v])
        v = k.op("dve", lambda e: e.reciprocal(out=rb, in_=rb), [a])
        for db in range(16):
            v = k.op("dve", lambda e: e.scalar_tensor_tensor(out=xn1[:, db, :], in0=h1T[:, db, :], scalar=nw[:, 16 + db:17 + db],
                                                            in1=rb, op0=ALU.mult, op1=ALU.mult))
        xn1_done = v

        ar.top = 8784
        are, aim, ldt = [ar.alloc([128, NPAIR]) for _ in range(3)]
        bre = ar.alloc([128, NPAIR * 16])
        bim = ar.alloc([128, NPAIR * 16])
        iot = ar.alloc([128, T])
        onesr = ar.alloc([128, T])
        ge_re = ar.alloc([128, NPAIR])
        ge_im = ar.alloc([128, NPAIR])
        f0 = ar.alloc([128, NPAIR, 2])
        c2 = k.dsem()
        gw = tmax(xn1_done, h_out_tok)
        k.dma("sp", are, are_d, c2, gw)
        k.dma("sp", aim, aim_d, c2)
        k.dma("sp", ldt, ldt_d, c2)
        k.dma("sp", bre, bre_d, c2)
        k.dma("sp", iot, iota_d, c2)
        c2t = k.dma("sp", bim, bim_d, c2)
        k.op("dve", lambda e: e.memset(onesr, 1.0), gw)
        cons = emit_ssm_consts(k, ar, ps, ident, are, aim, ldt, bre, bim, tmax(c2t, gw), bank_free, [SEG, 4], npow=4)
        ar.top = cons["keep_top"]
        cN, sN, rN, rot_tok = cons["rots"][SEG]
        _, _, rho4, rot4_tok = cons["rots"][4]
        kap4 = ar.alloc([128, NPAIR])
        k4_tok = k.op("dve", lambda e: e.tensor_scalar(out=kap4, in0=cons["kap"], scalar1=4.0, scalar2=None, op0=ALU.mult), tmax(cons["tok"]))

        NCH = SEG // 4
        wu = [ar.alloc([128, 16, 128], BF16) for _ in range(2)]
        uT = [ar.alloc([128, T], BF16) for _ in range(2)]
        pbs = [PairBufs(ar, NCH) for _ in range(2)]
        PC0, PC1 = C_P0, C_P0 + SEG
        pre_tok = tmax(cons["tok"], k4_tok, rot4_tok)
        wu_free = [pre_tok, pre_tok]
        uT_free = [pre_tok, pre_tok]
        step = 0
        last_copy = None
        for b in range(32):
            s = b % 2
            wl = k.dma("pool", wu[s].rearrange("p a b -> p (a b)"), wsu[b], wsem[s], tmax(wu_free[s], gw), max_dma_last_dim=8192)
            p = None
            a_u = None
            for ri, (c0, c1) in enumerate(RANGES):
                n = c1 - c0
                bk = 5 + ri
                for kc in range(16):
                    p = k.op("pe", lambda e: e.matmul(ps[:, bk, 0:n], lhsT=wu[s][:, kc, :], rhs=xn1[:, kc, c0:c1],
                                                      start=(kc == 0), stop=(kc == 15)),
                             tmax(wl, xn1_done, bank_free[bk]), inc=(kc == 15))
                a_u = k.op("act", lambda e: e.copy(out=uT[s][:, c0:c1], in_=ps[:, bk, 0:n]), tmax(p, uT_free[s]))
                bank_free[bk] = a_u
            wu_free[s] = p
            last_bu_pe = None
            for j4 in range(4):
                pair = 4 * b + j4
                pb = pbs[pair % 2]
                bkr = (step % 2) * 2
                step += 1
                pi_ = None
                for (bft, bk_) in ((cons["bft_re"], bkr), (cons["bft_im"], bkr + 1)):
                    for r in range(4):
                        rhs = uT[s][32 * j4:32 * j4 + 32, PC0 + r:PC0 + r + 4 * (NCH - 1) + 1:4]
                        pi_ = k.op("pe", lambda e: e.matmul(ps[:, bk_, 0:NCH], lhsT=bft[3 - r][32 * j4:32 * j4 + 32, b, :], rhs=rhs,
                                                            start=(r == 0), stop=(r == 3), tile_position=(32 * j4, 0)),
                                   tmax(a_u, bank_free[bk_], pre_tok), inc=(r == 3))
                last_bu_pe = pi_
                k.op("act", lambda e: e.copy(out=pb.bu_re, in_=ps[:, bkr, 0:NCH]), tmax(pi_, pb.t_prerot))
                a_e = k.op("act", lambda e: e.copy(out=pb.bu_im, in_=ps[:, bkr + 1, 0:NCH]))
                bank_free[bkr] = a_e
                bank_free[bkr + 1] = a_e
                kcol = kap4[:, pair:pair + 1]
                v = k.op("dve", lambda e: e.tensor_scalar(out=pb.ph, in0=iot[:, PC0:PC0 + NCH], scalar1=kcol, scalar2=None, op0=ALU.mult),
                         tmax(pb.t_tabact, pre_tok))
                v = k.op("dve", lambda e: e.tensor_scalar(out=pb.rn, in0=pb.ph, scalar1=MAGIC, scalar2=MAGIC, op0=ALU.add, op1=ALU.subtract))
                v = k.op("dve", lambda e: e.tensor_tensor(out=pb.fr, in0=pb.ph, in1=pb.rn, op=ALU.subtract))
                a = k.op("act", lambda e: e.activation(out=pb.sn, in_=pb.fr, func=ACT.Sin, scale=TWO_PI), tmax(v, pb.t_prerot))
                a = k.op("act", lambda e: e.activation(out=pb.ab, in_=pb.fr, func=ACT.Abs))
                a_t = k.op("act", lambda e: e.activation(out=pb.cs, in_=pb.ab, func=ACT.Sin, scale=-TWO_PI, bias=float(np.pi / 2)))
                pb.t_tabact = a_t
                a_r = k.op("act", lambda e: e.activation(out=pb.rrow, in_=onesr[:, 0:NCH], func=ACT.Copy,
                                                         scale=rho4[:, pair:pair + 1]), tmax(pb.t_scan))
                t1, t2 = pb.ph, pb.rn
                v = k.op("dve", lambda e: e.tensor_tensor(out=t1, in0=pb.cs, in1=pb.bu_re, op=ALU.mult), tmax(a_t, a_e))
                v = k.op("dve", lambda e: e.tensor_tensor(out=t2, in0=pb.sn, in1=pb.bu_im, op=ALU.mult))
                v = k.op("dve", lambda e: e.tensor_tensor(out=pb.zre, in0=t1, in1=t2, op=ALU.add))
                v = k.op("dve", lambda e: e.tensor_tensor(out=t1, in0=pb.cs, in1=pb.bu_im, op=ALU.mult))
                v = k.op("dve", lambda e: e.tensor_tensor(out=t2, in0=pb.sn, in1=pb.bu_re, op=ALU.mult))
                v = k.op("dve", lambda e: e.tensor_tensor(out=pb.zim, in0=t1, in1=t2, op=ALU.subtract))
                pb.t_prerot = v
                v = k.op("dve", lambda e: e.tensor_tensor_scan(out=pb.gre, data0=pb.rrow, data1=pb.zre, initial=0.0,
                                                               op0=ALU.mult, op1=ALU.add), tmax(a_r, pb.t_gread))
                v = k.op("dve", lambda e: e.tensor_tensor_scan(out=pb.gim, data0=pb.rrow, data1=pb.zim, initial=0.0,
                                                               op0=ALU.mult, op1=ALU.add))
                pb.t_scan = v
                k.op("act", lambda e: e.copy(out=ge_re[:, pair:pair + 1], in_=pb.gre[:, NCH - 1:NCH]), [v])
                a = k.op("act", lambda e: e.copy(out=ge_im[:, pair:pair + 1], in_=pb.gim[:, NCH - 1:NCH]))
                pb.t_gread = a
                last_copy = a
            uT_free[s] = last_bu_pe
        ta, tb_ = ar.alloc([128, NPAIR]), ar.alloc([128, NPAIR])
        v = k.op("dve", lambda e: e.tensor_tensor(out=ta, in0=cN, in1=ge_re, op=ALU.mult), tmax(last_copy, rot_tok))
        v = k.op("dve", lambda e: e.tensor_tensor(out=tb_, in0=sN, in1=ge_im, op=ALU.mult))
        v = k.op("dve", lambda e: e.tensor_tensor(out=f0[:, :, 0], in0=ta, in1=tb_, op=ALU.subtract))
        v = k.op("dve", lambda e: e.tensor_tensor(out=ta, in0=sN, in1=ge_re, op=ALU.mult))
        v = k.op("dve", lambda e: e.tensor_tensor(out=tb_, in0=cN, in1=ge_im, op=ALU.mult))
        v = k.op("dve", lambda e: e.tensor_tensor(out=f0[:, :, 1], in0=ta, in1=tb_, op=ALU.add))
        fsem = k.dsem()
        ft = k.dma("sp", f0_o, f0.rearrange("p a b -> p (a b)"), fsem, [v])
        k.wait("sp", [h_out_tok, out_ncv, ft])
    return nc


AR_B = 52700
_DBG_NBLK = 32
GV = lambda g: g[:, T - 102:T].rearrange("p (a b) -> p a b", b=34)[:, :, 33]


def build_b():
    nc = bass.Bass("TRN2", target_bir_lowering=False)
    dr = lambda name, shape, kind="ExternalInput": nc.dram_tensor(name, shape, F32, kind=kind).ap()
    h1T_i = dr("h1T_i", [128, 16 * T])
    f0p_d = dr("f0prev", [128, 7 * NPAIR * 2])
    ident_d = dr("ident", [128, 128])
    nw_d = dr("nw", [128, 48])
    are_d = dr("are", [128, NPAIR])
    aim_d = dr("aim", [128, NPAIR])
    ldt_d = dr("ldt", [128, NPAIR])
    bre_d = dr("bre", [128, NPAIR * 16])
    bim_d = dr("bim", [128, NPAIR * 16])
    iotc_d = dr("iotc", [128, 273])
    cbr_d = dr("cbr", [128, NPAIR * 32])
    cbi_d = dr("cbi", [128, NPAIR * 32])
    dv_d = dr("dvec", [128, 32])
    bg_d = dr("bgl", [128, 32])
    h0r_d = dr("h0r", [128, NPAIR * 2])
    h0i_d = dr("h0i", [128, NPAIR * 2])
    wsu = dr("wsu", [32, 128, 16 * 128])
    wsz = dr("wsz", [32, 128, 16 * 128])
    wgl = dr("wgl", [32, 128, 32 * 128])
    wso = dr("wso", [16, 128, 32 * 128])
    y_o = dr("yout", [T, D], "ExternalOutput")
    scr_bft = nc.dram_tensor("scr_bft", [8, 128, 32 * 128], BF16, kind="Internal").ap()
    scr_ck = nc.dram_tensor("scr_ck", [10, 128, NPAIR * 32], BF16, kind="Internal").ap()
    hf_o = dr("hfin", [128, NPAIR * 6], "ExternalOutput")

    with contextlib.ExitStack() as es:
        k = K(nc, es)
        arena_t = es.enter_context(nc.sbuf_tensor("arena", [128, AR_B], F32))
        ar = Arena(arena_t[:], AR_B)
        ps = es.enter_context(nc.psum_tensor("ps", [128, 8, 512], F32))
        small_t = es.enter_context(nc.sbuf_tensor("small", [128, 256], F32))
        sm = Arena(small_t[:], 256)
        ident = sm.alloc([128, 128])
        nw = sm.alloc([128, 48])
        dvec = sm.alloc([128, 32])
        bgl = sm.alloc([128, 32])
        csem = k.dsem()
        k.dma("sp", ident, ident_d, csem)
        k.dma("sp", nw, nw_d, csem)
        k.dma("sp", dvec, dv_d, csem)
        cst = k.dma("sp", bgl, bg_d, csem)
        wsem = [k.dsem(), k.dsem(), k.dsem(), k.dsem()]
        bank_free = [None] * 8

        xn1 = ar.alloc([128, 16, T], BF16)
        g_all = ar.alloc([128, 32, T], BF16)
        BASE = ar.top
        h1T = ar.alloc([128, 16, T])
        sq = [ar.alloc([128, T]) for _ in range(2)]
        rb = ar.alloc([128, T])
        ones_d = ar.alloc([128, 128])
        hsem = k.dsem()
        for db in range(16):
            k.dma("sp", h1T[:, db, :], h1T_i[:, db * T:(db + 1) * T], hsem)
        h_in = ("d", hsem, hsem.val)
        vm = k.op("dve", lambda e: e.memset(ones_d, 1.0 / D))
        sq_free = [None, None]
        p = None
        for db in range(16):
            s = db % 2
            a = k.op("act", lambda e: e.activation(out=sq[s], in_=h1T[:, db, :], func=ACT.Square), tmax(h_in, sq_free[s]))
            for ri, (c0, c1) in enumerate(RANGES):
                n = c1 - c0
                p = k.op("pe", lambda e: e.matmul(ps[:, 2 + ri, 0:n], lhsT=ones_d, rhs=sq[s][:, c0:c1], start=(db == 0), stop=(db == 15)),
                         tmax(a, vm), inc=(ri == 2))
            sq_free[s] = p
        v = None
        for ri, (c0, c1) in enumerate(RANGES):
            n = c1 - c0
            v = k.op("dve", lambda e: e.tensor_scalar(out=rb[:, c0:c1], in0=ps[:, 2 + ri, 0:n], scalar1=EPS, scalar2=None, op0=ALU.add), [p])
            bank_free[2 + ri] = v
        a = k.op("act", lambda e: e.activation(out=rb, in_=rb, func=ACT.Sqrt), [v])
        v = k.op("dve", lambda e: e.reciprocal(out=rb, in_=rb), [a])
        for db in range(16):
            v = k.op("dve", lambda e: e.scalar_tensor_tensor(out=xn1[:, db, :], in0=h1T[:, db, :], scalar=nw[:, 16 + db:17 + db],
                                                            in1=rb, op0=ALU.mult, op1=ALU.mult), [cst, h_in])
        xn1_done = v

        NC = 273
        SEGS = [(0, 8, C_A0), (8, 8, C_B0), (16, 257, C_P0)]
        ar.top = AR_B - (3 * NPAIR + 2 * NPAIR * 16 + 7 * NPAIR * 2)
        tr_base = ar.top
        are, aim, ldt = [ar.alloc([128, NPAIR]) for _ in range(3)]
        bre = ar.alloc([128, NPAIR * 16])
        bim = ar.alloc([128, NPAIR * 16])
        f0p = ar.alloc([128, 7, NPAIR, 2])
        ar.top = BASE
        c2 = k.dsem()
        gw = tmax(xn1_done)
        k.dma("sp", are, are_d, c2, gw)
        k.dma("sp", aim, aim_d, c2)
        k.dma("sp", ldt, ldt_d, c2)
        k.dma("sp", bre, bre_d, c2)
        k.dma("sp", bim, bim_d, c2)
        c2t = k.dma("sp", f0p.rearrange("p a b c -> p (a b c)"), f0p_d, c2)
        cons = emit_ssm_consts(k, ar, ps, ident, are, aim, ldt, bre, bim, tmax(c2t, gw, cst), bank_free, [SEG, 32, 4], npow=4,
                               dram_bft=scr_bft)
        assert ar.top <= tr_base, (ar.top, tr_base)
        ar.top = cons["keep_top"]
        cN, sN, rN, rtokN = cons["rots"][SEG]
        c32, s32, _, rtok32 = cons["rots"][32]
        _, _, rho4, rtok4 = cons["rots"][4]
        iotc = ar.alloc([128, NC])
        h0r = ar.alloc([128, NPAIR, 2])
        h0i = ar.alloc([128, NPAIR, 2])
        ge_re = ar.alloc([128, NPAIR, 3])
        ge_im = ar.alloc([128, NPAIR, 3])
        hi_re = ar.alloc([128, NPAIR, 3])
        hi_im = ar.alloc([128, NPAIR, 3])
        hfin = ar.alloc([128, NPAIR, 3, 2])
        hin_re, hin_im, anr, ani, w1_, w2_, kap4 = [ar.alloc([128, NPAIR]) for _ in range(7)]
        p3_top = ar.top
        c4s = k.dsem()
        k.dma("sp", iotc, iotc_d, c4s, tmax(cons["tok"]))
        k.dma("sp", h0r.rearrange("p a b -> p (a b)"), h0r_d, c4s)
        c4t = k.dma("sp", h0i.rearrange("p a b -> p (a b)"), h0i_d, c4s)
        v = k.op("dve", lambda e: e.tensor_scalar(out=kap4, in0=cons["kap"], scalar1=4.0, scalar2=None, op0=ALU.mult), tmax(cons["tok"]))
        v = k.op("dve", lambda e: e.tensor_tensor(out=anr, in0=rN, in1=cN, op=ALU.mult), tmax(rtokN, rtok32, rtok4, c2t, cons["tok"]))
        v = k.op("dve", lambda e: e.tensor_tensor(out=ani, in0=rN, in1=sN, op=ALU.mult))
        v = k.op("dve", lambda e: e.tensor_copy(out=hin_re, in_=f0p[:, 6, :, 0]))
        v = k.op("dve", lambda e: e.tensor_copy(out=hin_im, in_=f0p[:, 6, :, 1]))
        for j in range(5, -1, -1):
            k.op("dve", lambda e: e.tensor_tensor(out=w1_, in0=anr, in1=hin_re, op=ALU.mult))
            k.op("dve", lambda e: e.tensor_tensor(out=w2_, in0=ani, in1=hin_im, op=ALU.mult))
            k.op("dve", lambda e: e.tensor_tensor(out=w1_, in0=w1_, in1=w2_, op=ALU.subtract))
            k.op("dve", lambda e: e.tensor_tensor(out=w2_, in0=ani, in1=hin_re, op=ALU.mult))
            k.op("dve", lambda e: e.tensor_tensor(out=hin_re, in0=w1_, in1=f0p[:, j, :, 0], op=ALU.add))
            k.op("dve", lambda e: e.tensor_tensor(out=w1_, in0=anr, in1=hin_im, op=ALU.mult))
            k.op("dve", lambda e: e.tensor_tensor(out=w1_, in0=w1_, in1=w2_, op=ALU.add))
            v = k.op("dve", lambda e: e.tensor_tensor(out=hin_im, in0=w1_, in1=f0p[:, j, :, 1], op=ALU.add))
        k.op("dve", lambda e: e.tensor_copy(out=hi_re[:, :, 0:2], in_=h0r), [c4t])
        k.op("dve", lambda e: e.tensor_copy(out=hi_im[:, :, 0:2], in_=h0i))
        k.op("dve", lambda e: e.tensor_copy(out=hi_re[:, :, 2], in_=hin_re))
        carry_tok = k.op("dve", lambda e: e.tensor_copy(out=hi_im[:, :, 2], in_=hin_im))
        pws = [(None, None)]
        for kk in range(1, 5):
            pr_, pi_2 = ar.alloc([128, NPAIR]), ar.alloc([128, NPAIR])
            if kk == 1:
                k.op("dve", lambda e: e.tensor_copy(out=pr_, in_=cons["ab_re"]))
                k.op("dve", lambda e: e.tensor_copy(out=pi_2, in_=cons["ab_im"]))
            else:
                qr, qi = pws[kk - 1]
                k.op("dve", lambda e: e.tensor_tensor(out=w1_, in0=qr, in1=cons["ab_re"], op=ALU.mult))
                k.op("dve", lambda e: e.tensor_tensor(out=w2_, in0=qi, in1=cons["ab_im"], op=ALU.mult))
                k.op("dve", lambda e: e.tensor_tensor(out=pr_, in0=w1_, in1=w2_, op=ALU.subtract))
                k.op("dve", lambda e: e.tensor_tensor(out=w1_, in0=qr, in1=cons["ab_im"], op=ALU.mult))
                k.op("dve", lambda e: e.tensor_tensor(out=w2_, in0=qi, in1=cons["ab_re"], op=ALU.mult))
                k.op("dve", lambda e: e.tensor_tensor(out=pi_2, in0=w1_, in1=w2_, op=ALU.add))
            pws.append((pr_, pi_2))
        HP = NPAIR // 2
        csm_re = ar.alloc([128, HP, 32])
        csm_im = ar.alloc([128, HP, 32])
        ct1 = ar.alloc([128, HP, 32])
        ct2 = ar.alloc([128, HP, 32])
        cko = [ar.alloc([128, HP, 32], BF16) for _ in range(2)]
        assert ar.top <= tr_base, (ar.top, tr_base)
        c5 = k.dsem()
        cks = [k.dsem(), k.dsem()]
        ck_tok = [None, None]
        v = None
        for half in range(2):
            hs_ = slice(half * HP * 32, (half + 1) * HP * 32)
            k.dma("sp", csm_re.rearrange("p a b -> p (a b)"), cbr_d[:, hs_], c5, tmax(cons["tok"], v))
            c5t = k.dma("sp", csm_im.rearrange("p a b -> p (a b)"), cbi_d[:, hs_], c5)
            for kk in range(5):
                if kk == 0:
                    k.op("dve", lambda e: e.tensor_copy(out=cko[0], in_=csm_re), tmax(c5t, ck_tok[0]))
                    v = k.op("dve", lambda e: e.tensor_copy(out=cko[1], in_=csm_im), tmax(ck_tok[1]))
                else:
                    pr_, pi_2 = pws[kk]
                    prb = pr_[:, half * HP:(half + 1) * HP].unsqueeze(2).to_broadcast([128, HP, 32])
                    pib = pi_2[:, half * HP:(half + 1) * HP].unsqueeze(2).to_broadcast([128, HP, 32])
                    k.op("dve", lambda e: e.tensor_tensor(out=ct1, in0=csm_re, in1=prb, op=ALU.mult), [c5t])
                    k.op("dve", lambda e: e.tensor_tensor(out=ct2, in0=csm_im, in1=pib, op=ALU.mult))
                    k.op("dve", lambda e: e.tensor_tensor(out=cko[0], in0=ct1, in1=ct2, op=ALU.subtract), tmax(ck_tok[0]))
                    k.op("dve", lambda e: e.tensor_tensor(out=ct1, in0=csm_re, in1=pib, op=ALU.mult))
                    k.op("dve", lambda e: e.tensor_tensor(out=ct2, in0=csm_im, in1=prb, op=ALU.mult))
                    v = k.op("dve", lambda e: e.tensor_tensor(out=cko[1], in0=ct1, in1=ct2, op=ALU.add), tmax(ck_tok[1]))
                ck_tok[0] = k.dma("sp", scr_ck[2 * kk][:, hs_], cko[0].rearrange("p a b -> p (a b)"), cks[0], [v])
                ck_tok[1] = k.dma("sp", scr_ck[2 * kk + 1][:, hs_], cko[1].rearrange("p a b -> p (a b)"), cks[1])
        scr_done = tmax(ck_tok[0], ck_tok[1], cons["tok"], v)

        ar.top = p3_top
        wus = [ar.alloc([128, 16, 128], BF16) for _ in range(2)]
        uTs = [ar.alloc([128, T], BF16) for _ in range(2)]
        bfb = [ar.alloc([128, 8, 128], BF16) for _ in range(2)]
        ckb = [ar.alloc([128, 10, 128], BF16) for _ in range(2)]
        A_ = lambda: ar.alloc([128, NC])
        all_tok = tmax(scr_done, carry_tok, c4t)

        class PB:
            pass
        lsets = []
        for _ in range(4):
            q = PB()
            q.Lb_re = [ar.alloc([128, NC], BF16) for _ in range(4)]
            q.Lb_n = [ar.alloc([128, NC], BF16) for _ in range(4)]
            q.bu_re, q.bu_im = A_(), A_()
            q.He_re = ar.alloc([128, NC + 1], BF16)
            q.He_n = ar.alloc([128, NC + 1], BF16)
            q.t_yread = all_tok
            q.t_dve = all_tok
            q.t_gcopy = all_tok
            lsets.append(q)
        ssets = []
        for _ in range(2):
            q = PB()
            q.ph, q.rn, q.fr, q.sn, q.cs, q.rrow = [A_() for _ in range(6)]
            q.t_dve = all_tok
            q.t_gcopy = all_tok
            ssets.append(q)
        tmpy = ar.alloc([128, NC])
        wu_free = [all_tok, all_tok]
        uT_free = [all_tok, all_tok]
        blk_sem = [k.dsem(), k.dsem(), k.dsem(), k.dsem()]
        blk_free = [all_tok, all_tok]
        yb_free = [bank_free[0], bank_free[1], bank_free[2], bank_free[3]]
        lset = [(5, 6), (7, 4)]
        lstep = 0
        g_done = None
        BS = {}
        LST = {"lstep": 0}

        def uproj(b):
            sb_ = b % 2
            st = {}
            st["lb1"] = k.dma("sp", bfb[sb_], scr_bft[:, :, b * 128:(b + 1) * 128].rearrange("k p c -> p k c"), blk_sem[sb_], tmax(blk_free[sb_]))
            st["lb2"] = k.dma("sp", ckb[sb_], scr_ck[:, :, b * 128:(b + 1) * 128].rearrange("k p c -> p k c"), blk_sem[2 + sb_])
            wu = wus[sb_]
            uT = uTs[sb_]
            wl = k.dma("pool", wu.rearrange("p a b -> p (a b)"), wsu[b], wsem[sb_], tmax(wu_free[sb_]), max_dma_last_dim=8192)
            p = None
            a_u = None
            for ri, (c0, c1) in enumerate(RANGES):
                n = c1 - c0
                bk = 5 + ri
                for kc in range(16):
                    p = k.op("pe", lambda e: e.matmul(ps[:, bk, 0:n], lhsT=wu[:, kc, :], rhs=xn1[:, kc, c0:c1],
                                                      start=(kc == 0), stop=(kc == 15)),
                             tmax(wl, xn1_done, bank_free[bk]), inc=(kc == 15))
                a_u = k.op("act", lambda e: e.copy(out=uT[:, c0:c1], in_=ps[:, bk, 0:n]), tmax(p, uT_free[sb_]))
                bank_free[bk] = a_u
            wu_free[sb_] = p
            st["a_u"] = a_u
            st["last_y"] = None
            st["a_e"] = {}
            BS[b] = st

        LBK = [5, 6, 7, 4]
        SEGS2 = [(0, 16), (16, 257)]

        def seg_rhs(uT, rows, si, rp):
            if si == 0:
                return uT[rows, T - 68:T].rearrange("p (a b) -> p a b", b=34)[:, :, 2 + rp:2 + rp + 29:4]
            return uT[rows, C_P0 + rp:C_P0 + rp + 4 * 256 + 1:4]

        def seg_out(bank, si):
            if si == 0:
                return ps[:, bank, 0:16].rearrange("p (a b) -> p a b", a=2)
            return ps[:, bank, 16:NC]

        def lphase(b):
            sb_ = b % 2
            st = BS[b]
            uT = uTs[sb_]
            for r in range(4):
                for ri_ in (0, 1):
                    pl = None
                    for si in range(2):
                        for rp in range(r + 1):
                            for j4 in range(4):
                                rows = slice(32 * j4, 32 * j4 + 32)
                                pl = k.op("pe", lambda e: e.matmul(seg_out(LBK[j4], si), lhsT=bfb[sb_][rows, 2 * (r - rp) + ri_, :],
                                                                   rhs=seg_rhs(uT, rows, si, rp), start=(rp == 0), stop=(rp == r),
                                                                   tile_position=(32 * j4, 0)),
                                          tmax(st["a_u"], st["lb1"], bank_free[LBK[j4]]), inc=(si == 1 and rp == r and j4 == 3))
                    for j4 in range(4):
                        q = lsets[j4]
                        if ri_ == 0:
                            a = k.op("act", lambda e: e.copy(out=q.Lb_re[r], in_=ps[:, LBK[j4], 0:NC]), tmax(pl, q.t_yread))
                            if r == 3:
                                a = k.op("act", lambda e: e.copy(out=q.bu_re, in_=ps[:, LBK[j4], 0:NC]), tmax(q.t_dve))
                        else:
                            a = k.op("act", lambda e: e.activation(out=q.Lb_n[r], in_=ps[:, LBK[j4], 0:NC], func=ACT.Copy, scale=-1.0),
                                     tmax(pl, q.t_yread))
                            if r == 3:
                                a = k.op("act", lambda e: e.copy(out=q.bu_im, in_=ps[:, LBK[j4], 0:NC]), tmax(q.t_dve, q.t_gcopy))
                        bank_free[LBK[j4]] = a
                        st["a_e"][j4] = a

        def chain(b, j4):
            st = BS[b]
            pair = 4 * b + j4
            q = lsets[j4]
            s_ = ssets[j4 % 2]
            bu_re, bu_im, He_re, He_n = q.bu_re, q.bu_im, q.He_re, q.He_n
            ph, rn, fr, sn, cs, rrow = s_.ph, s_.rn, s_.fr, s_.sn, s_.cs, s_.rrow
            a_e = st["a_e"][j4]
            kcol = kap4[:, pair:pair + 1]
            v = k.op("dve", lambda e: e.tensor_scalar(out=ph, in0=iotc, scalar1=kcol, scalar2=None, op0=ALU.mult), tmax(s_.t_gcopy))
            v = k.op("dve", lambda e: e.tensor_scalar(out=rn, in0=ph, scalar1=MAGIC, scalar2=MAGIC, op0=ALU.add, op1=ALU.subtract))
            v = k.op("dve", lambda e: e.tensor_tensor(out=fr, in0=ph, in1=rn, op=ALU.subtract))
            a = k.op("act", lambda e: e.activation(out=sn, in_=fr, func=ACT.Sin, scale=TWO_PI), tmax(v, s_.t_dve))
            a = k.op("act", lambda e: e.activation(out=rn, in_=fr, func=ACT.Abs))
            a_t = k.op("act", lambda e: e.activation(out=cs, in_=rn, func=ACT.Sin, scale=-TWO_PI, bias=float(np.pi / 2)))
            a_r = k.op("act", lambda e: e.activation(out=rrow, in_=iotc, func=ACT.Identity, scale=0.0, bias=rho4[:, pair:pair + 1]))
            v = k.op("dve", lambda e: e.tensor_tensor(out=ph, in0=cs, in1=bu_re, op=ALU.mult), tmax(a_t, a_e, a_r))
            v = k.op("dve", lambda e: e.tensor_tensor(out=rn, in0=sn, in1=bu_im, op=ALU.mult))
            v = k.op("dve", lambda e: e.tensor_tensor(out=fr, in0=ph, in1=rn, op=ALU.add))
            v = k.op("dve", lambda e: e.tensor_tensor(out=ph, in0=cs, in1=bu_im, op=ALU.mult))
            v = k.op("dve", lambda e: e.tensor_tensor(out=rn, in0=sn, in1=bu_re, op=ALU.mult))
            v = k.op("dve", lambda e: e.tensor_tensor(out=bu_re, in0=ph, in1=rn, op=ALU.subtract))
            for si, (cc0, ln, tk0) in enumerate(SEGS):
                k.op("dve", lambda e: e.tensor_tensor_scan(out=bu_im[:, cc0:cc0 + ln], data0=rrow[:, cc0:cc0 + ln], data1=fr[:, cc0:cc0 + ln],
                                                           initial=hi_re[:, pair, si:si + 1], op0=ALU.mult, op1=ALU.add))
                v = k.op("dve", lambda e: e.tensor_tensor_scan(out=ph[:, cc0:cc0 + ln], data0=rrow[:, cc0:cc0 + ln], data1=bu_re[:, cc0:cc0 + ln],
                                                               initial=hi_im[:, pair, si:si + 1], op0=ALU.mult, op1=ALU.add))
            gre, gim = bu_im, ph
            k.op("act", lambda e: e.copy(out=ge_re[:, pair, 0:2], in_=gre[:, 7:16:8]), [v])
            k.op("act", lambda e: e.copy(out=ge_re[:, pair, 2:3], in_=gre[:, NC - 1:NC]))
            k.op("act", lambda e: e.copy(out=ge_im[:, pair, 0:2], in_=gim[:, 7:16:8]))
            gc = k.op("act", lambda e: e.copy(out=ge_im[:, pair, 2:3], in_=gim[:, NC - 1:NC]))
            q.t_gcopy = gc
            s_.t_gcopy = gc
            v = k.op("dve", lambda e: e.tensor_tensor(out=rn, in0=cs, in1=gre, op=ALU.mult))
            v = k.op("dve", lambda e: e.tensor_tensor(out=fr, in0=sn, in1=gim, op=ALU.mult))
            v = k.op("dve", lambda e: e.tensor_tensor(out=He_re[:, 1:NC + 1], in0=rn, in1=fr, op=ALU.subtract), tmax(q.t_yread))
            v = k.op("dve", lambda e: e.tensor_tensor(out=rn, in0=sn, in1=gre, op=ALU.mult))
            v = k.op("dve", lambda e: e.tensor_tensor(out=fr, in0=cs, in1=gim, op=ALU.mult))
            v = k.op("dve", lambda e: e.scalar_tensor_tensor(out=He_n[:, 1:NC + 1], in0=rn, scalar=-1.0, in1=fr, op0=ALU.mult, op1=ALU.subtract))
            v = k.op("dve", lambda e: e.tensor_copy(out=He_re[:, 0:17:8], in_=hi_re[:, pair, :]))
            v = k.op("dve", lambda e: e.tensor_scalar(out=He_n[:, 0:17:8], in0=hi_im[:, pair, :], scalar1=-1.0, scalar2=None, op0=ALU.mult))
            q.t_dve = v
            s_.t_dve = v
            return v

        def yphase(b, chain_toks):
            sb_ = b % 2
            st = BS[b]
            py = None
            for r in range(4):
                for step in range(4):
                    for j4 in range(4):
                        rows = slice(32 * j4, 32 * j4 + 32)
                        q = lsets[j4]
                        yo = ps[rows, r, 0:NC]
                        lhs_k = (0, 1, 2 * (r + 1), 2 * (r + 1) + 1)[step]
                        rhs = (q.Lb_re[r], q.Lb_n[r], q.He_re[:, 0:NC], q.He_n[:, 0:NC])[step]
                        py = k.op("pe", lambda e: e.matmul(yo, lhsT=ckb[sb_][:, lhs_k, rows], rhs=rhs, start=(step == 0), stop=(step == 3),
                                                           tile_position=(0, 32 * j4)),
                                  tmax(chain_toks, st["a_e"][j4], st["lb2"], yb_free[r]), inc=(r == 3 and step == 3 and j4 == 3))
            for j4 in range(4):
                lsets[j4].t_yread = py
            st["last_y"] = py

        def epilogue(b):
            sb_ = b % 2
            st = BS[b]
            uT = uTs[sb_]
            last_y = st["last_y"]
            blk_free[sb_] = last_y
            v = None
            a = None
            for r in range(4):
                for (cc0, ln, tk0) in SEGS:
                    tsl = slice(tk0 + r, tk0 + r + 4 * (ln - 1) + 1, 4)
                    v = k.op("dve", lambda e: e.scalar_tensor_tensor(out=tmpy[:, cc0:cc0 + ln], in0=uT[:, tsl], scalar=dvec[:, b:b + 1],
                                                                    in1=ps[:, r, cc0:cc0 + ln], op0=ALU.mult, op1=ALU.add), tmax(last_y, a))
                    a = k.op("act", lambda e: e.activation(out=g_all[:, b, tsl], in_=tmpy[:, cc0:cc0 + ln], func=ACT.Gelu_apprx_tanh), [v])
                yb_free[r] = v
            uT_free[sb_] = tmax(last_y, v)
            return a

        for b_ in range(_DBG_NBLK):
            uproj(b_)
            lphase(b_)
            toks = [chain(b_, j_) for j_ in range(4)]
            yphase(b_, tmax(*toks))
            g_done = epilogue(b_)
        for hc in (0, C_A0 - 2, C_B0 - 2):
            g_done = k.op("act", lambda e: e.activation(out=g_all[:, :, hc:hc + 2], in_=g_all[:, :, 2:4], func=ACT.Copy, scale=0.0), [g_done])
        v = None
        for si, (cc_, ss_, gi) in enumerate(((cN, sN, 2), (c32, s32, 0), (c32, s32, 1))):
            k.op("dve", lambda e: e.tensor_tensor(out=w1_, in0=cc_, in1=ge_re[:, :, gi], op=ALU.mult), tmax(ssets[0].t_gcopy, ssets[1].t_gcopy, g_done))
            k.op("dve", lambda e: e.tensor_tensor(out=w2_, in0=ss_, in1=ge_im[:, :, gi], op=ALU.mult))
            k.op("dve", lambda e: e.tensor_tensor(out=hfin[:, :, si, 0], in0=w1_, in1=w2_, op=ALU.subtract))
            k.op("dve", lambda e: e.tensor_tensor(out=w1_, in0=ss_, in1=ge_re[:, :, gi], op=ALU.mult))
            k.op("dve", lambda e: e.tensor_tensor(out=w2_, in0=cc_, in1=ge_im[:, :, gi], op=ALU.mult))
            v = k.op("dve", lambda e: e.tensor_tensor(out=hfin[:, :, si, 1], in0=w1_, in1=w2_, op=ALU.add))
        fsem = k.dsem()
        hf_tok = k.dma("sp", hf_o, hfin.rearrange("p a b c -> p (a b c)"), fsem, [v])

        ar.top = BASE
        y2 = ar.alloc([128, 32, T], BF16)
        wg = [ar.alloc([128, 32, 128], BF16) for _ in range(2)]
        wz = [ar.alloc([128, 16, 128], BF16) for _ in range(2)]
        sg = [ar.alloc([128, 366]) for _ in range(2)]
        zs = [ar.alloc([128, 366]) for _ in range(2)]
        guard = tmax(hf_tok, g_done, lsets[0].t_yread, v)
        wg_free = [guard, guard]
        tmp_free = [guard, guard]
        step = 0
        y2_done = None
        for ob in range(32):
            s = ob % 2
            wlg = k.dma("pool", wg[s].rearrange("p a b -> p (a b)"), wgl[ob], wsem[s], tmax(wg_free[s]), max_dma_last_dim=8192)
            wlz = k.dma("pool", wz[s].rearrange("p a b -> p (a b)"), wsz[ob], wsem[2 + s], max_dma_last_dim=8192)
            p = None
            for (c0, c1) in RANGES:
                n = c1 - c0
                st_ = step % 2
                step += 1
                bg_, bz_ = st_ * 2, st_ * 2 + 1
                for kb in range(32):
                    p = k.op("pe", lambda e: e.matmul(ps[:, bg_, 0:n], lhsT=wg[s][:, kb, :], rhs=g_all[:, kb, c0:c1],
                                                      start=(kb == 0), stop=(kb == 31)), tmax(wlg, g_done, bank_free[bg_]), inc=False)
                for kc in range(16):
                    p = k.op("pe", lambda e: e.matmul(ps[:, bz_, 0:n], lhsT=wz[s][:, kc, :], rhs=xn1[:, kc, c0:c1],
                                                      start=(kc == 0), stop=(kc == 15)), tmax(wlz, bank_free[bz_]), inc=(kc == 15))
                a1 = k.op("act", lambda e: e.activation(out=sg[st_][:, 0:n], in_=ps[:, bg_, 0:n], func=ACT.Sigmoid,
                                                        bias=bgl[:, ob:ob + 1]), tmax(p, tmp_free[st_]))
                a2 = k.op("act", lambda e: e.activation(out=zs[st_][:, 0:n], in_=ps[:, bz_, 0:n], func=ACT.Silu))
                bank_free[bg_] = a2
                bank_free[bz_] = a2
                v = k.op("dve", lambda e: e.tensor_tensor(out=sg[st_][:, 0:n], in0=sg[st_][:, 0:n], in1=g_all[:, ob, c0:c1], op=ALU.mult), [a2])
                v = k.op("dve", lambda e: e.tensor_tensor(out=y2[:, ob, c0:c1], in0=sg[st_][:, 0:n], in1=zs[st_][:, 0:n], op=ALU.mult))
                tmp_free[st_] = v
                y2_done = v
            wg_free[s] = p
        glu_pe_done = p

        ar.top = 0
        h2T = ar.alloc([128, 16, T])
        wo = [ar.alloc([128, 32, 128], BF16) for _ in range(2)]
        h1c = [ar.alloc([128, T]) for _ in range(2)]
        assert ar.top <= BASE
        ar.top = BASE + 17568
        xt = [ar.alloc([128, D]) for _ in range(2)]
        sq2 = [ar.alloc([128, T]) for _ in range(2)]
        rb2 = ar.alloc([128, T])
        ones2 = ar.alloc([128, 128])
        guard = tmax(glu_pe_done, y2_done)
        vm2 = k.op("dve", lambda e: e.memset(ones2, 1.0 / D), guard)
        wo_free = [guard, guard]
        h1c_free = [guard, guard]
        h1sem = [k.dsem(), k.dsem()]
        last_h = None
        for db in range(16):
            s = db % 2
            wl = k.dma("pool", wo[s].rearrange("p a b -> p (a b)"), wso[db], wsem[s], tmax(wo_free[s]), max_dma_last_dim=8192)
            hl = k.dma("sp", h1c[s], h1T_i[:, db * T:(db + 1) * T], h1sem[s], tmax(h1c_free[s]))
            p = None
            for ri, (c0, c1) in enumerate(RANGES):
                n = c1 - c0
                bk = 2 + (db % 2) * 3 + ri
                for eb in range(32):
                    p = k.op("pe", lambda e: e.matmul(ps[:, bk, 0:n], lhsT=wo[s][:, eb, :], rhs=y2[:, eb, c0:c1],
                                                      start=(eb == 0), stop=(eb == 31)), tmax(wl, y2_done, bank_free[bk]), inc=(eb == 31))
                bank_free[bk] = k.op("dve", lambda e: e.tensor_tensor(out=h2T[:, db, c0:c1], in0=ps[:, bk, 0:n], in1=h1c[s][:, c0:c1],
                                                                     op=ALU.add), tmax(p, hl, guard))
                last_h = bank_free[bk]
            wo_free[s] = p
            h1c_free[s] = last_h
        sq_free = [guard, guard]
        p = None
        for db in range(16):
            s = db % 2
            a = k.op("act", lambda e: e.activation(out=sq2[s], in_=h2T[:, db, :], func=ACT.Square), tmax(last_h, sq_free[s]))
            for ri, (c0, c1) in enumerate(RANGES):
                n = c1 - c0
                p = k.op("pe", lambda e: e.matmul(ps[:, 2 + ri, 0:n], lhsT=ones2, rhs=sq2[s][:, c0:c1], start=(db == 0), stop=(db == 15)),
                         tmax(a, vm2, bank_free[2 + ri], last_h), inc=(ri == 2))
            sq_free[s] = p
        v = None
        for ri, (c0, c1) in enumerate(RANGES):
            n = c1 - c0
            v = k.op("dve", lambda e: e.tensor_scalar(out=rb2[:, c0:c1], in0=ps[:, 2 + ri, 0:n], scalar1=EPS, scalar2=None, op0=ALU.add), [p])
            bank_free[2 + ri] = v
        a = k.op("act", lambda e: e.activation(out=rb2, in_=rb2, func=ACT.Sqrt), [v])
        v = k.op("dve", lambda e: e.reciprocal(out=rb2, in_=rb2), [a])
        for db in range(16):
            v = k.op("dve", lambda e: e.scalar_tensor_tensor(out=h2T[:, db, :], in0=h2T[:, db, :], scalar=nw[:, 32 + db:33 + db],
                                                            in1=rb2, op0=ALU.mult, op1=ALU.mult))
        yn_done = v
        osem = [k.dsem(), k.dsem()]
        xt_free = [None, None]
        outs = []
        for i in range(NTT):
            r0 = 128 * i
            nr = min(128, T - r0)
            s = i % 2
            p = None
            a3 = None
            for g in range(4):
                bk = g % 2
                for q in range(4):
                    kc = 4 * g + q
                    p = k.op("pe", lambda e: e.transpose(out=ps[0:nr, bk, q * 128:(q + 1) * 128], in_=h2T[:, kc, r0:r0 + nr],
                                                         identity=ident), tmax(yn_done, bank_free[bk]), inc=(q == 3))
                a3 = k.op("act", lambda e: e.copy(out=xt[s][0:nr, 512 * g:512 * (g + 1)], in_=ps[0:nr, bk, :]), tmax(p, xt_free[s]))
                bank_free[bk] = a3
            od = k.dma("sp", y_o[r0:r0 + nr, :], xt[s][0:nr, :], osem[s], [a3])
            xt_free[s] = od
            outs.append(od)
        k.wait("sp", tmax(hf_tok, outs[-1], outs[-2]))
    return nc

def host_layout_a(inp):
    f = np.float32
    xp = inp["x_prompt"][0]
    P = np.concatenate([np.zeros((16, D), f), inp["meta_tokens"].astype(f), xp], axis=0)
    Ppad = np.concatenate([np.zeros((2, D), f), P], axis=0)
    xs = inp["x_sample"]
    z2 = np.zeros((2, D), f)
    xin = []
    for c in range(NCORE):
        rows = [Ppad[SEG * c:SEG * c + SEG + 2], z2, xs[2 * c], z2, xs[2 * c + 1]]
        xin.append(np.ascontiguousarray(np.concatenate(rows, axis=0)))
    nw = np.concatenate([inp["norm_w"][0].reshape(16, 128).T, inp["norm_w"][1].reshape(16, 128).T], axis=1)
    wci = inp["conv_w_in"][0].reshape(16, 128, 4, 32, 128)
    wci = np.ascontiguousarray(wci.transpose(3, 1, 0, 2, 4)).reshape(32, 128, 16 * 512)
    cwv = np.concatenate([inp["conv_w"][0], inp["conv_b"]], axis=0)
    cw = np.ascontiguousarray(cwv.reshape(4, 32, 128).transpose(2, 1, 0)).reshape(128, 128)
    cache = inp["cache_conv"][0]
    ccs = []
    for c in range(NCORE):
        a = cache[2 * c:2 * c + 2].reshape(2, 2, 32, 128)
        ccs.append(np.ascontiguousarray(a.transpose(3, 2, 0, 1)).reshape(128, 128))
    wco = inp["conv_w_out"][0].reshape(32, 128, 16, 128)
    wco = np.ascontiguousarray(wco.transpose(2, 1, 0, 3)).reshape(16, 128, 32 * 128)
    sm = lambda a: np.ascontiguousarray(a.reshape(NPAIR, 2, 64).transpose(1, 2, 0)).reshape(128, NPAIR)
    are = sm(inp["ssm_a_re"][0])
    aim = sm(inp["ssm_a_im"][0])
    ldt = sm(np.repeat(inp["ssm_log_dt"][0][:, None], 64, axis=1))
    smb = lambda a: np.ascontiguousarray(a.reshape(NPAIR, 2, 64, 16).transpose(1, 2, 0, 3)).reshape(128, NPAIR * 16)
    bre = smb(inp["ssm_b_re"][0])
    bim = smb(inp["ssm_b_im"][0])
    wsu = inp["ssm_w_in"][0][:, :E].reshape(16, 128, 32, 128)
    wsu = np.ascontiguousarray(wsu.transpose(2, 1, 0, 3)).reshape(32, 128, 16 * 128)
    io = np.zeros((T,), f)
    io[C_P0:C_P0 + SEG] = np.arange(1, SEG + 1)
    io[C_A0:C_A0 + 32] = np.arange(1, 33)
    io[C_B0:C_B0 + 32] = np.arange(1, 33)
    iota = np.ascontiguousarray(np.broadcast_to(io[None, :], (128, T)))
    shared = dict(ident=np.eye(128, dtype=f), nw=np.ascontiguousarray(nw.astype(f)), wci=wci, cw=cw, wco=wco, are=are, aim=aim,
                  ldt=ldt, bre=bre, bim=bim, wsu=wsu, iota=iota)
    return [dict(shared, xin=xin[c], cc=ccs[c]) for c in range(NCORE)]


def host_layout_b(inp, res_a):
    f = np.float32
    base = host_layout_a.cache
    f0 = [np.asarray(r["f0"]).reshape(128, NPAIR, 2) for r in res_a]
    nw48 = np.ascontiguousarray(np.concatenate([base["nw"], inp["final_norm_w"].astype(f).reshape(16, 128).T], axis=1))

    def cbd(cm):
        c4 = cm.reshape(NPAIR, 2, 16, 64)
        out = np.zeros((2, 64, NPAIR, 2, 16), f)
        for g2 in range(2):
            out[g2, :, :, g2, :] = c4[:, g2].transpose(2, 0, 1)
        return out.reshape(128, NPAIR * 32)
    cbr = cbd(inp["ssm_c_re"][0])
    cbi = cbd(inp["ssm_c_im"][0])
    dvec = np.ascontiguousarray(inp["ssm_d"][0].reshape(32, 128).T)
    bgl = np.ascontiguousarray(inp["ssm_b_glu"][0].reshape(32, 128).T)
    wsz = inp["ssm_w_in"][0][:, E:].reshape(16, 128, 32, 128)
    wsz = np.ascontiguousarray(wsz.transpose(2, 1, 0, 3)).reshape(32, 128, 16 * 128)
    wgl = inp["ssm_w_glu"][0].reshape(32, 128, 32, 128)
    wgl = np.ascontiguousarray(wgl.transpose(2, 1, 0, 3)).reshape(32, 128, 32 * 128)
    wso = inp["ssm_w_out"][0].reshape(32, 128, 16, 128)
    wso = np.ascontiguousarray(wso.transpose(2, 1, 0, 3)).reshape(16, 128, 32 * 128)
    ioc = np.concatenate([np.arange(1, 9), np.arange(1, 9), np.arange(1, 258)]).astype(f)
    iotc = np.ascontiguousarray(np.broadcast_to(ioc[None, :], (128, 273)))
    shared = dict(ident=base["ident"], nw=nw48, iotc=iotc, are=base["are"], aim=base["aim"], ldt=base["ldt"], bre=base["bre"],
                  bim=base["bim"], cbr=cbr, cbi=cbi, dvec=dvec, bgl=bgl, wsu=base["wsu"], wsz=wsz, wgl=wgl, wso=wso)
    maps = []
    for c in range(NCORE):
        fp = np.zeros((128, 7, NPAIR, 2), f)
        for j in range(7):
            if c - 1 - j >= 0:
                fp[:, j] = f0[c - 1 - j]
        hs = []
        for st in (inp["state_ssm_re"][0], inp["state_ssm_im"][0]):
            a = st[2 * c:2 * c + 2].reshape(2, NPAIR, 2, 64).transpose(2, 3, 1, 0)
            hs.append(np.ascontiguousarray(a).reshape(128, NPAIR * 2))
        maps.append(dict(shared, h1T_i=np.asarray(res_a[c]["h1T"]), f0prev=fp.reshape(128, 7 * NPAIR * 2), h0r=hs[0], h0i=hs[1]))
    return maps


def assemble(inp, res_a, res_b):
    f = np.float32
    y_prompt = np.zeros((1, 8192, D), f)
    y_sample = np.zeros((16, 32, D), f)
    ncp = np.zeros((1, 1, 2, E), f)
    ncs = np.zeros((1, 16, 2, E), f)
    hpr = np.zeros((1, 1, 256, 64), f)
    hpi = np.zeros((1, 1, 256, 64), f)
    hsr = np.zeros((1, 16, 256, 64), f)
    hsi = np.zeros((1, 16, 256, 64), f)
    for c in range(NCORE):
        yo = np.asarray(res_b[c]["yout"])
        lo = SEG * c - 32
        c0 = C_P0 + max(0, -lo)
        y_prompt[0, max(lo, 0):lo + SEG] = yo[c0:C_P0 + SEG]
        y_sample[2 * c] = yo[C_A0:C_A0 + 32]
        y_sample[2 * c + 1] = yo[C_B0:C_B0 + 32]
        ncv = np.asarray(res_a[c]["ncv"]).reshape(128, 32, 3, 2)
        hf = np.asarray(res_b[c]["hfin"]).reshape(2, 64, NPAIR, 3, 2)
        hf = hf.transpose(3, 4, 2, 0, 1).reshape(3, 2, 256, 64)
        if c == NCORE - 1:
            ncp[0, 0] = ncv[:, :, 0, :].transpose(2, 1, 0).reshape(2, E)
            hpr[0, 0] = hf[0, 0]
            hpi[0, 0] = hf[0, 1]
        for s in range(2):
            ncs[0, 2 * c + s] = ncv[:, :, 1 + s, :].transpose(2, 1, 0).reshape(2, E)
            hsr[0, 2 * c + s] = hf[1 + s, 0]
            hsi[0, 2 * c + s] = hf[1 + s, 1]
    return (y_prompt, y_sample, ncp, ncs, hpr, hpi, hsr, hsi)


_PROGS = {}


def kernel(**inp):
    inp = {k_: np.asarray(v) for k_, v in inp.items()}
    maps_a = host_layout_a(inp)
    host_layout_a.cache = maps_a[0]
    if "a" not in _PROGS:
        _PROGS["a"] = build_a()
    res_a = run_bass_kernel_spmd(_PROGS["a"], maps_a, core_ids=list(range(NCORE))).results
    maps_b = host_layout_b(inp, res_a)
    if "b" not in _PROGS:
        _PROGS["b"] = build_b()
    res_b = run_bass_kernel_spmd(_PROGS["b"], maps_b, core_ids=list(range(NCORE))).results
    return assemble(inp, res_a, res_b)
```

```python
import contextlib
import numpy as np
import concourse.bass as bass
import concourse.mybir as mybir
from concourse.bass_utils import run_bass_kernel_spmd

F32 = mybir.dt.float32
BF16 = mybir.dt.bfloat16
ACT = mybir.ActivationFunctionType
ALU = mybir.AluOpType

D = 2048
E = 4096
NCORE = 8
SEG = 1028
T = 1098
C_P0, C_A0, C_B0 = 2, 1032, 1066
RANGES = [(0, 366), (366, 732), (732, 1098)]
NTT = 9
EPS = 1e-6
MAGIC = 12582912.0
TWO_PI = float(2.0 * np.pi)
NPAIR = 128
STRICT_SAME_ENGINE = True


class DSem:
    def __init__(self, h):
        self.h = h
        self.val = 0


class K:
    def __init__(self, nc, es):
        self.nc = nc
        self.es = es
        self.eng = {"pe": nc.tensor, "act": nc.scalar, "dve": nc.vector, "pool": nc.gpsimd, "sp": nc.sync}
        self.sem = {k: es.enter_context(nc.semaphore("s_" + k)) for k in ("pe", "act", "dve", "pool")}
        self.cnt = {k: 0 for k in self.sem}
        self.seen = {k: {} for k in self.eng}
        self.nd = 0
        self.selfseen = {k_: 0 for k_ in self.sem}

    def dsem(self):
        self.nd += 1
        return DSem(self.es.enter_context(self.nc.semaphore("d%d" % self.nd)))

    def wait(self, en, toks):
        for t in toks:
            if t is None:
                continue
            if t[0] == "c":
                _, src, val = t
                if src == en:
                    continue
                if self.seen[en].get(src, 0) >= val:
                    continue
                self.eng[en].wait_ge(self.sem[src], val)
                self.seen[en][src] = val
            else:
                _, ds, val = t
                key = id(ds)
                if self.seen[en].get(key, 0) >= val:
                    continue
                self.eng[en].wait_ge(ds.h, val)
                self.seen[en][key] = val

    def wait_self(self, en, tok):
        if tok is None:
            return
        _, src, val = tok
        assert src == en
        self.eng[en].wait_ge(self.sem[src], val)

    def op(self, en, fn, waits=(), inc=True):
        self.wait(en, waits)
        if STRICT_SAME_ENGINE and en in ("act", "dve", "pool") and self.cnt[en] > self.selfseen[en]:
            self.eng[en].wait_ge(self.sem[en], self.cnt[en])
            self.selfseen[en] = self.cnt[en]
        ins = fn(self.eng[en])
        if inc:
            self.cnt[en] += 1
            ins.then_inc(self.sem[en], 1)
            return ("c", en, self.cnt[en])
        return None

    def dma(self, en, out, in_, ds, waits=(), **kw):
        self.wait(en, waits)
        ins = self.eng[en].dma_start(out=out, in_=in_, **kw)
        ds.val += 16
        ins.then_inc(ds.h, 16)
        return ("d", ds, ds.val)


def tmax(*toks):
    out = []
    for t in toks:
        if t is None:
            continue
        if isinstance(t, list):
            out.extend(t)
        else:
            out.append(t)
    return out


class Arena:
    def __init__(self, ap, nwords):
        self.ap = ap
        self.n = nwords
        self.top = 0

    def alloc(self, shape, dt=F32):
        n = 1
        for s_ in shape[1:]:
            n *= s_
        words = (n + 1) // 2 if dt == BF16 else n
        words = (words + 1) // 2 * 2
        off = self.top
        self.top += words
        assert self.top <= self.n, ("arena overflow", self.top, self.n)
        v = self.ap[:, off:off + words]
        if dt == BF16:
            v = v.bitcast(BF16)
        v = v[:, 0:n]
        if len(shape) == 3:
            v = v.rearrange("p (a b) -> p a b", a=shape[1])
        elif len(shape) == 4:
            v = v.rearrange("p (a b c) -> p a b c", a=shape[1], b=shape[2])
        return v


def emit_ssm_consts(k, ar, ps, ident, are, aim, ldt, bre, bim, waits, bank_free, nsteps_list, npow=1, dram_bft=None):
    A = lambda: ar.alloc([128, NPAIR])
    dt_, xr, rho, kap, fr, t0, t1, sn, cs, fre, fim, den, t2 = [A() for _ in range(13)]
    a = k.op("act", lambda e: e.activation(out=dt_, in_=ldt, func=ACT.Exp), waits)
    v = k.op("dve", lambda e: e.tensor_tensor(out=xr, in0=are, in1=dt_, op=ALU.mult), tmax(a, waits))
    a = k.op("act", lambda e: e.activation(out=rho, in_=xr, func=ACT.Exp), [v])
    v = k.op("dve", lambda e: e.scalar_tensor_tensor(out=kap, in0=aim, scalar=1.0 / TWO_PI, in1=dt_, op0=ALU.mult, op1=ALU.mult))

    def cossin(mult, c_out, s_out):
        v_ = k.op("dve", lambda e: e.tensor_scalar(out=t0, in0=kap, scalar1=float(mult), scalar2=None, op0=ALU.mult))
        v_ = k.op("dve", lambda e: e.tensor_scalar(out=t1, in0=t0, scalar1=MAGIC, scalar2=MAGIC, op0=ALU.add, op1=ALU.subtract))
        v_ = k.op("dve", lambda e: e.tensor_tensor(out=fr, in0=t0, in1=t1, op=ALU.subtract))
        a_ = k.op("act", lambda e: e.activation(out=s_out, in_=fr, func=ACT.Sin, scale=TWO_PI), [v_])
        a_ = k.op("act", lambda e: e.activation(out=t1, in_=fr, func=ACT.Abs))
        k.wait_self("act", a_)
        a_ = k.op("act", lambda e: e.activation(out=c_out, in_=t1, func=ACT.Sin, scale=-TWO_PI, bias=float(np.pi / 2)))
        k.wait("dve", [a_])
        return a_

    a = cossin(1.0, cs, sn)
    abr, abi = t0, t1
    ab_re = A() if npow > 1 else abr
    ab_im = A() if npow > 1 else None
    v = k.op("dve", lambda e: e.tensor_tensor(out=ab_re, in0=rho, in1=cs, op=ALU.mult), [a])
    v = k.op("dve", lambda e: e.tensor_scalar(out=abr, in0=ab_re, scalar1=-1.0, scalar2=None, op0=ALU.add))
    v = k.op("dve", lambda e: e.tensor_tensor(out=abi, in0=rho, in1=sn, op=ALU.mult))
    if npow > 1:
        v = k.op("dve", lambda e: e.tensor_copy(out=ab_im, in_=abi))
    v = k.op("dve", lambda e: e.tensor_tensor(out=den, in0=are, in1=are, op=ALU.mult))
    v = k.op("dve", lambda e: e.tensor_tensor(out=t2, in0=aim, in1=aim, op=ALU.mult))
    v = k.op("dve", lambda e: e.tensor_tensor(out=den, in0=den, in1=t2, op=ALU.add))
    v = k.op("dve", lambda e: e.reciprocal(out=den, in_=den))
    v = k.op("dve", lambda e: e.tensor_tensor(out=fre, in0=abr, in1=are, op=ALU.mult))
    v = k.op("dve", lambda e: e.tensor_tensor(out=t2, in0=abi, in1=aim, op=ALU.mult))
    v = k.op("dve", lambda e: e.tensor_tensor(out=fre, in0=fre, in1=t2, op=ALU.add))
    v = k.op("dve", lambda e: e.tensor_tensor(out=fre, in0=fre, in1=den, op=ALU.mult))
    v = k.op("dve", lambda e: e.tensor_tensor(out=fim, in0=abi, in1=are, op=ALU.mult))
    v = k.op("dve", lambda e: e.tensor_tensor(out=t2, in0=abr, in1=aim, op=ALU.mult))
    v = k.op("dve", lambda e: e.tensor_tensor(out=fim, in0=fim, in1=t2, op=ALU.subtract))
    v = k.op("dve", lambda e: e.tensor_tensor(out=fim, in0=fim, in1=den, op=ALU.mult))
    rots = {}
    for nst in nsteps_list:
        cN, sN, rN = A(), A(), A()
        a = cossin(float(nst), cN, sN)
        a = k.op("act", lambda e: e.activation(out=rN, in_=xr, func=ACT.Exp, scale=float(nst)))
        rots[nst] = (cN, sN, rN, a)
    if dram_bft is None:
        bft_re = [ar.alloc([128, 32, 128], BF16) for _ in range(npow)]
        bft_im = [ar.alloc([128, 32, 128], BF16) for _ in range(npow)]
    else:
        bft_re = bft_im = None
    pw_re, pw_im, pa, pb_ = sn, cs, den, t2
    keep_top = ar.top
    bd_re = ar.alloc([128, NPAIR, 32], BF16)
    bd_im = ar.alloc([128, NPAIR, 32], BF16)
    u1 = ar.alloc([128, NPAIR, 16])
    u2 = ar.alloc([128, NPAIR, 16])
    ident_bf = ar.alloc([128, 128], BF16)
    k.op("dve", lambda e: e.tensor_copy(out=ident_bf, in_=ident))
    k.op("dve", lambda e: e.memset(bd_re, 0.0))
    k.op("dve", lambda e: e.memset(bd_im, 0.0))
    b3r = bre.rearrange("p (j c) -> p j c", c=16)
    b3i = bim.rearrange("p (j c) -> p j c", c=16)
    k.op("dve", lambda e: e.tensor_copy(out=pw_re, in_=fre))
    k.op("dve", lambda e: e.tensor_copy(out=pw_im, in_=fim))
    if dram_bft is not None:
        stage = [ar.alloc([128, 32, 128], BF16) for _ in range(2)]
        stage_sem = [k.dsem(), k.dsem()]
        stage_tok = [None, None]
        sink_toks = [None, None]
    else:
        sink_toks = [None, None]
    last = None
    vlast = None
    i = 0
    for pw in range(npow):
        if pw > 0:
            k.op("dve", lambda e: e.tensor_tensor(out=pa, in0=pw_re, in1=ab_re, op=ALU.mult), tmax(last))
            k.op("dve", lambda e: e.tensor_tensor(out=pb_, in0=pw_im, in1=ab_im, op=ALU.mult))
            k.op("dve", lambda e: e.tensor_tensor(out=pa, in0=pa, in1=pb_, op=ALU.subtract))
            k.op("dve", lambda e: e.tensor_tensor(out=pb_, in0=pw_re, in1=ab_im, op=ALU.mult))
            k.op("dve", lambda e: e.tensor_tensor(out=pw_im, in0=pw_im, in1=ab_re, op=ALU.mult))
            k.op("dve", lambda e: e.tensor_tensor(out=pw_im, in0=pw_im, in1=pb_, op=ALU.add))
            k.op("dve", lambda e: e.tensor_copy(out=pw_re, in_=pa))
        frb = pw_re.unsqueeze(2).to_broadcast([128, NPAIR, 16])
        fib = pw_im.unsqueeze(2).to_broadcast([128, NPAIR, 16])
        k.op("dve", lambda e: e.tensor_tensor(out=u1, in0=b3r, in1=frb, op=ALU.mult), tmax(last))
        k.op("dve", lambda e: e.tensor_tensor(out=u2, in0=b3i, in1=fib, op=ALU.mult))
        for (lo, hi, c0) in ((0, 64, 0), (64, 128, 16)):
            k.op("dve", lambda e: e.tensor_tensor(out=bd_re[lo:hi, :, c0:c0 + 16], in0=u1[lo:hi], in1=u2[lo:hi], op=ALU.subtract))
        k.op("dve", lambda e: e.tensor_tensor(out=u1, in0=b3i, in1=frb, op=ALU.mult))
        k.op("dve", lambda e: e.tensor_tensor(out=u2, in0=b3r, in1=fib, op=ALU.mult))
        for (lo, hi, c0) in ((0, 64, 0), (64, 128, 16)):
            vlast = k.op("dve", lambda e: e.tensor_tensor(out=bd_im[lo:hi, :, c0:c0 + 16], in0=u1[lo:hi], in1=u2[lo:hi], op=ALU.add))
        for ri_, src in enumerate((bd_re, bd_im)):
            if dram_bft is None:
                dst = (bft_re, bft_im)[ri_][pw]
                dst_wait = None
            else:
                dst = stage[ri_]
                dst_wait = stage_tok[ri_]
            for b in range(32):
                bk = i % 2
                pv = ps[:, bk, 0:64].bitcast(BF16)
                p = k.op("pe", lambda e: e.transpose(out=pv, in_=src[:, 4 * b:4 * b + 4, :].rearrange("p a c -> p (a c)"),
                                                     identity=ident_bf), tmax(vlast, bank_free[bk]))
                bank_free[bk] = k.op("act", lambda e: e.copy(out=dst[:, b, :], in_=pv), tmax(p, dst_wait))
                last = bank_free[bk]
                i += 1
            if dram_bft is not None:
                stage_tok[ri_] = k.dma("sp", dram_bft[2 * pw + ri_], dst.rearrange("p a b -> p (a b)"), stage_sem[ri_], [last])
                sink_toks[ri_] = stage_tok[ri_]
    return dict(rho=rho, kap=kap, xr=xr, bft_re=bft_re, bft_im=bft_im, rots=rots, tok=tmax(last, vlast, sink_toks[0], sink_toks[1]),
                keep_top=keep_top, ab_re=ab_re, ab_im=ab_im)


class PairBufs:
    def __init__(self, ar, ncols):
        A = lambda: ar.alloc([128, ncols])
        self.bu_re, self.bu_im = A(), A()
        self.ph, self.rn, self.fr = A(), A(), A()
        self.ab = self.rn
        self.sn, self.cs, self.rrow = A(), A(), A()
        self.zre, self.zim, self.gre, self.gim = A(), A(), A(), A()
        self.t_prerot = None
        self.t_tabact = None
        self.t_scan = None
        self.t_gread = None


AR_WORDS = 51500


def build_a(stop_after=99):
    nc = bass.Bass("TRN2", target_bir_lowering=False)
    dr = lambda name, shape, kind="ExternalInput": nc.dram_tensor(name, shape, F32, kind=kind).ap()
    xin = dr("xin", [T, D])
    ident_d = dr("ident", [128, 128])
    nw_d = dr("nw", [128, 32])
    wci = dr("wci", [32, 128, 16 * 512])
    cw_d = dr("cw", [128, 32 * 4])
    cc_d = dr("cc", [128, 32 * 4])
    wco = dr("wco", [16, 128, 32 * 128])
    are_d = dr("are", [128, NPAIR])
    aim_d = dr("aim", [128, NPAIR])
    ldt_d = dr("ldt", [128, NPAIR])
    bre_d = dr("bre", [128, NPAIR * 16])
    bim_d = dr("bim", [128, NPAIR * 16])
    wsu = dr("wsu", [32, 128, 16 * 128])
    iota_d = dr("iota", [128, T])
    h1T_o = dr("h1T", [128, 16 * T], "ExternalOutput")
    ncv_o = dr("ncv", [128, 32 * 6], "ExternalOutput")
    f0_o = dr("f0", [128, NPAIR * 2], "ExternalOutput")

    with contextlib.ExitStack() as es:
        k = K(nc, es)
        arena_t = es.enter_context(nc.sbuf_tensor("arena", [128, AR_WORDS], F32))
        ar = Arena(arena_t[:], AR_WORDS)
        ps = es.enter_context(nc.psum_tensor("ps", [128, 8, 512], F32))
        small_t = es.enter_context(nc.sbuf_tensor("small", [128, 800], F32))
        sm = Arena(small_t[:], 800)
        ident = sm.alloc([128, 128])
        nw = sm.alloc([128, 32])
        cw = sm.alloc([128, 32, 4])
        cc = sm.alloc([128, 32, 2, 2])
        ncv = sm.alloc([128, 32, 6])
        csem = k.dsem()
        k.dma("sp", ident, ident_d, csem)
        k.dma("sp", nw, nw_d, csem)
        k.dma("sp", cw.rearrange("p a b -> p (a b)"), cw_d, csem)
        cst = k.dma("sp", cc.rearrange("p a b c -> p (a b c)"), cc_d, csem)
        xsem = [k.dsem(), k.dsem()]
        wsem = [k.dsem(), k.dsem()]
        bank_free = [None] * 8

        ar.top = 0
        y0T = ar.alloc([128, 32, T], BF16)
        top_h1 = ar.top
        xn = ar.alloc([128, 16, T], BF16)
        wc = [ar.alloc([128, 16, 512], BF16) for _ in range(2)]
        top_s2 = ar.top
        xs = [ar.alloc([128, D]) for _ in range(2)]
        ss = ar.alloc([128, 16])
        rstd = ar.alloc([128, 16])
        ar.top = AR_WORDS - 2 * D
        xbuf = [ar.alloc([128, D]) for _ in range(2)]

        k.op("dve", lambda e: e.memset(ss, 0.0))
        ss_zero = k.op("dve", lambda e: e.memset(rstd, 0.0))
        xbuf_free = [None, None]
        xs_free = [None, None]
        last_xn = None
        for i in range(NTT):
            r0 = 128 * i
            nr = min(128, T - r0)
            s = i % 2
            ld = k.dma("sp", xbuf[s][0:nr, :], xin[r0:r0 + nr, :], xsem[s], tmax(xbuf_free[s]))
            a1 = k.op("act", lambda e: e.activation(out=xs[s][0:nr, :], in_=xbuf[s][0:nr, :], func=ACT.Square,
                                                    accum_out=ss[0:nr, i:i + 1]), tmax(ld, xs_free[s], cst, ss_zero))
            v1 = k.op("dve", lambda e: e.tensor_scalar(out=rstd[0:nr, i:i + 1], in0=ss[0:nr, i:i + 1], scalar1=1.0 / D,
                                                       scalar2=EPS, op0=ALU.mult, op1=ALU.add), [a1])
            a2 = k.op("act", lambda e: e.activation(out=rstd[0:nr, i:i + 1], in_=rstd[0:nr, i:i + 1], func=ACT.Sqrt), [v1])
            v2 = k.op("dve", lambda e: e.reciprocal(out=rstd[0:nr, i:i + 1], in_=rstd[0:nr, i:i + 1]), [a2])
            a3 = k.op("act", lambda e: e.activation(out=xs[s][0:nr, :], in_=xbuf[s][0:nr, :], func=ACT.Copy,
                                                    scale=rstd[0:nr, i:i + 1]), [v2])
            xbuf_free[s] = a3
            p = None
            for g in range(4):
                bk = g % 2
                for q in range(4):
                    kc = 4 * g + q
                    p = k.op("pe", lambda e: e.transpose(out=ps[:, bk, q * 128:q * 128 + nr],
                                                         in_=xs[s][0:nr, kc * 128:(kc + 1) * 128], identity=ident[0:nr, 0:nr]),
                             tmax(a3, bank_free[bk]), inc=(q == 3))
                src = ps[:, bk, :].rearrange("p (a c) -> p a c", a=4)[:, :, 0:nr]
                bank_free[bk] = k.op("dve", lambda e: e.tensor_tensor(
                    out=xn[:, 4 * g:4 * g + 4, r0:r0 + nr], in0=src,
                    in1=nw[:, 4 * g:4 * g + 4].unsqueeze(2).to_broadcast([128, 4, nr]), op=ALU.mult), [p])
                last_xn = bank_free[bk]
            xs_free[s] = p

        ar.top = top_s2
        cvb = [ar.alloc([128, T]) for _ in range(2)]
        bsb = [ar.alloc([128, T]) for _ in range(2)]
        szb = [ar.alloc([128, T]) for _ in range(2)]
        acc = [ar.alloc([128, T]) for _ in range(2)]
        csb = [ar.alloc([128, 366]) for _ in range(2)]
        assert ar.top <= AR_WORDS - 2 * D
        k.op("dve", lambda e: e.memset(y0T[:, :, 0:2], 0.0))
        wc_free = [None, None]
        blk_free = [last_xn, last_xn]
        csb_free = [last_xn, last_xn]
        step = 0
        last_pe = None
        y0_done = None
        for eb in range(32):
            s = eb % 2
            wl = k.dma("pool", wc[s].rearrange("p a b -> p (a b)"), wci[eb], wsem[s], tmax(wc_free[s]), max_dma_last_dim=8192)
            evs = []
            for (c0, c1) in RANGES:
                n = c1 - c0
                st_ = step % 2
                step += 1
                p = None
                for q in range(4):
                    bk = st_ * 4 + q
                    for kc in range(16):
                        p = k.op("pe", lambda e: e.matmul(ps[:, bk, 0:n], lhsT=wc[s][:, kc, q * 128:(q + 1) * 128],
                                                          rhs=xn[:, kc, c0:c1], start=(kc == 0), stop=(kc == 15)),
                                 tmax(wl, last_xn, bank_free[bk]), inc=(q == 3 and kc == 15))
                last_pe = p
                b0 = st_ * 4
                a_c = k.op("act", lambda e: e.copy(out=csb[st_][:, 0:n], in_=ps[:, b0 + 1, 0:n]), tmax(p, csb_free[st_]))
                v_cv = k.op("dve", lambda e: e.tensor_tensor(out=cvb[s][:, c0:c1], in0=ps[:, b0 + 2, 0:n], in1=csb[st_][:, 0:n],
                                                            op=ALU.mult), tmax(p, a_c, blk_free[s]))
                csb_free[st_] = v_cv
                a_b = k.op("act", lambda e: e.copy(out=bsb[s][:, c0:c1], in_=ps[:, b0 + 0, 0:n]), tmax(blk_free[s]))
                a_z = k.op("act", lambda e: e.activation(out=szb[s][:, c0:c1], in_=ps[:, b0 + 3, 0:n], func=ACT.Silu))
                for q in range(4):
                    bank_free[b0 + q] = tmax(a_z, v_cv)
                evs = tmax(a_z, v_cv)
            wc_free[s] = last_pe
            k.op("dve", lambda e: e.tensor_copy(out=cvb[s][:, C_A0 - 2:C_A0], in_=cc[:, eb, 0, :]), evs)
            v = k.op("dve", lambda e: e.tensor_copy(out=cvb[s][:, C_B0 - 2:C_B0], in_=cc[:, eb, 1, :]))
            a = k.op("act", lambda e: e.activation(out=acc[s][:, 2:T], in_=cvb[s][:, 2:T], func=ACT.Identity,
                                                   scale=cw[:, eb, 2:3], bias=cw[:, eb, 3:4]), tmax(v, evs))
            v = k.op("dve", lambda e: e.scalar_tensor_tensor(out=acc[s][:, 2:T], in0=cvb[s][:, 1:T - 1], scalar=cw[:, eb, 1:2],
                                                            in1=acc[s][:, 2:T], op0=ALU.mult, op1=ALU.add), [a])
            v = k.op("dve", lambda e: e.scalar_tensor_tensor(out=acc[s][:, 2:T], in0=cvb[s][:, 0:T - 2], scalar=cw[:, eb, 0:1],
                                                            in1=acc[s][:, 2:T], op0=ALU.mult, op1=ALU.add))
            v = k.op("dve", lambda e: e.tensor_tensor(out=acc[s][:, 2:T], in0=acc[s][:, 2:T], in1=bsb[s][:, 2:T], op=ALU.mult), evs)
            v = k.op("dve", lambda e: e.tensor_tensor(out=y0T[:, eb, 2:T], in0=acc[s][:, 2:T], in1=szb[s][:, 2:T], op=ALU.mult))
            src = cvb[s][:, T - 102:T].rearrange("p (a b) -> p a b", b=34)[:, :, 32:34]
            v = k.op("dve", lambda e: e.tensor_copy(out=ncv[:, eb, :].rearrange("p (a b) -> p a b", b=2), in_=src))
            blk_free[s] = v
            y0_done = v
        s2_pe_done = last_pe
        st_ncv = k.dsem()
        out_ncv = k.dma("sp", ncv_o, ncv.rearrange("p a b -> p (a b)"), st_ncv, [y0_done])

        ar.top = top_h1
        h1T = ar.alloc([128, 16, T])
        wo = [ar.alloc([128, 32, 128], BF16) for _ in range(2)]
        assert ar.top <= AR_WORDS - 2 * D
        guard = tmax(s2_pe_done, y0_done)
        xbuf_free = [guard, guard]
        last_fill = None
        for i in range(NTT):
            r0 = 128 * i
            nr = min(128, T - r0)
            s = i % 2
            ld = k.dma("sp", xbuf[s][0:nr, :], xin[r0:r0 + nr, :], xsem[s], tmax(xbuf_free[s]))
            p = None
            for g in range(4):
                bk = g % 2
                for q in range(4):
                    kc = 4 * g + q
                    p = k.op("pe", lambda e: e.transpose(out=ps[:, bk, q * 128:q * 128 + nr],
                                                         in_=xbuf[s][0:nr, kc * 128:(kc + 1) * 128], identity=ident[0:nr, 0:nr]),
                             tmax(ld, bank_free[bk], guard), inc=(q == 3))
                src = ps[:, bk, :].rearrange("p (a c) -> p a c", a=4)[:, :, 0:nr]
                bank_free[bk] = k.op("act", lambda e: e.copy(out=h1T[:, 4 * g:4 * g + 4, r0:r0 + nr], in_=src), tmax(p, guard))
                last_fill = bank_free[bk]
            xbuf_free[s] = p
        wo_free = [guard, guard]
        hsem = k.dsem()
        last_h = None
        for db in range(16):
            s = db % 2
            wl = k.dma("pool", wo[s].rearrange("p a b -> p (a b)"), wco[db], wsem[s], tmax(wo_free[s]), max_dma_last_dim=8192)
            p = None
            for ri, (c0, c1) in enumerate(RANGES):
                n = c1 - c0
                bk = 2 + (db % 2) * 3 + ri
                for eb in range(32):
                    p = k.op("pe", lambda e: e.matmul(ps[:, bk, 0:n], lhsT=wo[s][:, eb, :], rhs=y0T[:, eb, c0:c1],
                                                      start=(eb == 0), stop=(eb == 31)),
                             tmax(wl, y0_done, bank_free[bk]), inc=(eb == 31))
                bank_free[bk] = k.op("dve", lambda e: e.tensor_tensor(out=h1T[:, db, c0:c1], in0=ps[:, bk, 0:n], in1=h1T[:, db, c0:c1],
                                                                     op=ALU.add), tmax(p, last_fill))
                last_h = bank_free[bk]
            wo_free[s] = p
            k.dma("sp", h1T_o[:, db * T:(db + 1) * T], h1T[:, db, :], hsem, [last_h])
        h_out_tok = ("d", hsem, hsem.val)
        if stop_after <= 3:
            k.wait("sp", [h_out_tok, out_ncv])
            return nc

        ar.top = 0
        xn1 = ar.alloc([128, 16, T], BF16)
        sq = [ar.alloc([128, T]) for _ in range(2)]
        rb = ar.alloc([128, T])
        ones_d = ar.alloc([128, 128])
        assert ar.top <= top_h1
        k.op("dve", lambda e: e.memset(ones_d, 1.0 / D), [last_h])
        vm = k.op("dve", lambda e: e.memset(rb, 0.0))
        sq_free = [last_h, last_h]
        p = None
        for db in range(16):
            s = db % 2
            a = k.op("act", lambda e: e.activation(out=sq[s], in_=h1T[:, db, :], func=ACT.Square), tmax(last_h, sq_free[s]))
            for ri, (c0, c1) in enumerate(RANGES):
                n = c1 - c0
                p = k.op("pe", lambda e: e.matmul(ps[:, 2 + ri, 0:n], lhsT=ones_d, rhs=sq[s][:, c0:c1], start=(db == 0), stop=(db == 15)),
                         tmax(a, vm, bank_free[2 + ri]), inc=(ri == 2))
            sq_free[s] = p
        v = None
        for ri, (c0, c1) in enumerate(RANGES):
            n = c1 - c0
            v = k.op("dve", lambda e: e.tensor_scalar(out=rb[:, c0:c1], in0=ps[:, 2 + ri, 0:n], scalar1=EPS, scalar2=None, op0=ALU.add), [p])
            bank_free[2 + ri] = v
        a = k.op("act", lambda e: e.activation(out=rb, in_=rb, func=ACT.Sqrt), [v])
        v = k.op("dve", lambda e: e.reciprocal(out=rb, in_=rb), [a])
        for db in range(16):
            v = k.op("dve", lambda e: e.scalar_tensor_tensor(out=xn1[:, db, :], in0=h1T[:, db, :], scalar=nw[:, 16 + db:17 + db],
                                                            in1=rb, op0=ALU.mult, op1=ALU.mult))
        xn1_done = v

        ar.top = 8784
        are, aim, ldt = [ar.alloc([128, NPAIR]) for _ in range(3)]
        bre = ar.alloc([128, NPAIR * 16])
        bim = ar.alloc([128, NPAIR * 16])
        iot = ar.alloc([128, T])
        onesr = ar.alloc([128, T])
        ge_re = ar.alloc([128, NPAIR])
        ge_im = ar.alloc([128, NPAIR])
        f0 = ar.alloc([128, NPAIR, 2])
        c2 = k.dsem()
        gw = tmax(xn1_done, h_out_tok)
        k.dma("sp", are, are_d, c2, gw)
        k.dma("sp", aim, aim_d, c2)
        k.dma("sp", ldt, ldt_d, c2)
        k.dma("sp", bre, bre_d, c2)
        k.dma("sp", iot, iota_d, c2)
        c2t = k.dma("sp", bim, bim_d, c2)
        k.op("dve", lambda e: e.memset(onesr, 1.0), gw)
        cons = emit_ssm_consts(k, ar, ps, ident, are, aim, ldt, bre, bim, tmax(c2t, gw), bank_free, [SEG, 4], npow=4)
        ar.top = cons["keep_top"]
        cN, sN, rN, rot_tok = cons["rots"][SEG]
        _, _, rho4, rot4_tok = cons["rots"][4]
        kap4 = ar.alloc([128, NPAIR])
        k4_tok = k.op("dve", lambda e: e.tensor_scalar(out=kap4, in0=cons["kap"], scalar1=4.0, scalar2=None, op0=ALU.mult), tmax(cons["tok"]))

        NCH = SEG // 4
        wu = [ar.alloc([128, 16, 128], BF16) for _ in range(2)]
        uT = [ar.alloc([128, T], BF16) for _ in range(2)]
        pbs = [PairBufs(ar, NCH) for _ in range(2)]
        PC0, PC1 = C_P0, C_P0 + SEG
        pre_tok = tmax(cons["tok"], k4_tok, rot4_tok)
        wu_free = [pre_tok, pre_tok]
        uT_free = [pre_tok, pre_tok]
        step = 0
        last_copy = None
        for b in range(32):
            s = b % 2
            wl = k.dma("pool", wu[s].rearrange("p a b -> p (a b)"), wsu[b], wsem[s], tmax(wu_free[s], gw), max_dma_last_dim=8192)
            p = None
            a_u = None
            for ri, (c0, c1) in enumerate(RANGES):
                n = c1 - c0
                bk = 5 + ri
                for kc in range(16):
                    p = k.op("pe", lambda e: e.matmul(ps[:, bk, 0:n], lhsT=wu[s][:, kc, :], rhs=xn1[:, kc, c0:c1],
                                                      start=(kc == 0), stop=(kc == 15)),
                             tmax(wl, xn1_done, bank_free[bk]), inc=(kc == 15))
                a_u = k.op("act", lambda e: e.copy(out=uT[s][:, c0:c1], in_=ps[:, bk, 0:n]), tmax(p, uT_free[s]))
                bank_free[bk] = a_u
            wu_free[s] = p
            last_bu_pe = None
            for j4 in range(4):
                pair = 4 * b + j4
                pb = pbs[pair % 2]
                bkr = (step % 2) * 2
                step += 1
                pi_ = None
                for (bft, bk_) in ((cons["bft_re"], bkr), (cons["bft_im"], bkr + 1)):
                    for r in range(4):
                        rhs = uT[s][32 * j4:32 * j4 + 32, PC0 + r:PC0 + r + 4 * (NCH - 1) + 1:4]
                        pi_ = k.op("pe", lambda e: e.matmul(ps[:, bk_, 0:NCH], lhsT=bft[3 - r][32 * j4:32 * j4 + 32, b, :], rhs=rhs,
                                                            start=(r == 0), stop=(r == 3), tile_position=(32 * j4, 0)),
                                   tmax(a_u, bank_free[bk_], pre_tok), inc=(r == 3))
                last_bu_pe = pi_
                k.op("act", lambda e: e.copy(out=pb.bu_re, in_=ps[:, bkr, 0:NCH]), tmax(pi_, pb.t_prerot))
                a_e = k.op("act", lambda e: e.copy(out=pb.bu_im, in_=ps[:, bkr + 1, 0:NCH]))
                bank_free[bkr] = a_e
                bank_free[bkr + 1] = a_e
                kcol = kap4[:, pair:pair + 1]
                v = k.op("dve", lambda e: e.tensor_scalar(out=pb.ph, in0=iot[:, PC0:PC0 + NCH], scalar1=kcol, scalar2=None, op0=ALU.mult),
                         tmax(pb.t_tabact, pre_tok))
                v = k.op("dve", lambda e: e.tensor_scalar(out=pb.rn, in0=pb.ph, scalar1=MAGIC, scalar2=MAGIC, op0=ALU.add, op1=ALU.subtract))
                v = k.op("dve", lambda e: e.tensor_tensor(out=pb.fr, in0=pb.ph, in1=pb.rn, op=ALU.subtract))
                a = k.op("act", lambda e: e.activation(out=pb.sn, in_=pb.fr, func=ACT.Sin, scale=TWO_PI), tmax(v, pb.t_prerot))
                a = k.op("act", lambda e: e.activation(out=pb.ab, in_=pb.fr, func=ACT.Abs))
                a_t = k.op("act", lambda e: e.activation(out=pb.cs, in_=pb.ab, func=ACT.Sin, scale=-TWO_PI, bias=float(np.pi / 2)))
                pb.t_tabact = a_t
                a_r = k.op("act", lambda e: e.activation(out=pb.rrow, in_=onesr[:, 0:NCH], func=ACT.Copy,
                                                         scale=rho4[:, pair:pair + 1]), tmax(pb.t_scan))
                t1, t2 = pb.ph, pb.rn
                v = k.op("dve", lambda e: e.tensor_tensor(out=t1, in0=pb.cs, in1=pb.bu_re, op=ALU.mult), tmax(a_t, a_e))
                v = k.op("dve", lambda e: e.tensor_tensor(out=t2, in0=pb.sn, in1=pb.bu_im, op=ALU.mult))
                v = k.op("dve", lambda e: e.tensor_tensor(out=pb.zre, in0=t1, in1=t2, op=ALU.add))
                v = k.op("dve", lambda e: e.tensor_tensor(out=t1, in0=pb.cs, in1=pb.bu_im, op=ALU.mult))
                v = k.op("dve", lambda e: e.tensor_tensor(out=t2, in0=pb.sn, in1=pb.bu_re, op=ALU.mult))
                v = k.op("dve", lambda e: e.tensor_tensor(out=pb.zim, in0=t1, in1=t2, op=ALU.subtract))
                pb.t_prerot = v
                v = k.op("dve", lambda e: e.tensor_tensor_scan(out=pb.gre, data0=pb.rrow, data1=pb.zre, initial=0.0,
                                                               op0=ALU.mult, op1=ALU.add), tmax(a_r, pb.t_gread))
                v = k.op("dve", lambda e: e.tensor_tensor_scan(out=pb.gim, data0=pb.rrow, data1=pb.zim, initial=0.0,
                                                               op0=ALU.mult, op1=ALU.add))
                pb.t_scan = v
                k.op("act", lambda e: e.copy(out=ge_re[:, pair:pair + 1], in_=pb.gre[:, NCH - 1:NCH]), [v])
                a = k.op("act", lambda e: e.copy(out=ge_im[:, pair:pair + 1], in_=pb.gim[:, NCH - 1:NCH]))
                pb.t_gread = a
                last_copy = a
            uT_free[s] = last_bu_pe
        ta, tb_ = ar.alloc([128, NPAIR]), ar.alloc([128, NPAIR])
        v = k.op("dve", lambda e: e.tensor_tensor(out=ta, in0=cN, in1=ge_re, op=ALU.mult), tmax(last_copy, rot_tok))
        v = k.op("dve", lambda e: e.tensor_tensor(out=tb_, in0=sN, in1=ge_im, op=ALU.mult))
        v = k.op("dve", lambda e: e.tensor_tensor(out=f0[:, :, 0], in0=ta, in1=tb_, op=ALU.subtract))
        v = k.op("dve", lambda e: e.tensor_tensor(out=ta, in0=sN, in1=ge_re, op=ALU.mult))
        v = k.op("dve", lambda e: e.tensor_tensor(out=tb_, in0=cN, in1=ge_im, op=ALU.mult))
        v = k.op("dve", lambda e: e.tensor_tensor(out=f0[:, :, 1], in0=ta, in1=tb_, op=ALU.add))
        fsem = k.dsem()
        ft = k.dma("sp", f0_o, f0.rearrange("p a b -> p (a b)"), fsem, [v])
        k.wait("sp", [h_out_tok, out_ncv, ft])
    return nc


AR_B = 52700
_DBG_NBLK = 32
GV = lambda g: g[:, T - 102:T].rearrange("p (a b) -> p a b", b=34)[:, :, 33]


def build_b():
    nc = bass.Bass("TRN2", target_bir_lowering=False)
    dr = lambda name, shape, kind="ExternalInput": nc.dram_tensor(name, shape, F32, kind=kind).ap()
    h1T_i = dr("h1T_i", [128, 16 * T])
    f0p_d = dr("f0prev", [128, 7 * NPAIR * 2])
    ident_d = dr("ident", [128, 128])
    nw_d = dr("nw", [128, 48])
    are_d = dr("are", [128, NPAIR])
    aim_d = dr("aim", [128, NPAIR])
    ldt_d = dr("ldt", [128, NPAIR])
    bre_d = dr("bre", [128, NPAIR * 16])
    bim_d = dr("bim", [128, NPAIR * 16])
    iotc_d = dr("iotc", [128, 273])
    cbr_d = dr("cbr", [128, NPAIR * 32])
    cbi_d = dr("cbi", [128, NPAIR * 32])
    dv_d = dr("dvec", [128, 32])
    bg_d = dr("bgl", [128, 32])
    h0r_d = dr("h0r", [128, NPAIR * 2])
    h0i_d = dr("h0i", [128, NPAIR * 2])
    wsu = dr("wsu", [32, 128, 16 * 128])
    wsz = dr("wsz", [32, 128, 16 * 128])
    wgl = dr("wgl", [32, 128, 32 * 128])
    wso = dr("wso", [16, 128, 32 * 128])
    y_o = dr("yout", [T, D], "ExternalOutput")
    scr_bft = nc.dram_tensor("scr_bft", [8, 128, 32 * 128], BF16, kind="Internal").ap()
    scr_ck = nc.dram_tensor("scr_ck", [10, 128, NPAIR * 32], BF16, kind="Internal").ap()
    hf_o = dr("hfin", [128, NPAIR * 6], "ExternalOutput")

    with contextlib.ExitStack() as es:
        k = K(nc, es)
        arena_t = es.enter_context(nc.sbuf_tensor("arena", [128, AR_B], F32))
        ar = Arena(arena_t[:], AR_B)
        ps = es.enter_context(nc.psum_tensor("ps", [128, 8, 512], F32))
        small_t = es.enter_context(nc.sbuf_tensor("small", [128, 256], F32))
        sm = Arena(small_t[:], 256)
        ident = sm.alloc([128, 128])
        nw = sm.alloc([128, 48])
        dvec = sm.alloc([128, 32])
        bgl = sm.alloc([128, 32])
        csem = k.dsem()
        k.dma("sp", ident, ident_d, csem)
        k.dma("sp", nw, nw_d, csem)
        k.dma("sp", dvec, dv_d, csem)
        cst = k.dma("sp", bgl, bg_d, csem)
        wsem = [k.dsem(), k.dsem(), k.dsem(), k.dsem()]
        bank_free = [None] * 8

        xn1 = ar.alloc([128, 16, T], BF16)
        g_all = ar.alloc([128, 32, T], BF16)
        BASE = ar.top
        h1T = ar.alloc([128, 16, T])
        sq = [ar.alloc([128, T]) for _ in range(2)]
        rb = ar.alloc([128, T])
        ones_d = ar.alloc([128, 128])
        hsem = k.dsem()
        for db in range(16):
            k.dma("sp", h1T[:, db, :], h1T_i[:, db * T:(db + 1) * T], hsem)
        h_in = ("d", hsem, hsem.val)
        vm = k.op("dve", lambda e: e.memset(ones_d, 1.0 / D))
        sq_free = [None, None]
        p = None
        for db in range(16):
            s = db % 2
            a = k.op("act", lambda e: e.activation(out=sq[s], in_=h1T[:, db, :], func=ACT.Square), tmax(h_in, sq_free[s]))
            for ri, (c0, c1) in enumerate(RANGES):
                n = c1 - c0
                p = k.op("pe", lambda e: e.matmul(ps[:, 2 + ri, 0:n], lhsT=ones_d, rhs=sq[s][:, c0:c1], start=(db == 0), stop=(db == 15)),
                         tmax(a, vm), inc=(ri == 2))
            sq_free[s] = p
        v = None
        for ri, (c0, c1) in enumerate(RANGES):
            n = c1 - c0
            v = k.op("dve", lambda e: e.tensor_scalar(out=rb[:, c0:c1], in0=ps[:, 2 + ri, 0:n], scalar1=EPS, scalar2=None, op0=ALU.add), [p])
            bank_free[2 + ri] = v
        a = k.op("act", lambda e: e.activation(out=rb, in_=rb, func=ACT.Sqrt), [v])
        v = k.op("dve", lambda e: e.reciprocal(out=rb, in_=rb), [a])
        for db in range(16):
            v = k.op("dve", lambda e: e.scalar_tensor_tensor(out=xn1[:, db, :], in0=h1T[:, db, :], scalar=nw[:, 16 + db:17 + db],
                                                            in1=rb, op0=ALU.mult, op1=ALU.mult), [cst, h_in])
        xn1_done = v

        NC = 273
        SEGS = [(0, 8, C_A0), (8, 8, C_B0), (16, 257, C_P0)]
        ar.top = AR_B - (3 * NPAIR + 2 * NPAIR * 16 + 7 * NPAIR * 2)
        tr_base = ar.top
        are, aim, ldt = [ar.alloc([128, NPAIR]) for _ in range(3)]
        bre = ar.alloc([128, NPAIR * 16])
        bim = ar.alloc([128, NPAIR * 16])
        f0p = ar.alloc([128, 7, NPAIR, 2])
        ar.top = BASE
        c2 = k.dsem()
        gw = tmax(xn1_done)
        k.dma("sp", are, are_d, c2, gw)
        k.dma("sp", aim, aim_d, c2)
        k.dma("sp", ldt, ldt_d, c2)
        k.dma("sp", bre, bre_d, c2)
        k.dma("sp", bim, bim_d, c2)
        c2t = k.dma("sp", f0p.rearrange("p a b c -> p (a b c)"), f0p_d, c2)
        cons = emit_ssm_consts(k, ar, ps, ident, are, aim, ldt, bre, bim, tmax(c2t, gw, cst), bank_free, [SEG, 32, 4], npow=4,
                               dram_bft=scr_bft)
        assert ar.top <= tr_base, (ar.top, tr_base)
        ar.top = cons["keep_top"]
        cN, sN, rN, rtokN = cons["rots"][SEG]
        c32, s32, _, rtok32 = cons["rots"][32]
        _, _, rho4, rtok4 = cons["rots"][4]
        iotc = ar.alloc([128, NC])
        h0r = ar.alloc([128, NPAIR, 2])
        h0i = ar.alloc([128, NPAIR, 2])
        ge_re = ar.alloc([128, NPAIR, 3])
        ge_im = ar.alloc([128, NPAIR, 3])
        hi_re = ar.alloc([128, NPAIR, 3])
        hi_im = ar.alloc([128, NPAIR, 3])
        hfin = ar.alloc([128, NPAIR, 3, 2])
        hin_re, hin_im, anr, ani, w1_, w2_, kap4 = [ar.alloc([128, NPAIR]) for _ in range(7)]
        p3_top = ar.top
        c4s = k.dsem()
        k.dma("sp", iotc, iotc_d, c4s, tmax(cons["tok"]))
        k.dma("sp", h0r.rearrange("p a b -> p (a b)"), h0r_d, c4s)
        c4t = k.dma("sp", h0i.rearrange("p a b -> p (a b)"), h0i_d, c4s)
        v = k.op("dve", lambda e: e.tensor_scalar(out=kap4, in0=cons["kap"], scalar1=4.0, scalar2=None, op0=ALU.mult), tmax(cons["tok"]))
        v = k.op("dve", lambda e: e.tensor_tensor(out=anr, in0=rN, in1=cN, op=ALU.mult), tmax(rtokN, rtok32, rtok4, c2t, cons["tok"]))
        v = k.op("dve", lambda e: e.tensor_tensor(out=ani, in0=rN, in1=sN, op=ALU.mult))
        v = k.op("dve", lambda e: e.tensor_copy(out=hin_re, in_=f0p[:, 6, :, 0]))
        v = k.op("dve", lambda e: e.tensor_copy(out=hin_im, in_=f0p[:, 6, :, 1]))
        for j in range(5, -1, -1):
            k.op("dve", lambda e: e.tensor_tensor(out=w1_, in0=anr, in1=hin_re, op=ALU.mult))
            k.op("dve", lambda e: e.tensor_tensor(out=w2_, in0=ani, in1=hin_im, op=ALU.mult))
            k.op("dve", lambda e: e.tensor_tensor(out=w1_, in0=w1_, in1=w2_, op=ALU.subtract))
            k.op("dve", lambda e: e.tensor_tensor(out=w2_, in0=ani, in1=hin_re, op=ALU.mult))
            k.op("dve", lambda e: e.tensor_tensor(out=hin_re, in0=w1_, in1=f0p[:, j, :, 0], op=ALU.add))
            k.op("dve", lambda e: e.tensor_tensor(out=w1_, in0=anr, in1=hin_im, op=ALU.mult))
            k.op("dve", lambda e: e.tensor_tensor(out=w1_, in0=w1_, in1=w2_, op=ALU.add))
            v = k.op("dve", lambda e: e.tensor_tensor(out=hin_im, in0=w1_, in1=f0p[:, j, :, 1], op=ALU.add))
        k.op("dve", lambda e: e.tensor_copy(out=hi_re[:, :, 0:2], in_=h0r), [c4t])
        k.op("dve", lambda e: e.tensor_copy(out=hi_im[:, :, 0:2], in_=h0i))
        k.op("dve", lambda e: e.tensor_copy(out=hi_re[:, :, 2], in_=hin_re))
        carry_tok = k.op("dve", lambda e: e.tensor_copy(out=hi_im[:, :, 2], in_=hin_im))
        pws = [(None, None)]
        for kk in range(1, 5):
            pr_, pi_2 = ar.alloc([128, NPAIR]), ar.alloc([128, NPAIR])
            if kk == 1:
                k.op("dve", lambda e: e.tensor_copy(out=pr_, in_=cons["ab_re"]))
                k.op("dve", lambda e: e.tensor_copy(out=pi_2, in_=cons["ab_im"]))
            else:
                qr, qi = pws[kk - 1]
                k.op("dve", lambda e: e.tensor_tensor(out=w1_, in0=qr, in1=cons["ab_re"], op=ALU.mult))
                k.op("dve", lambda e: e.tensor_tensor(out=w2_, in0=qi, in1=cons["ab_im"], op=ALU.mult))
                k.op("dve", lambda e: e.tensor_tensor(out=pr_, in0=w1_, in1=w2_, op=ALU.subtract))
                k.op("dve", lambda e: e.tensor_tensor(out=w1_, in0=qr, in1=cons["ab_im"], op=ALU.mult))
                k.op("dve", lambda e: e.tensor_tensor(out=w2_, in0=qi, in1=cons["ab_re"], op=ALU.mult))
                k.op("dve", lambda e: e.tensor_tensor(out=pi_2, in0=w1_, in1=w2_, op=ALU.add))
            pws.append((pr_, pi_2))
        HP = NPAIR // 2
        csm_re = ar.alloc([128, HP, 32])
        csm_im = ar.alloc([128, HP, 32])
        ct1 = ar.alloc([128, HP, 32])
        ct2 = ar.alloc([128, HP, 32])
        cko = [ar.alloc([128, HP, 32], BF16) for _ in range(2)]
        assert ar.top <= tr_base, (ar.top, tr_base)
        c5 = k.dsem()
        cks = [k.dsem(), k.dsem()]
        ck_tok = [None, None]
        v = None
        for half in range(2):
            hs_ = slice(half * HP * 32, (half + 1) * HP * 32)
            k.dma("sp", csm_re.rearrange("p a b -> p (a b)"), cbr_d[:, hs_], c5, tmax(cons["tok"], v))
            c5t = k.dma("sp", csm_im.rearrange("p a b -> p (a b)"), cbi_d[:, hs_], c5)
            for kk in range(5):
                if kk == 0:
                    k.op("dve", lambda e: e.tensor_copy(out=cko[0], in_=csm_re), tmax(c5t, ck_tok[0]))
                    v = k.op("dve", lambda e: e.tensor_copy(out=cko[1], in_=csm_im), tmax(ck_tok[1]))
                else:
                    pr_, pi_2 = pws[kk]
                    prb = pr_[:, half * HP:(half + 1) * HP].unsqueeze(2).to_broadcast([128, HP, 32])
                    pib = pi_2[:, half * HP:(half + 1) * HP].unsqueeze(2).to_broadcast([128, HP, 32])
                    k.op("dve", lambda e: e.tensor_tensor(out=ct1, in0=csm_re, in1=prb, op=ALU.mult), [c5t])
                    k.op("dve", lambda e: e.tensor_tensor(out=ct2, in0=csm_im, in1=pib, op=ALU.mult))
                    k.op("dve", lambda e: e.tensor_tensor(out=cko[0], in0=ct1, in1=ct2, op=ALU.subtract), tmax(ck_tok[0]))
                    k.op("dve", lambda e: e.tensor_tensor(out=ct1, in0=csm_re, in1=pib, op=ALU.mult))
                    k.op("dve", lambda e: e.tensor_tensor(out=ct2, in0=csm_im, in1=prb, op=ALU.mult))
                    v = k.op("dve", lambda e: e.tensor_tensor(out=cko[1], in0=ct1, in1=ct2, op=ALU.add), tmax(ck_tok[1]))
                ck_tok[0] = k.dma("sp", scr_ck[2 * kk][:, hs_], cko[0].rearrange("p a b -> p (a b)"), cks[0], [v])
                ck_tok[1] = k.dma("sp", scr_ck[2 * kk + 1][:, hs_], cko[1].rearrange("p a b -> p (a b)"), cks[1])
        scr_done = tmax(ck_tok[0], ck_tok[1], cons["tok"], v)

        ar.top = p3_top
        wus = [ar.alloc([128, 16, 128], BF16) for _ in range(2)]
        uTs = [ar.alloc([128, T], BF16) for _ in range(2)]
        bfb = [ar.alloc([128, 8, 128], BF16) for _ in range(2)]
        ckb = [ar.alloc([128, 10, 128], BF16) for _ in range(2)]
        A_ = lambda: ar.alloc([128, NC])
        all_tok = tmax(scr_done, carry_tok, c4t)

        class PB:
            pass
        lsets = []
        for _ in range(4):
            q = PB()
            q.Lb_re = [ar.alloc([128, NC], BF16) for _ in range(4)]
            q.Lb_n = [ar.alloc([128, NC], BF16) for _ in range(4)]
            q.bu_re, q.bu_im = A_(), A_()
            q.He_re = ar.alloc([128, NC + 1], BF16)
            q.He_n = ar.alloc([128, NC + 1], BF16)
            q.t_yread = all_tok
            q.t_dve = all_tok
            q.t_gcopy = all_tok
            lsets.append(q)
        ssets = []
        for _ in range(2):
            q = PB()
            q.ph, q.rn, q.fr, q.sn, q.cs, q.rrow = [A_() for _ in range(6)]
            q.t_dve = all_tok
            q.t_gcopy = all_tok
            ssets.append(q)
        tmpys = [ar.alloc([128, NC]) for _ in range(2)]
        wu_free = [all_tok, all_tok]
        uT_free = [all_tok, all_tok]
        blk_sem = [k.dsem(), k.dsem(), k.dsem(), k.dsem()]
        blk_free = [all_tok, all_tok]
        yb_free = [bank_free[0], bank_free[1], bank_free[2], bank_free[3]]
        lset = [(5, 6), (7, 4)]
        lstep = 0
        g_done = None
        BS = {}
        LST = {"lstep": 0}
        EPI = {"a": [None, None]}

        def uproj(b):
            sb_ = b % 2
            st = {}
            st["lb1"] = k.dma("sp", bfb[sb_], scr_bft[:, :, b * 128:(b + 1) * 128].rearrange("k p c -> p k c"), blk_sem[sb_], tmax(blk_free[sb_]))
            st["lb2"] = k.dma("sp", ckb[sb_], scr_ck[:, :, b * 128:(b + 1) * 128].rearrange("k p c -> p k c"), blk_sem[2 + sb_])
            wu = wus[sb_]
            uT = uTs[sb_]
            wl = k.dma("pool", wu.rearrange("p a b -> p (a b)"), wsu[b], wsem[sb_], tmax(wu_free[sb_]), max_dma_last_dim=8192)
            p = None
            a_u = None
            for ri, (c0, c1) in enumerate(RANGES):
                n = c1 - c0
                bk = 5 + ri
                for kc in range(16):
                    p = k.op("pe", lambda e: e.matmul(ps[:, bk, 0:n], lhsT=wu[:, kc, :], rhs=xn1[:, kc, c0:c1],
                                                      start=(kc == 0), stop=(kc == 15)),
                             tmax(wl, xn1_done, bank_free[bk]), inc=(kc == 15))
                a_u = k.op("act", lambda e: e.copy(out=uT[:, c0:c1], in_=ps[:, bk, 0:n]), tmax(p, uT_free[sb_]))
                bank_free[bk] = a_u
            wu_free[sb_] = p
            st["a_u"] = a_u
            st["last_y"] = None
            st["a_e"] = {}
            st["tab"] = {}
            BS[b] = st

        LBK = [5, 6, 7, 4]
        SEGS2 = [(0, 16), (16, 257)]

        def seg_rhs(uT, rows, si, rp):
            if si == 0:
                return uT[rows, T - 68:T].rearrange("p (a b) -> p a b", b=34)[:, :, 2 + rp:2 + rp + 29:4]
            return uT[rows, C_P0 + rp:C_P0 + rp + 4 * 256 + 1:4]

        def seg_out(bank, si):
            if si == 0:
                return ps[:, bank, 0:16].rearrange("p (a b) -> p a b", a=2)
            return ps[:, bank, 16:NC]

        def lphase(b):
            sb_ = b % 2
            st = BS[b]
            uT = uTs[sb_]
            for r in range(4):
                for ri_ in (0, 1):
                    pl = None
                    for si in range(2):
                        for rp in range(r + 1):
                            for j4 in range(4):
                                rows = slice(32 * j4, 32 * j4 + 32)
                                pl = k.op("pe", lambda e: e.matmul(seg_out(LBK[j4], si), lhsT=bfb[sb_][rows, 2 * (r - rp) + ri_, :],
                                                                   rhs=seg_rhs(uT, rows, si, rp), start=(rp == 0), stop=(rp == r),
                                                                   tile_position=(32 * j4, 0)),
                                          tmax(st["a_u"], st["lb1"], bank_free[LBK[j4]]), inc=(si == 1 and rp == r and j4 == 3))
                    for j4 in range(4):
                        q = lsets[j4]
                        if ri_ == 0:
                            a = k.op("act", lambda e: e.copy(out=q.Lb_re[r], in_=ps[:, LBK[j4], 0:NC]), tmax(pl, q.t_yread))
                            if r == 3:
                                a = k.op("act", lambda e: e.copy(out=q.bu_re, in_=ps[:, LBK[j4], 0:NC]), tmax(q.t_dve))
                        else:
                            a = k.op("act", lambda e: e.activation(out=q.Lb_n[r], in_=ps[:, LBK[j4], 0:NC], func=ACT.Copy, scale=-1.0),
                                     tmax(pl, q.t_yread))
                            if r == 3:
                                a = k.op("act", lambda e: e.copy(out=q.bu_im, in_=ps[:, LBK[j4], 0:NC]), tmax(q.t_dve, q.t_gcopy))
                        bank_free[LBK[j4]] = a
                        st["a_e"][j4] = a

        def tables(b, j4):
            st = BS[b]
            pair = 4 * b + j4
            s_ = ssets[j4 % 2]
            ph, rn, fr, sn, cs, rrow = s_.ph, s_.rn, s_.fr, s_.sn, s_.cs, s_.rrow
            kcol = kap4[:, pair:pair + 1]
            v = k.op("dve", lambda e: e.tensor_scalar(out=ph, in0=iotc, scalar1=kcol, scalar2=None, op0=ALU.mult), tmax(s_.t_gcopy))
            v = k.op("dve", lambda e: e.tensor_scalar(out=rn, in0=ph, scalar1=MAGIC, scalar2=MAGIC, op0=ALU.add, op1=ALU.subtract))
            v = k.op("dve", lambda e: e.tensor_tensor(out=fr, in0=ph, in1=rn, op=ALU.subtract))
            a = k.op("act", lambda e: e.activation(out=sn, in_=fr, func=ACT.Sin, scale=TWO_PI), tmax(v, s_.t_dve))
            a = k.op("act", lambda e: e.activation(out=rn, in_=fr, func=ACT.Abs))
            a_t = k.op("act", lambda e: e.activation(out=cs, in_=rn, func=ACT.Sin, scale=-TWO_PI, bias=float(np.pi / 2)))
            a_r = k.op("act", lambda e: e.activation(out=rrow, in_=iotc, func=ACT.Identity, scale=0.0, bias=rho4[:, pair:pair + 1]))
            st["tab"][j4] = tmax(a_t, a_r)

        def chain(b, j4):
            st = BS[b]
            pair = 4 * b + j4
            q = lsets[j4]
            s_ = ssets[j4 % 2]
            bu_re, bu_im, He_re, He_n = q.bu_re, q.bu_im, q.He_re, q.He_n
            ph, rn, fr, sn, cs, rrow = s_.ph, s_.rn, s_.fr, s_.sn, s_.cs, s_.rrow
            a_e = st["a_e"][j4]
            v = k.op("dve", lambda e: e.tensor_tensor(out=ph, in0=cs, in1=bu_re, op=ALU.mult), tmax(st["tab"][j4], a_e))
            v = k.op("dve", lambda e: e.tensor_tensor(out=rn, in0=sn, in1=bu_im, op=ALU.mult))
            v = k.op("dve", lambda e: e.tensor_tensor(out=fr, in0=ph, in1=rn, op=ALU.add))
            v = k.op("dve", lambda e: e.tensor_tensor(out=ph, in0=cs, in1=bu_im, op=ALU.mult))
            v = k.op("dve", lambda e: e.tensor_tensor(out=rn, in0=sn, in1=bu_re, op=ALU.mult))
            v = k.op("dve", lambda e: e.tensor_tensor(out=bu_re, in0=ph, in1=rn, op=ALU.subtract))
            for si, (cc0, ln, tk0) in enumerate(SEGS):
                k.op("dve", lambda e: e.tensor_tensor_scan(out=bu_im[:, cc0:cc0 + ln], data0=rrow[:, cc0:cc0 + ln], data1=fr[:, cc0:cc0 + ln],
                                                           initial=hi_re[:, pair, si:si + 1], op0=ALU.mult, op1=ALU.add))
                v = k.op("dve", lambda e: e.tensor_tensor_scan(out=ph[:, cc0:cc0 + ln], data0=rrow[:, cc0:cc0 + ln], data1=bu_re[:, cc0:cc0 + ln],
                                                               initial=hi_im[:, pair, si:si + 1], op0=ALU.mult, op1=ALU.add))
            gre, gim = bu_im, ph
            k.op("act", lambda e: e.copy(out=ge_re[:, pair, 0:2], in_=gre[:, 7:16:8]), [v])
            k.op("act", lambda e: e.copy(out=ge_re[:, pair, 2:3], in_=gre[:, NC - 1:NC]))
            k.op("act", lambda e: e.copy(out=ge_im[:, pair, 0:2], in_=gim[:, 7:16:8]))
            gc = k.op("act", lambda e: e.copy(out=ge_im[:, pair, 2:3], in_=gim[:, NC - 1:NC]))
            q.t_gcopy = gc
            s_.t_gcopy = gc
            v = k.op("dve", lambda e: e.tensor_tensor(out=rn, in0=cs, in1=gre, op=ALU.mult))
            v = k.op("dve", lambda e: e.tensor_tensor(out=fr, in0=sn, in1=gim, op=ALU.mult))
            v = k.op("dve", lambda e: e.tensor_tensor(out=He_re[:, 1:NC + 1], in0=rn, in1=fr, op=ALU.subtract), tmax(q.t_yread))
            v = k.op("dve", lambda e: e.tensor_tensor(out=rn, in0=sn, in1=gre, op=ALU.mult))
            v = k.op("dve", lambda e: e.tensor_tensor(out=fr, in0=cs, in1=gim, op=ALU.mult))
            v = k.op("dve", lambda e: e.scalar_tensor_tensor(out=He_n[:, 1:NC + 1], in0=rn, scalar=-1.0, in1=fr, op0=ALU.mult, op1=ALU.subtract))
            v = k.op("dve", lambda e: e.tensor_copy(out=He_re[:, 0:17:8], in_=hi_re[:, pair, :]))
            v = k.op("dve", lambda e: e.tensor_scalar(out=He_n[:, 0:17:8], in0=hi_im[:, pair, :], scalar1=-1.0, scalar2=None, op0=ALU.mult))
            q.t_dve = v
            s_.t_dve = v
            return v

        def yphase(b, chain_toks):
            sb_ = b % 2
            st = BS[b]
            py = None
            for r in range(4):
                for step in range(4):
                    for j4 in range(4):
                        rows = slice(32 * j4, 32 * j4 + 32)
                        q = lsets[j4]
                        yo = ps[rows, r, 0:NC]
                        lhs_k = (0, 1, 2 * (r + 1), 2 * (r + 1) + 1)[step]
                        rhs = (q.Lb_re[r], q.Lb_n[r], q.He_re[:, 0:NC], q.He_n[:, 0:NC])[step]
                        py = k.op("pe", lambda e: e.matmul(yo, lhsT=ckb[sb_][:, lhs_k, rows], rhs=rhs, start=(step == 0), stop=(step == 3),
                                                           tile_position=(0, 32 * j4)),
                                  tmax(chain_toks, st["a_e"][j4], st["lb2"], yb_free[r]), inc=(r == 3 and step == 3 and j4 == 3))
            for j4 in range(4):
                lsets[j4].t_yread = py
            st["last_y"] = py

        def epilogue(b):
            sb_ = b % 2
            st = BS[b]
            uT = uTs[sb_]
            last_y = st["last_y"]
            blk_free[sb_] = last_y
            v = None
            a = None
            for r in range(4):
                tmpy = tmpys[r % 2]
                for (cc0, ln, tk0) in SEGS:
                    tsl = slice(tk0 + r, tk0 + r + 4 * (ln - 1) + 1, 4)
                    v = k.op("dve", lambda e: e.scalar_tensor_tensor(out=tmpy[:, cc0:cc0 + ln], in0=uT[:, tsl], scalar=dvec[:, b:b + 1],
                                                                    in1=ps[:, r, cc0:cc0 + ln], op0=ALU.mult, op1=ALU.add),
                             tmax(last_y, EPI["a"][r % 2]))
                    a = k.op("act", lambda e: e.activation(out=g_all[:, b, tsl], in_=tmpy[:, cc0:cc0 + ln], func=ACT.Gelu_apprx_tanh), [v])
                EPI["a"][r % 2] = a
                yb_free[r] = v
            uT_free[sb_] = tmax(last_y, v)
            return a

        for b_ in range(_DBG_NBLK):
            uproj(b_)
            tables(b_, 0)
            tables(b_, 1)
            lphase(b_)
            toks = [chain(b_, 0)]
            tables(b_, 2)
            toks.append(chain(b_, 1))
            tables(b_, 3)
            toks.append(chain(b_, 2))
            toks.append(chain(b_, 3))
            yphase(b_, tmax(*toks))
            g_done = epilogue(b_)
        for hc in (0, C_A0 - 2, C_B0 - 2):
            g_done = k.op("act", lambda e: e.activation(out=g_all[:, :, hc:hc + 2], in_=g_all[:, :, 2:4], func=ACT.Copy, scale=0.0), [g_done])
        v = None
        for si, (cc_, ss_, gi) in enumerate(((cN, sN, 2), (c32, s32, 0), (c32, s32, 1))):
            k.op("dve", lambda e: e.tensor_tensor(out=w1_, in0=cc_, in1=ge_re[:, :, gi], op=ALU.mult), tmax(ssets[0].t_gcopy, ssets[1].t_gcopy, g_done))
            k.op("dve", lambda e: e.tensor_tensor(out=w2_, in0=ss_, in1=ge_im[:, :, gi], op=ALU.mult))
            k.op("dve", lambda e: e.tensor_tensor(out=hfin[:, :, si, 0], in0=w1_, in1=w2_, op=ALU.subtract))
            k.op("dve", lambda e: e.tensor_tensor(out=w1_, in0=ss_, in1=ge_re[:, :, gi], op=ALU.mult))
            k.op("dve", lambda e: e.tensor_tensor(out=w2_, in0=cc_, in1=ge_im[:, :, gi], op=ALU.mult))
            v = k.op("dve", lambda e: e.tensor_tensor(out=hfin[:, :, si, 1], in0=w1_, in1=w2_, op=ALU.add))
        fsem = k.dsem()
        hf_tok = k.dma("sp", hf_o, hfin.rearrange("p a b c -> p (a b c)"), fsem, [v])

        ar.top = BASE
        y2 = ar.alloc([128, 32, T], BF16)
        wg = [ar.alloc([128, 32, 128], BF16) for _ in range(2)]
        wz = [ar.alloc([128, 16, 128], BF16) for _ in range(2)]
        sg = [ar.alloc([128, 366]) for _ in range(2)]
        zs = [ar.alloc([128, 366]) for _ in range(2)]
        guard = tmax(hf_tok, g_done, lsets[0].t_yread, v)
        wg_free = [guard, guard]
        tmp_free = [guard, guard]
        step = 0
        y2_done = None
        for ob in range(32):
            s = ob % 2
            wlg = k.dma("pool", wg[s].rearrange("p a b -> p (a b)"), wgl[ob], wsem[s], tmax(wg_free[s]), max_dma_last_dim=8192)
            wlz = k.dma("pool", wz[s].rearrange("p a b -> p (a b)"), wsz[ob], wsem[2 + s], max_dma_last_dim=8192)
            p = None
            for (c0, c1) in RANGES:
                n = c1 - c0
                st_ = step % 2
                step += 1
                bg_, bz_ = st_ * 2, st_ * 2 + 1
                for kb in range(32):
                    p = k.op("pe", lambda e: e.matmul(ps[:, bg_, 0:n], lhsT=wg[s][:, kb, :], rhs=g_all[:, kb, c0:c1],
                                                      start=(kb == 0), stop=(kb == 31)), tmax(wlg, g_done, bank_free[bg_]), inc=False)
                for kc in range(16):
                    p = k.op("pe", lambda e: e.matmul(ps[:, bz_, 0:n], lhsT=wz[s][:, kc, :], rhs=xn1[:, kc, c0:c1],
                                                      start=(kc == 0), stop=(kc == 15)), tmax(wlz, bank_free[bz_]), inc=(kc == 15))
                a1 = k.op("act", lambda e: e.activation(out=sg[st_][:, 0:n], in_=ps[:, bg_, 0:n], func=ACT.Sigmoid,
                                                        bias=bgl[:, ob:ob + 1]), tmax(p, tmp_free[st_]))
                a2 = k.op("act", lambda e: e.activation(out=zs[st_][:, 0:n], in_=ps[:, bz_, 0:n], func=ACT.Silu))
                bank_free[bg_] = a2
                bank_free[bz_] = a2
                v = k.op("dve", lambda e: e.tensor_tensor(out=sg[st_][:, 0:n], in0=sg[st_][:, 0:n], in1=g_all[:, ob, c0:c1], op=ALU.mult), [a2])
                v = k.op("dve", lambda e: e.tensor_tensor(out=y2[:, ob, c0:c1], in0=sg[st_][:, 0:n], in1=zs[st_][:, 0:n], op=ALU.mult))
                tmp_free[st_] = v
                y2_done = v
            wg_free[s] = p
        glu_pe_done = p

        ar.top = 0
        h2T = ar.alloc([128, 16, T])
        wo = [ar.alloc([128, 32, 128], BF16) for _ in range(2)]
        h1c = [ar.alloc([128, T]) for _ in range(2)]
        assert ar.top <= BASE
        ar.top = BASE + 17568
        xt = [ar.alloc([128, D]) for _ in range(2)]
        sq2 = [ar.alloc([128, T]) for _ in range(2)]
        rb2 = ar.alloc([128, T])
        ones2 = ar.alloc([128, 128])
        guard = tmax(glu_pe_done, y2_done)
        vm2 = k.op("dve", lambda e: e.memset(ones2, 1.0 / D), guard)
        wo_free = [guard, guard]
        h1c_free = [guard, guard]
        h1sem = [k.dsem(), k.dsem()]
        last_h = None
        for db in range(16):
            s = db % 2
            wl = k.dma("pool", wo[s].rearrange("p a b -> p (a b)"), wso[db], wsem[s], tmax(wo_free[s]), max_dma_last_dim=8192)
            hl = k.dma("sp", h1c[s], h1T_i[:, db * T:(db + 1) * T], h1sem[s], tmax(h1c_free[s]))
            p = None
            for ri, (c0, c1) in enumerate(RANGES):
                n = c1 - c0
                bk = 2 + (db % 2) * 3 + ri
                for eb in range(32):
                    p = k.op("pe", lambda e: e.matmul(ps[:, bk, 0:n], lhsT=wo[s][:, eb, :], rhs=y2[:, eb, c0:c1],
                                                      start=(eb == 0), stop=(eb == 31)), tmax(wl, y2_done, bank_free[bk]), inc=(eb == 31))
                bank_free[bk] = k.op("dve", lambda e: e.tensor_tensor(out=h2T[:, db, c0:c1], in0=ps[:, bk, 0:n], in1=h1c[s][:, c0:c1],
                                                                     op=ALU.add), tmax(p, hl, guard))
                last_h = bank_free[bk]
            wo_free[s] = p
            h1c_free[s] = last_h
        sq_free = [guard, guard]
        p = None
        for db in range(16):
            s = db % 2
            a = k.op("act", lambda e: e.activation(out=sq2[s], in_=h2T[:, db, :], func=ACT.Square), tmax(last_h, sq_free[s]))
            for ri, (c0, c1) in enumerate(RANGES):
                n = c1 - c0
                p = k.op("pe", lambda e: e.matmul(ps[:, 2 + ri, 0:n], lhsT=ones2, rhs=sq2[s][:, c0:c1], start=(db == 0), stop=(db == 15)),
                         tmax(a, vm2, bank_free[2 + ri], last_h), inc=(ri == 2))
            sq_free[s] = p
        v = None
        for ri, (c0, c1) in enumerate(RANGES):
            n = c1 - c0
            v = k.op("dve", lambda e: e.tensor_scalar(out=rb2[:, c0:c1], in0=ps[:, 2 + ri, 0:n], scalar1=EPS, scalar2=None, op0=ALU.add), [p])
            bank_free[2 + ri] = v
        a = k.op("act", lambda e: e.activation(out=rb2, in_=rb2, func=ACT.Sqrt), [v])
        v = k.op("dve", lambda e: e.reciprocal(out=rb2, in_=rb2), [a])
        for db in range(16):
            v = k.op("dve", lambda e: e.scalar_tensor_tensor(out=h2T[:, db, :], in0=h2T[:, db, :], scalar=nw[:, 32 + db:33 + db],
                                                            in1=rb2, op0=ALU.mult, op1=ALU.mult))
        yn_done = v
        osem = [k.dsem(), k.dsem()]
        xt_free = [None, None]
        outs = []
        for i in range(NTT):
            r0 = 128 * i
            nr = min(128, T - r0)
            s = i % 2
            p = None
            a3 = None
            for g in range(4):
                bk = g % 2
                for q in range(4):
                    kc = 4 * g + q
                    p = k.op("pe", lambda e: e.transpose(out=ps[0:nr, bk, q * 128:(q + 1) * 128], in_=h2T[:, kc, r0:r0 + nr],
                                                         identity=ident), tmax(yn_done, bank_free[bk]), inc=(q == 3))
                a3 = k.op("act", lambda e: e.copy(out=xt[s][0:nr, 512 * g:512 * (g + 1)], in_=ps[0:nr, bk, :]), tmax(p, xt_free[s]))
                bank_free[bk] = a3
            od = k.dma("sp", y_o[r0:r0 + nr, :], xt[s][0:nr, :], osem[s], [a3])
            xt_free[s] = od
            outs.append(od)
        k.wait("sp", tmax(hf_tok, outs[-1], outs[-2]))
    return nc

def host_layout_a(inp):
    f = np.float32
    xp = inp["x_prompt"][0]
    P = np.concatenate([np.zeros((16, D), f), inp["meta_tokens"].astype(f), xp], axis=0)
    Ppad = np.concatenate([np.zeros((2, D), f), P], axis=0)
    xs = inp["x_sample"]
    z2 = np.zeros((2, D), f)
    xin = []
    for c in range(NCORE):
        rows = [Ppad[SEG * c:SEG * c + SEG + 2], z2, xs[2 * c], z2, xs[2 * c + 1]]
        xin.append(np.ascontiguousarray(np.concatenate(rows, axis=0)))
    nw = np.concatenate([inp["norm_w"][0].reshape(16, 128).T, inp["norm_w"][1].reshape(16, 128).T], axis=1)
    wci = inp["conv_w_in"][0].reshape(16, 128, 4, 32, 128)
    wci = np.ascontiguousarray(wci.transpose(3, 1, 0, 2, 4)).reshape(32, 128, 16 * 512)
    cwv = np.concatenate([inp["conv_w"][0], inp["conv_b"]], axis=0)
    cw = np.ascontiguousarray(cwv.reshape(4, 32, 128).transpose(2, 1, 0)).reshape(128, 128)
    cache = inp["cache_conv"][0]
    ccs = []
    for c in range(NCORE):
        a = cache[2 * c:2 * c + 2].reshape(2, 2, 32, 128)
        ccs.append(np.ascontiguousarray(a.transpose(3, 2, 0, 1)).reshape(128, 128))
    wco = inp["conv_w_out"][0].reshape(32, 128, 16, 128)
    wco = np.ascontiguousarray(wco.transpose(2, 1, 0, 3)).reshape(16, 128, 32 * 128)
    sm = lambda a: np.ascontiguousarray(a.reshape(NPAIR, 2, 64).transpose(1, 2, 0)).reshape(128, NPAIR)
    are = sm(inp["ssm_a_re"][0])
    aim = sm(inp["ssm_a_im"][0])
    ldt = sm(np.repeat(inp["ssm_log_dt"][0][:, None], 64, axis=1))
    smb = lambda a: np.ascontiguousarray(a.reshape(NPAIR, 2, 64, 16).transpose(1, 2, 0, 3)).reshape(128, NPAIR * 16)
    bre = smb(inp["ssm_b_re"][0])
    bim = smb(inp["ssm_b_im"][0])
    wsu = inp["ssm_w_in"][0][:, :E].reshape(16, 128, 32, 128)
    wsu = np.ascontiguousarray(wsu.transpose(2, 1, 0, 3)).reshape(32, 128, 16 * 128)
    io = np.zeros((T,), f)
    io[C_P0:C_P0 + SEG] = np.arange(1, SEG + 1)
    io[C_A0:C_A0 + 32] = np.arange(1, 33)
    io[C_B0:C_B0 + 32] = np.arange(1, 33)
    iota = np.ascontiguousarray(np.broadcast_to(io[None, :], (128, T)))
    shared = dict(ident=np.eye(128, dtype=f), nw=np.ascontiguousarray(nw.astype(f)), wci=wci, cw=cw, wco=wco, are=are, aim=aim,
                  ldt=ldt, bre=bre, bim=bim, wsu=wsu, iota=iota)
    return [dict(shared, xin=xin[c], cc=ccs[c]) for c in range(NCORE)]


def host_layout_b(inp, res_a):
    f = np.float32
    base = host_layout_a.cache
    f0 = [np.asarray(r["f0"]).reshape(128, NPAIR, 2) for r in res_a]
    nw48 = np.ascontiguousarray(np.concatenate([base["nw"], inp["final_norm_w"].astype(f).reshape(16, 128).T], axis=1))

    def cbd(cm):
        c4 = cm.reshape(NPAIR, 2, 16, 64)
        out = np.zeros((2, 64, NPAIR, 2, 16), f)
        for g2 in range(2):
            out[g2, :, :, g2, :] = c4[:, g2].transpose(2, 0, 1)
        return out.reshape(128, NPAIR * 32)
    cbr = cbd(inp["ssm_c_re"][0])
    cbi = cbd(inp["ssm_c_im"][0])
    dvec = np.ascontiguousarray(inp["ssm_d"][0].reshape(32, 128).T)
    bgl = np.ascontiguousarray(inp["ssm_b_glu"][0].reshape(32, 128).T)
    wsz = inp["ssm_w_in"][0][:, E:].reshape(16, 128, 32, 128)
    wsz = np.ascontiguousarray(wsz.transpose(2, 1, 0, 3)).reshape(32, 128, 16 * 128)
    wgl = inp["ssm_w_glu"][0].reshape(32, 128, 32, 128)
    wgl = np.ascontiguousarray(wgl.transpose(2, 1, 0, 3)).reshape(32, 128, 32 * 128)
    wso = inp["ssm_w_out"][0].reshape(32, 128, 16, 128)
    wso = np.ascontiguousarray(wso.transpose(2, 1, 0, 3)).reshape(16, 128, 32 * 128)
    ioc = np.concatenate([np.arange(1, 9), np.arange(1, 9), np.arange(1, 258)]).astype(f)
    iotc = np.ascontiguousarray(np.broadcast_to(ioc[None, :], (128, 273)))
    shared = dict(ident=base["ident"], nw=nw48, iotc=iotc, are=base["are"], aim=base["aim"], ldt=base["ldt"], bre=base["bre"],
                  bim=base["bim"], cbr=cbr, cbi=cbi, dvec=dvec, bgl=bgl, wsu=base["wsu"], wsz=wsz, wgl=wgl, wso=wso)
    maps = []
    for c in range(NCORE):
        fp = np.zeros((128, 7, NPAIR, 2), f)
        for j in range(7):
            if c - 1 - j >= 0:
                fp[:, j] = f0[c - 1 - j]
        hs = []
        for st in (inp["state_ssm_re"][0], inp["state_ssm_im"][0]):
            a = st[2 * c:2 * c + 2].reshape(2, NPAIR, 2, 64).transpose(2, 3, 1, 0)
            hs.append(np.ascontiguousarray(a).reshape(128, NPAIR * 2))
        maps.append(dict(shared, h1T_i=np.asarray(res_a[c]["h1T"]), f0prev=fp.reshape(128, 7 * NPAIR * 2), h0r=hs[0], h0i=hs[1]))
    return maps


def assemble(inp, res_a, res_b):
    f = np.float32
    y_prompt = np.zeros((1, 8192, D), f)
    y_sample = np.zeros((16, 32, D), f)
    ncp = np.zeros((1, 1, 2, E), f)
    ncs = np.zeros((1, 16, 2, E), f)
    hpr = np.zeros((1, 1, 256, 64), f)
    hpi = np.zeros((1, 1, 256, 64), f)
    hsr = np.zeros((1, 16, 256, 64), f)
    hsi = np.zeros((1, 16, 256, 64), f)
    for c in range(NCORE):
        yo = np.asarray(res_b[c]["yout"])
        lo = SEG * c - 32
        c0 = C_P0 + max(0, -lo)
        y_prompt[0, max(lo, 0):lo + SEG] = yo[c0:C_P0 + SEG]
        y_sample[2 * c] = yo[C_A0:C_A0 + 32]
        y_sample[2 * c + 1] = yo[C_B0:C_B0 + 32]
        ncv = np.asarray(res_a[c]["ncv"]).reshape(128, 32, 3, 2)
        hf = np.asarray(res_b[c]["hfin"]).reshape(2, 64, NPAIR, 3, 2)
        hf = hf.transpose(3, 4, 2, 0, 1).reshape(3, 2, 256, 64)
        if c == NCORE - 1:
            ncp[0, 0] = ncv[:, :, 0, :].transpose(2, 1, 0).reshape(2, E)
            hpr[0, 0] = hf[0, 0]
            hpi[0, 0] = hf[0, 1]
        for s in range(2):
            ncs[0, 2 * c + s] = ncv[:, :, 1 + s, :].transpose(2, 1, 0).reshape(2, E)
            hsr[0, 2 * c + s] = hf[1 + s, 0]
            hsi[0, 2 * c + s] = hf[1 + s, 1]
    return (y_prompt, y_sample, ncp, ncs, hpr, hpi, hsr, hsi)


_PROGS = {}


def kernel(**inp):
    inp = {k_: np.asarray(v) for k_, v in inp.items()}
    maps_a = host_layout_a(inp)
    host_layout_a.cache = maps_a[0]
    if "a" not in _PROGS:
        _PROGS["a"] = build_a()
    res_a = run_bass_kernel_spmd(_PROGS["a"], maps_a, core_ids=list(range(NCORE))).results
    maps_b = host_layout_b(inp, res_a)
    if "b" not in _PROGS:
        _PROGS["b"] = build_b()
    res_b = run_bass_kernel_spmd(_PROGS["b"], maps_b, core_ids=list(range(NCORE))).results
    return assemble(inp, res_a, res_b)
```

```python
import contextlib
import numpy as np
import concourse.bass as bass
import concourse.mybir as mybir
from concourse.bass_utils import run_bass_kernel_spmd

F32 = mybir.dt.float32
BF16 = mybir.dt.bfloat16
ACT = mybir.ActivationFunctionType
ALU = mybir.AluOpType

D = 2048
E = 4096
NCORE = 8
SEG = 1028
T = 1098
C_P0, C_A0, C_B0 = 2, 1032, 1066
RANGES = [(0, 366), (366, 732), (732, 1098)]
NTT = 9
EPS = 1e-6
MAGIC = 12582912.0
TWO_PI = float(2.0 * np.pi)
NPAIR = 128
STRICT_SAME_ENGINE = True


class DSem:
    def __init__(self, h):
        self.h = h
        self.val = 0


class K:
    def __init__(self, nc, es):
        self.nc = nc
        self.es = es
        self.eng = {"pe": nc.tensor, "act": nc.scalar, "dve": nc.vector, "pool": nc.gpsimd, "sp": nc.sync}
        self.sem = {k: es.enter_context(nc.semaphore("s_" + k)) for k in ("pe", "act", "dve", "pool")}
        self.cnt = {k: 0 for k in self.sem}
        self.seen = {k: {} for k in self.eng}
        self.nd = 0
        self.selfseen = {k_: 0 for k_ in self.sem}

    def dsem(self):
        self.nd += 1
        return DSem(self.es.enter_context(self.nc.semaphore("d%d" % self.nd)))

    def wait(self, en, toks):
        for t in toks:
            if t is None:
                continue
            if t[0] == "c":
                _, src, val = t
                if src == en:
                    continue
                if self.seen[en].get(src, 0) >= val:
                    continue
                self.eng[en].wait_ge(self.sem[src], val)
                self.seen[en][src] = val
            else:
                _, ds, val = t
                key = id(ds)
                if self.seen[en].get(key, 0) >= val:
                    continue
                self.eng[en].wait_ge(ds.h, val)
                self.seen[en][key] = val

    def wait_self(self, en, tok):
        if tok is None:
            return
        _, src, val = tok
        assert src == en
        self.eng[en].wait_ge(self.sem[src], val)

    def op(self, en, fn, waits=(), inc=True):
        self.wait(en, waits)
        if STRICT_SAME_ENGINE and en in ("act", "dve", "pool") and self.cnt[en] > self.selfseen[en]:
            self.eng[en].wait_ge(self.sem[en], self.cnt[en])
            self.selfseen[en] = self.cnt[en]
        ins = fn(self.eng[en])
        if inc:
            self.cnt[en] += 1
            ins.then_inc(self.sem[en], 1)
            return ("c", en, self.cnt[en])
        return None

    def dma(self, en, out, in_, ds, waits=(), **kw):
        self.wait(en, waits)
        ins = self.eng[en].dma_start(out=out, in_=in_, **kw)
        ds.val += 16
        ins.then_inc(ds.h, 16)
        return ("d", ds, ds.val)


def tmax(*toks):
    out = []
    for t in toks:
        if t is None:
            continue
        if isinstance(t, list):
            out.extend(t)
        else:
            out.append(t)
    return out


class Arena:
    def __init__(self, ap, nwords):
        self.ap = ap
        self.n = nwords
        self.top = 0

    def alloc(self, shape, dt=F32):
        n = 1
        for s_ in shape[1:]:
            n *= s_
        words = (n + 1) // 2 if dt == BF16 else n
        words = (words + 1) // 2 * 2
        off = self.top
        self.top += words
        assert self.top <= self.n, ("arena overflow", self.top, self.n)
        v = self.ap[:, off:off + words]
        if dt == BF16:
            v = v.bitcast(BF16)
        v = v[:, 0:n]
        if len(shape) == 3:
            v = v.rearrange("p (a b) -> p a b", a=shape[1])
        elif len(shape) == 4:
            v = v.rearrange("p (a b c) -> p a b c", a=shape[1], b=shape[2])
        return v


def emit_ssm_consts(k, ar, ps, ident, are, aim, ldt, bre, bim, waits, bank_free, nsteps_list, npow=1, dram_bft=None):
    A = lambda: ar.alloc([128, NPAIR])
    dt_, xr, rho, kap, fr, t0, t1, sn, cs, fre, fim, den, t2 = [A() for _ in range(13)]
    a = k.op("act", lambda e: e.activation(out=dt_, in_=ldt, func=ACT.Exp), waits)
    v = k.op("dve", lambda e: e.tensor_tensor(out=xr, in0=are, in1=dt_, op=ALU.mult), tmax(a, waits))
    a = k.op("act", lambda e: e.activation(out=rho, in_=xr, func=ACT.Exp), [v])
    v = k.op("dve", lambda e: e.scalar_tensor_tensor(out=kap, in0=aim, scalar=1.0 / TWO_PI, in1=dt_, op0=ALU.mult, op1=ALU.mult))

    def cossin(mult, c_out, s_out):
        v_ = k.op("dve", lambda e: e.tensor_scalar(out=t0, in0=kap, scalar1=float(mult), scalar2=None, op0=ALU.mult))
        v_ = k.op("dve", lambda e: e.tensor_scalar(out=t1, in0=t0, scalar1=MAGIC, scalar2=MAGIC, op0=ALU.add, op1=ALU.subtract))
        v_ = k.op("dve", lambda e: e.tensor_tensor(out=fr, in0=t0, in1=t1, op=ALU.subtract))
        a_ = k.op("act", lambda e: e.activation(out=s_out, in_=fr, func=ACT.Sin, scale=TWO_PI), [v_])
        a_ = k.op("act", lambda e: e.activation(out=t1, in_=fr, func=ACT.Abs))
        k.wait_self("act", a_)
        a_ = k.op("act", lambda e: e.activation(out=c_out, in_=t1, func=ACT.Sin, scale=-TWO_PI, bias=float(np.pi / 2)))
        k.wait("dve", [a_])
        return a_

    a = cossin(1.0, cs, sn)
    abr, abi = t0, t1
    ab_re = A() if npow > 1 else abr
    ab_im = A() if npow > 1 else None
    v = k.op("dve", lambda e: e.tensor_tensor(out=ab_re, in0=rho, in1=cs, op=ALU.mult), [a])
    v = k.op("dve", lambda e: e.tensor_scalar(out=abr, in0=ab_re, scalar1=-1.0, scalar2=None, op0=ALU.add))
    v = k.op("dve", lambda e: e.tensor_tensor(out=abi, in0=rho, in1=sn, op=ALU.mult))
    if npow > 1:
        v = k.op("dve", lambda e: e.tensor_copy(out=ab_im, in_=abi))
    v = k.op("dve", lambda e: e.tensor_tensor(out=den, in0=are, in1=are, op=ALU.mult))
    v = k.op("dve", lambda e: e.tensor_tensor(out=t2, in0=aim, in1=aim, op=ALU.mult))
    v = k.op("dve", lambda e: e.tensor_tensor(out=den, in0=den, in1=t2, op=ALU.add))
    v = k.op("dve", lambda e: e.reciprocal(out=den, in_=den))
    v = k.op("dve", lambda e: e.tensor_tensor(out=fre, in0=abr, in1=are, op=ALU.mult))
    v = k.op("dve", lambda e: e.tensor_tensor(out=t2, in0=abi, in1=aim, op=ALU.mult))
    v = k.op("dve", lambda e: e.tensor_tensor(out=fre, in0=fre, in1=t2, op=ALU.add))
    v = k.op("dve", lambda e: e.tensor_tensor(out=fre, in0=fre, in1=den, op=ALU.mult))
    v = k.op("dve", lambda e: e.tensor_tensor(out=fim, in0=abi, in1=are, op=ALU.mult))
    v = k.op("dve", lambda e: e.tensor_tensor(out=t2, in0=abr, in1=aim, op=ALU.mult))
    v = k.op("dve", lambda e: e.tensor_tensor(out=fim, in0=fim, in1=t2, op=ALU.subtract))
    v = k.op("dve", lambda e: e.tensor_tensor(out=fim, in0=fim, in1=den, op=ALU.mult))
    rots = {}
    for nst in nsteps_list:
        cN, sN, rN = A(), A(), A()
        a = cossin(float(nst), cN, sN)
        a = k.op("act", lambda e: e.activation(out=rN, in_=xr, func=ACT.Exp, scale=float(nst)))
        rots[nst] = (cN, sN, rN, a)
    if dram_bft is None:
        bft_re = [ar.alloc([128, 32, 128], BF16) for _ in range(npow)]
        bft_im = [ar.alloc([128, 32, 128], BF16) for _ in range(npow)]
    else:
        bft_re = bft_im = None
    pw_re, pw_im, pa, pb_ = sn, cs, den, t2
    keep_top = ar.top
    bd_re = ar.alloc([128, NPAIR, 32], BF16)
    bd_im = ar.alloc([128, NPAIR, 32], BF16)
    u1 = ar.alloc([128, NPAIR, 16])
    u2 = ar.alloc([128, NPAIR, 16])
    ident_bf = ar.alloc([128, 128], BF16)
    k.op("dve", lambda e: e.tensor_copy(out=ident_bf, in_=ident))
    k.op("dve", lambda e: e.memset(bd_re, 0.0))
    k.op("dve", lambda e: e.memset(bd_im, 0.0))
    b3r = bre.rearrange("p (j c) -> p j c", c=16)
    b3i = bim.rearrange("p (j c) -> p j c", c=16)
    k.op("dve", lambda e: e.tensor_copy(out=pw_re, in_=fre))
    k.op("dve", lambda e: e.tensor_copy(out=pw_im, in_=fim))
    if dram_bft is not None:
        stage = [ar.alloc([128, 32, 128], BF16) for _ in range(2)]
        stage_sem = [k.dsem(), k.dsem()]
        stage_tok = [None, None]
        sink_toks = [None, None]
    else:
        sink_toks = [None, None]
    last = None
    vlast = None
    i = 0
    for pw in range(npow):
        if pw > 0:
            k.op("dve", lambda e: e.tensor_tensor(out=pa, in0=pw_re, in1=ab_re, op=ALU.mult), tmax(last))
            k.op("dve", lambda e: e.tensor_tensor(out=pb_, in0=pw_im, in1=ab_im, op=ALU.mult))
            k.op("dve", lambda e: e.tensor_tensor(out=pa, in0=pa, in1=pb_, op=ALU.subtract))
            k.op("dve", lambda e: e.tensor_tensor(out=pb_, in0=pw_re, in1=ab_im, op=ALU.mult))
            k.op("dve", lambda e: e.tensor_tensor(out=pw_im, in0=pw_im, in1=ab_re, op=ALU.mult))
            k.op("dve", lambda e: e.tensor_tensor(out=pw_im, in0=pw_im, in1=pb_, op=ALU.add))
            k.op("dve", lambda e: e.tensor_copy(out=pw_re, in_=pa))
        frb = pw_re.unsqueeze(2).to_broadcast([128, NPAIR, 16])
        fib = pw_im.unsqueeze(2).to_broadcast([128, NPAIR, 16])
        k.op("dve", lambda e: e.tensor_tensor(out=u1, in0=b3r, in1=frb, op=ALU.mult), tmax(last))
        k.op("dve", lambda e: e.tensor_tensor(out=u2, in0=b3i, in1=fib, op=ALU.mult))
        for (lo, hi, c0) in ((0, 64, 0), (64, 128, 16)):
            k.op("dve", lambda e: e.tensor_tensor(out=bd_re[lo:hi, :, c0:c0 + 16], in0=u1[lo:hi], in1=u2[lo:hi], op=ALU.subtract))
        k.op("dve", lambda e: e.tensor_tensor(out=u1, in0=b3i, in1=frb, op=ALU.mult))
        k.op("dve", lambda e: e.tensor_tensor(out=u2, in0=b3r, in1=fib, op=ALU.mult))
        for (lo, hi, c0) in ((0, 64, 0), (64, 128, 16)):
            vlast = k.op("dve", lambda e: e.tensor_tensor(out=bd_im[lo:hi, :, c0:c0 + 16], in0=u1[lo:hi], in1=u2[lo:hi], op=ALU.add))
        for ri_, src in enumerate((bd_re, bd_im)):
            if dram_bft is None:
                dst = (bft_re, bft_im)[ri_][pw]
                dst_wait = None
            else:
                dst = stage[ri_]
                dst_wait = stage_tok[ri_]
            for b in range(32):
                bk = i % 2
                pv = ps[:, bk, 0:64].bitcast(BF16)
                p = k.op("pe", lambda e: e.transpose(out=pv, in_=src[:, 4 * b:4 * b + 4, :].rearrange("p a c -> p (a c)"),
                                                     identity=ident_bf), tmax(vlast, bank_free[bk]))
                bank_free[bk] = k.op("act", lambda e: e.copy(out=dst[:, b, :], in_=pv), tmax(p, dst_wait))
                last = bank_free[bk]
                i += 1
            if dram_bft is not None:
                stage_tok[ri_] = k.dma("sp", dram_bft[2 * pw + ri_], dst.rearrange("p a b -> p (a b)"), stage_sem[ri_], [last])
                sink_toks[ri_] = stage_tok[ri_]
    return dict(rho=rho, kap=kap, xr=xr, bft_re=bft_re, bft_im=bft_im, rots=rots, tok=tmax(last, vlast, sink_toks[0], sink_toks[1]),
                keep_top=keep_top, ab_re=ab_re, ab_im=ab_im)


class PairBufs:
    def __init__(self, ar, ncols):
        A = lambda: ar.alloc([128, ncols])
        self.bu_re, self.bu_im = A(), A()
        self.ph, self.rn, self.fr = A(), A(), A()
        self.ab = self.rn
        self.sn, self.cs, self.rrow = A(), A(), A()
        self.zre, self.zim, self.gre, self.gim = A(), A(), A(), A()
        self.t_prerot = None
        self.t_tabact = None
        self.t_scan = None
        self.t_gread = None


AR_WORDS = 51500


def build_a(stop_after=99):
    nc = bass.Bass("TRN2", target_bir_lowering=False)
    dr = lambda name, shape, kind="ExternalInput": nc.dram_tensor(name, shape, F32, kind=kind).ap()
    xin = dr("xin", [T, D])
    ident_d = dr("ident", [128, 128])
    nw_d = dr("nw", [128, 32])
    wci = dr("wci", [32, 128, 16 * 512])
    cw_d = dr("cw", [128, 32 * 4])
    cc_d = dr("cc", [128, 32 * 4])
    wco = dr("wco", [16, 128, 32 * 128])
    are_d = dr("are", [128, NPAIR])
    aim_d = dr("aim", [128, NPAIR])
    ldt_d = dr("ldt", [128, NPAIR])
    bre_d = dr("bre", [128, NPAIR * 16])
    bim_d = dr("bim", [128, NPAIR * 16])
    wsu = dr("wsu", [32, 128, 16 * 128])
    iota_d = dr("iota", [128, T])
    h1T_o = dr("h1T", [128, 16 * T], "ExternalOutput")
    ncv_o = dr("ncv", [128, 32 * 6], "ExternalOutput")
    f0_o = dr("f0", [128, NPAIR * 2], "ExternalOutput")

    with contextlib.ExitStack() as es:
        k = K(nc, es)
        arena_t = es.enter_context(nc.sbuf_tensor("arena", [128, AR_WORDS], F32))
        ar = Arena(arena_t[:], AR_WORDS)
        ps = es.enter_context(nc.psum_tensor("ps", [128, 8, 512], F32))
        small_t = es.enter_context(nc.sbuf_tensor("small", [128, 800], F32))
        sm = Arena(small_t[:], 800)
        ident = sm.alloc([128, 128])
        nw = sm.alloc([128, 32])
        cw = sm.alloc([128, 32, 4])
        cc = sm.alloc([128, 32, 2, 2])
        ncv = sm.alloc([128, 32, 6])
        csem = k.dsem()
        k.dma("sp", ident, ident_d, csem)
        k.dma("sp", nw, nw_d, csem)
        k.dma("sp", cw.rearrange("p a b -> p (a b)"), cw_d, csem)
        cst = k.dma("sp", cc.rearrange("p a b c -> p (a b c)"), cc_d, csem)
        xsem = [k.dsem(), k.dsem()]
        wsem = [k.dsem(), k.dsem()]
        bank_free = [None] * 8

        ar.top = 0
        y0T = ar.alloc([128, 32, T], BF16)
        top_h1 = ar.top
        xn = ar.alloc([128, 16, T], BF16)
        wc = [ar.alloc([128, 16, 512], BF16) for _ in range(2)]
        top_s2 = ar.top
        xs = [ar.alloc([128, D]) for _ in range(2)]
        ss = ar.alloc([128, 16])
        rstd = ar.alloc([128, 16])
        ar.top = AR_WORDS - 2 * D
        xbuf = [ar.alloc([128, D]) for _ in range(2)]

        k.op("dve", lambda e: e.memset(ss, 0.0))
        ss_zero = k.op("dve", lambda e: e.memset(rstd, 0.0))
        xbuf_free = [None, None]
        xs_free = [None, None]
        last_xn = None
        for i in range(NTT):
            r0 = 128 * i
            nr = min(128, T - r0)
            s = i % 2
            ld = k.dma("sp", xbuf[s][0:nr, :], xin[r0:r0 + nr, :], xsem[s], tmax(xbuf_free[s]))
            a1 = k.op("act", lambda e: e.activation(out=xs[s][0:nr, :], in_=xbuf[s][0:nr, :], func=ACT.Square,
                                                    accum_out=ss[0:nr, i:i + 1]), tmax(ld, xs_free[s], cst, ss_zero))
            v1 = k.op("dve", lambda e: e.tensor_scalar(out=rstd[0:nr, i:i + 1], in0=ss[0:nr, i:i + 1], scalar1=1.0 / D,
                                                       scalar2=EPS, op0=ALU.mult, op1=ALU.add), [a1])
            a2 = k.op("act", lambda e: e.activation(out=rstd[0:nr, i:i + 1], in_=rstd[0:nr, i:i + 1], func=ACT.Sqrt), [v1])
            v2 = k.op("dve", lambda e: e.reciprocal(out=rstd[0:nr, i:i + 1], in_=rstd[0:nr, i:i + 1]), [a2])
            a3 = k.op("act", lambda e: e.activation(out=xs[s][0:nr, :], in_=xbuf[s][0:nr, :], func=ACT.Copy,
                                                    scale=rstd[0:nr, i:i + 1]), [v2])
            xbuf_free[s] = a3
            p = None
            for g in range(4):
                bk = g % 2
                for q in range(4):
                    kc = 4 * g + q
                    p = k.op("pe", lambda e: e.transpose(out=ps[:, bk, q * 128:q * 128 + nr],
                                                         in_=xs[s][0:nr, kc * 128:(kc + 1) * 128], identity=ident[0:nr, 0:nr]),
                             tmax(a3, bank_free[bk]), inc=(q == 3))
                src = ps[:, bk, :].rearrange("p (a c) -> p a c", a=4)[:, :, 0:nr]
                bank_free[bk] = k.op("dve", lambda e: e.tensor_tensor(
                    out=xn[:, 4 * g:4 * g + 4, r0:r0 + nr], in0=src,
                    in1=nw[:, 4 * g:4 * g + 4].unsqueeze(2).to_broadcast([128, 4, nr]), op=ALU.mult), [p])
                last_xn = bank_free[bk]
            xs_free[s] = p

        ar.top = top_s2
        cvb = [ar.alloc([128, T]) for _ in range(2)]
        bsb = [ar.alloc([128, T]) for _ in range(2)]
        szb = [ar.alloc([128, T]) for _ in range(2)]
        acc = [ar.alloc([128, T]) for _ in range(2)]
        csb = [ar.alloc([128, 366]) for _ in range(2)]
        assert ar.top <= AR_WORDS - 2 * D
        k.op("dve", lambda e: e.memset(y0T[:, :, 0:2], 0.0))
        wc_free = [None, None]
        blk_free = [last_xn, last_xn]
        csb_free = [last_xn, last_xn]
        step = 0
        last_pe = None
        y0_done = None
        for eb in range(32):
            s = eb % 2
            wl = k.dma("pool", wc[s].rearrange("p a b -> p (a b)"), wci[eb], wsem[s], tmax(wc_free[s]), max_dma_last_dim=8192)
            evs = []
            for (c0, c1) in RANGES:
                n = c1 - c0
                st_ = step % 2
                step += 1
                p = None
                for q in range(4):
                    bk = st_ * 4 + q
                    for kc in range(16):
                        p = k.op("pe", lambda e: e.matmul(ps[:, bk, 0:n], lhsT=wc[s][:, kc, q * 128:(q + 1) * 128],
                                                          rhs=xn[:, kc, c0:c1], start=(kc == 0), stop=(kc == 15)),
                                 tmax(wl, last_xn, bank_free[bk]), inc=(q == 3 and kc == 15))
                last_pe = p
                b0 = st_ * 4
                a_c = k.op("act", lambda e: e.copy(out=csb[st_][:, 0:n], in_=ps[:, b0 + 1, 0:n]), tmax(p, csb_free[st_]))
                v_cv = k.op("dve", lambda e: e.tensor_tensor(out=cvb[s][:, c0:c1], in0=ps[:, b0 + 2, 0:n], in1=csb[st_][:, 0:n],
                                                            op=ALU.mult), tmax(p, a_c, blk_free[s]))
                csb_free[st_] = v_cv
                a_b = k.op("act", lambda e: e.copy(out=bsb[s][:, c0:c1], in_=ps[:, b0 + 0, 0:n]), tmax(blk_free[s]))
                a_z = k.op("act", lambda e: e.activation(out=szb[s][:, c0:c1], in_=ps[:, b0 + 3, 0:n], func=ACT.Silu))
                for q in range(4):
                    bank_free[b0 + q] = tmax(a_z, v_cv)
                evs = tmax(a_z, v_cv)
            wc_free[s] = last_pe
            k.op("dve", lambda e: e.tensor_copy(out=cvb[s][:, C_A0 - 2:C_A0], in_=cc[:, eb, 0, :]), evs)
            v = k.op("dve", lambda e: e.tensor_copy(out=cvb[s][:, C_B0 - 2:C_B0], in_=cc[:, eb, 1, :]))
            a = k.op("act", lambda e: e.activation(out=acc[s][:, 2:T], in_=cvb[s][:, 2:T], func=ACT.Identity,
                                                   scale=cw[:, eb, 2:3], bias=cw[:, eb, 3:4]), tmax(v, evs))
            v = k.op("dve", lambda e: e.scalar_tensor_tensor(out=acc[s][:, 2:T], in0=cvb[s][:, 1:T - 1], scalar=cw[:, eb, 1:2],
                                                            in1=acc[s][:, 2:T], op0=ALU.mult, op1=ALU.add), [a])
            v = k.op("dve", lambda e: e.scalar_tensor_tensor(out=acc[s][:, 2:T], in0=cvb[s][:, 0:T - 2], scalar=cw[:, eb, 0:1],
                                                            in1=acc[s][:, 2:T], op0=ALU.mult, op1=ALU.add))
            v = k.op("dve", lambda e: e.tensor_tensor(out=acc[s][:, 2:T], in0=acc[s][:, 2:T], in1=bsb[s][:, 2:T], op=ALU.mult), evs)
            v = k.op("dve", lambda e: e.tensor_tensor(out=y0T[:, eb, 2:T], in0=acc[s][:, 2:T], in1=szb[s][:, 2:T], op=ALU.mult))
            src = cvb[s][:, T - 102:T].rearrange("p (a b) -> p a b", b=34)[:, :, 32:34]
            v = k.op("dve", lambda e: e.tensor_copy(out=ncv[:, eb, :].rearrange("p (a b) -> p a b", b=2), in_=src))
            blk_free[s] = v
            y0_done = v
        s2_pe_done = last_pe
        st_ncv = k.dsem()
        out_ncv = k.dma("sp", ncv_o, ncv.rearrange("p a b -> p (a b)"), st_ncv, [y0_done])

        ar.top = top_h1
        h1T = ar.alloc([128, 16, T])
        wo = [ar.alloc([128, 32, 128], BF16) for _ in range(2)]
        assert ar.top <= AR_WORDS - 2 * D
        guard = tmax(s2_pe_done, y0_done)
        xbuf_free = [guard, guard]
        last_fill = None
        for i in range(NTT):
            r0 = 128 * i
            nr = min(128, T - r0)
            s = i % 2
            ld = k.dma("sp", xbuf[s][0:nr, :], xin[r0:r0 + nr, :], xsem[s], tmax(xbuf_free[s]))
            p = None
            for g in range(4):
                bk = g % 2
                for q in range(4):
                    kc = 4 * g + q
                    p = k.op("pe", lambda e: e.transpose(out=ps[:, bk, q * 128:q * 128 + nr],
                                                         in_=xbuf[s][0:nr, kc * 128:(kc + 1) * 128], identity=ident[0:nr, 0:nr]),
                             tmax(ld, bank_free[bk], guard), inc=(q == 3))
                src = ps[:, bk, :].rearrange("p (a c) -> p a c", a=4)[:, :, 0:nr]
                bank_free[bk] = k.op("act", lambda e: e.copy(out=h1T[:, 4 * g:4 * g + 4, r0:r0 + nr], in_=src), tmax(p, guard))
                last_fill = bank_free[bk]
            xbuf_free[s] = p
        wo_free = [guard, guard]
        hsem = k.dsem()
        last_h = None
        for db in range(16):
            s = db % 2
            wl = k.dma("pool", wo[s].rearrange("p a b -> p (a b)"), wco[db], wsem[s], tmax(wo_free[s]), max_dma_last_dim=8192)
            p = None
            for ri, (c0, c1) in enumerate(RANGES):
                n = c1 - c0
                bk = 2 + (db % 2) * 3 + ri
                for eb in range(32):
                    p = k.op("pe", lambda e: e.matmul(ps[:, bk, 0:n], lhsT=wo[s][:, eb, :], rhs=y0T[:, eb, c0:c1],
                                                      start=(eb == 0), stop=(eb == 31)),
                             tmax(wl, y0_done, bank_free[bk]), inc=(eb == 31))
                bank_free[bk] = k.op("dve", lambda e: e.tensor_tensor(out=h1T[:, db, c0:c1], in0=ps[:, bk, 0:n], in1=h1T[:, db, c0:c1],
                                                                     op=ALU.add), tmax(p, last_fill))
                last_h = bank_free[bk]
            wo_free[s] = p
            k.dma("sp", h1T_o[:, db * T:(db + 1) * T], h1T[:, db, :], hsem, [last_h])
        h_out_tok = ("d", hsem, hsem.val)
        if stop_after <= 3:
            k.wait("sp", [h_out_tok, out_ncv])
            return nc

        ar.top = 0
        xn1 = ar.alloc([128, 16, T], BF16)
        sq = [ar.alloc([128, T]) for _ in range(2)]
        rb = ar.alloc([128, T])
        ones_d = ar.alloc([128, 128])
        assert ar.top <= top_h1
        k.op("dve", lambda e: e.memset(ones_d, 1.0 / D), [last_h])
        vm = k.op("dve", lambda e: e.memset(rb, 0.0))
        sq_free = [last_h, last_h]
        p = None
        for db in range(16):
            s = db % 2
            a = k.op("act", lambda e: e.activation(out=sq[s], in_=h1T[:, db, :], func=ACT.Square), tmax(last_h, sq_free[s]))
            for ri, (c0, c1) in enumerate(RANGES):
                n = c1 - c0
                p = k.op("pe", lambda e: e.matmul(ps[:, 2 + ri, 0:n], lhsT=ones_d, rhs=sq[s][:, c0:c1], start=(db == 0), stop=(db == 15)),
                         tmax(a, vm, bank_free[2 + ri]), inc=(ri == 2))
            sq_free[s] = p
        v = None
        for ri, (c0, c1) in enumerate(RANGES):
            n = c1 - c0
            v = k.op("dve", lambda e: e.tensor_scalar(out=rb[:, c0:c1], in0=ps[:, 2 + ri, 0:n], scalar1=EPS, scalar2=None, op0=ALU.add), [p])
            bank_free[2 + ri] = v
        a = k.op("act", lambda e: e.activation(out=rb, in_=rb, func=ACT.Sqrt), [v])
        v = k.op("dve", lambda e: e.reciprocal(out=rb, in_=rb), [a])
        for db in range(16):
            v = k.op("dve", lambda e: e.scalar_tensor_tensor(out=xn1[:, db, :], in0=h1T[:, db, :], scalar=nw[:, 16 + db:17 + db],
                                                            in1=rb, op0=ALU.mult, op1=ALU.mult))
        xn1_done = v

        ar.top = 8784
        are, aim, ldt = [ar.alloc([128, NPAIR]) for _ in range(3)]
        bre = ar.alloc([128, NPAIR * 16])
        bim = ar.alloc([128, NPAIR * 16])
        iot = ar.alloc([128, T])
        onesr = ar.alloc([128, T])
        ge_re = ar.alloc([128, NPAIR])
        ge_im = ar.alloc([128, NPAIR])
        f0 = ar.alloc([128, NPAIR, 2])
        c2 = k.dsem()
        gw = tmax(xn1_done, h_out_tok)
        k.dma("sp", are, are_d, c2, gw)
        k.dma("sp", aim, aim_d, c2)
        k.dma("sp", ldt, ldt_d, c2)
        k.dma("sp", bre, bre_d, c2)
        k.dma("sp", iot, iota_d, c2)
        c2t = k.dma("sp", bim, bim_d, c2)
        k.op("dve", lambda e: e.memset(onesr, 1.0), gw)
        cons = emit_ssm_consts(k, ar, ps, ident, are, aim, ldt, bre, bim, tmax(c2t, gw), bank_free, [SEG, 4], npow=4)
        ar.top = cons["keep_top"]
        cN, sN, rN, rot_tok = cons["rots"][SEG]
        _, _, rho4, rot4_tok = cons["rots"][4]
        kap4 = ar.alloc([128, NPAIR])
        k4_tok = k.op("dve", lambda e: e.tensor_scalar(out=kap4, in0=cons["kap"], scalar1=4.0, scalar2=None, op0=ALU.mult), tmax(cons["tok"]))

        NCH = SEG // 4
        wu = [ar.alloc([128, 16, 128], BF16) for _ in range(2)]
        uT = [ar.alloc([128, T], BF16) for _ in range(2)]
        pbs = [PairBufs(ar, NCH) for _ in range(2)]
        PC0, PC1 = C_P0, C_P0 + SEG
        pre_tok = tmax(cons["tok"], k4_tok, rot4_tok)
        wu_free = [pre_tok, pre_tok]
        uT_free = [pre_tok, pre_tok]
        step = 0
        last_copy = None
        for b in range(32):
            s = b % 2
            wl = k.dma("pool", wu[s].rearrange("p a b -> p (a b)"), wsu[b], wsem[s], tmax(wu_free[s], gw), max_dma_last_dim=8192)
            p = None
            a_u = None
            for ri, (c0, c1) in enumerate(RANGES):
                n = c1 - c0
                bk = 5 + ri
                for kc in range(16):
                    p = k.op("pe", lambda e: e.matmul(ps[:, bk, 0:n], lhsT=wu[s][:, kc, :], rhs=xn1[:, kc, c0:c1],
                                                      start=(kc == 0), stop=(kc == 15)),
                             tmax(wl, xn1_done, bank_free[bk]), inc=(kc == 15))
                a_u = k.op("act", lambda e: e.copy(out=uT[s][:, c0:c1], in_=ps[:, bk, 0:n]), tmax(p, uT_free[s]))
                bank_free[bk] = a_u
            wu_free[s] = p
            last_bu_pe = None
            for j4 in range(4):
                pair = 4 * b + j4
                pb = pbs[pair % 2]
                bkr = (step % 2) * 2
                step += 1
                pi_ = None
                for (bft, bk_) in ((cons["bft_re"], bkr), (cons["bft_im"], bkr + 1)):
                    for r in range(4):
                        rhs = uT[s][32 * j4:32 * j4 + 32, PC0 + r:PC0 + r + 4 * (NCH - 1) + 1:4]
                        pi_ = k.op("pe", lambda e: e.matmul(ps[:, bk_, 0:NCH], lhsT=bft[3 - r][32 * j4:32 * j4 + 32, b, :], rhs=rhs,
                                                            start=(r == 0), stop=(r == 3), tile_position=(32 * j4, 0)),
                                   tmax(a_u, bank_free[bk_], pre_tok), inc=(r == 3))
                last_bu_pe = pi_
                k.op("act", lambda e: e.copy(out=pb.bu_re, in_=ps[:, bkr, 0:NCH]), tmax(pi_, pb.t_prerot))
                a_e = k.op("act", lambda e: e.copy(out=pb.bu_im, in_=ps[:, bkr + 1, 0:NCH]))
                bank_free[bkr] = a_e
                bank_free[bkr + 1] = a_e
                kcol = kap4[:, pair:pair + 1]
                v = k.op("dve", lambda e: e.tensor_scalar(out=pb.ph, in0=iot[:, PC0:PC0 + NCH], scalar1=kcol, scalar2=None, op0=ALU.mult),
                         tmax(pb.t_tabact, pre_tok))
                v = k.op("dve", lambda e: e.tensor_scalar(out=pb.rn, in0=pb.ph, scalar1=MAGIC, scalar2=MAGIC, op0=ALU.add, op1=ALU.subtract))
                v = k.op("dve", lambda e: e.tensor_tensor(out=pb.fr, in0=pb.ph, in1=pb.rn, op=ALU.subtract))
                a = k.op("act", lambda e: e.activation(out=pb.sn, in_=pb.fr, func=ACT.Sin, scale=TWO_PI), tmax(v, pb.t_prerot))
                a = k.op("act", lambda e: e.activation(out=pb.ab, in_=pb.fr, func=ACT.Abs))
                a_t = k.op("act", lambda e: e.activation(out=pb.cs, in_=pb.ab, func=ACT.Sin, scale=-TWO_PI, bias=float(np.pi / 2)))
                pb.t_tabact = a_t
                a_r = k.op("act", lambda e: e.activation(out=pb.rrow, in_=onesr[:, 0:NCH], func=ACT.Copy,
                                                         scale=rho4[:, pair:pair + 1]), tmax(pb.t_scan))
                t1, t2 = pb.ph, pb.rn
                v = k.op("dve", lambda e: e.tensor_tensor(out=t1, in0=pb.cs, in1=pb.bu_re, op=ALU.mult), tmax(a_t, a_e))
                v = k.op("dve", lambda e: e.tensor_tensor(out=t2, in0=pb.sn, in1=pb.bu_im, op=ALU.mult))
                v = k.op("dve", lambda e: e.tensor_tensor(out=pb.zre, in0=t1, in1=t2, op=ALU.add))
                v = k.op("dve", lambda e: e.tensor_tensor(out=t1, in0=pb.cs, in1=pb.bu_im, op=ALU.mult))
                v = k.op("dve", lambda e: e.tensor_tensor(out=t2, in0=pb.sn, in1=pb.bu_re, op=ALU.mult))
                v = k.op("dve", lambda e: e.tensor_tensor(out=pb.zim, in0=t1, in1=t2, op=ALU.subtract))
                pb.t_prerot = v
                v = k.op("dve", lambda e: e.tensor_tensor_scan(out=pb.gre, data0=pb.rrow, data1=pb.zre, initial=0.0,
                                                               op0=ALU.mult, op1=ALU.add), tmax(a_r, pb.t_gread))
                v = k.op("dve", lambda e: e.tensor_tensor_scan(out=pb.gim, data0=pb.rrow, data1=pb.zim, initial=0.0,
                                                               op0=ALU.mult, op1=ALU.add))
                pb.t_scan = v
                k.op("act", lambda e: e.copy(out=ge_re[:, pair:pair + 1], in_=pb.gre[:, NCH - 1:NCH]), [v])
                a = k.op("act", lambda e: e.copy(out=ge_im[:, pair:pair + 1], in_=pb.gim[:, NCH - 1:NCH]))
                pb.t_gread = a
                last_copy = a
            uT_free[s] = last_bu_pe
        ta, tb_ = ar.alloc([128, NPAIR]), ar.alloc([128, NPAIR])
        v = k.op("dve", lambda e: e.tensor_tensor(out=ta, in0=cN, in1=ge_re, op=ALU.mult), tmax(last_copy, rot_tok))
        v = k.op("dve", lambda e: e.tensor_tensor(out=tb_, in0=sN, in1=ge_im, op=ALU.mult))
        v = k.op("dve", lambda e: e.tensor_tensor(out=f0[:, :, 0], in0=ta, in1=tb_, op=ALU.subtract))
        v = k.op("dve", lambda e: e.tensor_tensor(out=ta, in0=sN, in1=ge_re, op=ALU.mult))
        v = k.op("dve", lambda e: e.tensor_tensor(out=tb_, in0=cN, in1=ge_im, op=ALU.mult))
        v = k.op("dve", lambda e: e.tensor_tensor(out=f0[:, :, 1], in0=ta, in1=tb_, op=ALU.add))
        fsem = k.dsem()
        ft = k.dma("sp", f0_o, f0.rearrange("p a b -> p (a b)"), fsem, [v])
        k.wait("sp", [h_out_tok, out_ncv, ft])
    return nc


AR_B = 52700
_DBG_NBLK = 32
GV = lambda g: g[:, T - 102:T].rearrange("p (a b) -> p a b", b=34)[:, :, 33]


def build_b():
    nc = bass.Bass("TRN2", target_bir_lowering=False)
    dr = lambda name, shape, kind="ExternalInput": nc.dram_tensor(name, shape, F32, kind=kind).ap()
    h1T_i = dr("h1T_i", [128, 16 * T])
    f0p_d = dr("f0prev", [128, 7 * NPAIR * 2])
    ident_d = dr("ident", [128, 128])
    nw_d = dr("nw", [128, 48])
    are_d = dr("are", [128, NPAIR])
    aim_d = dr("aim", [128, NPAIR])
    ldt_d = dr("ldt", [128, NPAIR])
    bre_d = dr("bre", [128, NPAIR * 16])
    bim_d = dr("bim", [128, NPAIR * 16])
    iotc_d = dr("iotc", [128, 273])
    cbr_d = dr("cbr", [128, NPAIR * 32])
    cbi_d = dr("cbi", [128, NPAIR * 32])
    dv_d = dr("dvec", [128, 32])
    bg_d = dr("bgl", [128, 32])
    h0r_d = dr("h0r", [128, NPAIR * 2])
    h0i_d = dr("h0i", [128, NPAIR * 2])
    wsu = dr("wsu", [32, 128, 16 * 128])
    wsz = dr("wsz", [32, 128, 16 * 128])
    wgl = dr("wgl", [32, 128, 32 * 128])
    wso = dr("wso", [16, 128, 32 * 128])
    y_o = dr("yout", [T, D], "ExternalOutput")
    scr_bft = nc.dram_tensor("scr_bft", [8, 128, 32 * 128], BF16, kind="Internal").ap()
    scr_ck = nc.dram_tensor("scr_ck", [10, 128, NPAIR * 32], BF16, kind="Internal").ap()
    hf_o = dr("hfin", [128, NPAIR * 6], "ExternalOutput")

    with contextlib.ExitStack() as es:
        k = K(nc, es)
        arena_t = es.enter_context(nc.sbuf_tensor("arena", [128, AR_B], F32))
        ar = Arena(arena_t[:], AR_B)
        ps = es.enter_context(nc.psum_tensor("ps", [128, 8, 512], F32))
        small_t = es.enter_context(nc.sbuf_tensor("small", [128, 256], F32))
        sm = Arena(small_t[:], 256)
        ident = sm.alloc([128, 128])
        nw = sm.alloc([128, 48])
        dvec = sm.alloc([128, 32])
        bgl = sm.alloc([128, 32])
        csem = k.dsem()
        k.dma("sp", ident, ident_d, csem)
        k.dma("sp", nw, nw_d, csem)
        k.dma("sp", dvec, dv_d, csem)
        cst = k.dma("sp", bgl, bg_d, csem)
        wsem = [k.dsem(), k.dsem(), k.dsem(), k.dsem()]
        bank_free = [None] * 8

        xn1 = ar.alloc([128, 16, T], BF16)
        g_all = ar.alloc([128, 32, T], BF16)
        BASE = ar.top
        h1T = ar.alloc([128, 16, T])
        sq = [ar.alloc([128, T]) for _ in range(2)]
        rb = ar.alloc([128, T])
        ones_d = ar.alloc([128, 128])
        hsem = k.dsem()
        for db in range(16):
            k.dma("sp", h1T[:, db, :], h1T_i[:, db * T:(db + 1) * T], hsem)
        h_in = ("d", hsem, hsem.val)
        vm = k.op("dve", lambda e: e.memset(ones_d, 1.0 / D))
        sq_free = [None, None]
        p = None
        for db in range(16):
            s = db % 2
            a = k.op("act", lambda e: e.activation(out=sq[s], in_=h1T[:, db, :], func=ACT.Square), tmax(h_in, sq_free[s]))
            for ri, (c0, c1) in enumerate(RANGES):
                n = c1 - c0
                p = k.op("pe", lambda e: e.matmul(ps[:, 2 + ri, 0:n], lhsT=ones_d, rhs=sq[s][:, c0:c1], start=(db == 0), stop=(db == 15)),
                         tmax(a, vm), inc=(ri == 2))
            sq_free[s] = p
        v = None
        for ri, (c0, c1) in enumerate(RANGES):
            n = c1 - c0
            v = k.op("dve", lambda e: e.tensor_scalar(out=rb[:, c0:c1], in0=ps[:, 2 + ri, 0:n], scalar1=EPS, scalar2=None, op0=ALU.add), [p])
            bank_free[2 + ri] = v
        a = k.op("act", lambda e: e.activation(out=rb, in_=rb, func=ACT.Sqrt), [v])
        v = k.op("dve", lambda e: e.reciprocal(out=rb, in_=rb), [a])
        for db in range(16):
            v = k.op("dve", lambda e: e.scalar_tensor_tensor(out=xn1[:, db, :], in0=h1T[:, db, :], scalar=nw[:, 16 + db:17 + db],
                                                            in1=rb, op0=ALU.mult, op1=ALU.mult), [cst, h_in])
        xn1_done = v

        NC = 273
        SEGS = [(0, 8, C_A0), (8, 8, C_B0), (16, 257, C_P0)]
        ar.top = AR_B - (3 * NPAIR + 2 * NPAIR * 16 + 7 * NPAIR * 2)
        tr_base = ar.top
        are, aim, ldt = [ar.alloc([128, NPAIR]) for _ in range(3)]
        bre = ar.alloc([128, NPAIR * 16])
        bim = ar.alloc([128, NPAIR * 16])
        f0p = ar.alloc([128, 7, NPAIR, 2])
        ar.top = BASE
        c2 = k.dsem()
        gw = tmax(xn1_done)
        k.dma("sp", are, are_d, c2, gw)
        k.dma("sp", aim, aim_d, c2)
        k.dma("sp", ldt, ldt_d, c2)
        k.dma("sp", bre, bre_d, c2)
        k.dma("sp", bim, bim_d, c2)
        c2t = k.dma("sp", f0p.rearrange("p a b c -> p (a b c)"), f0p_d, c2)
        cons = emit_ssm_consts(k, ar, ps, ident, are, aim, ldt, bre, bim, tmax(c2t, gw, cst), bank_free, [SEG, 32, 4], npow=4,
                               dram_bft=scr_bft)
        assert ar.top <= tr_base, (ar.top, tr_base)
        ar.top = cons["keep_top"]
        cN, sN, rN, rtokN = cons["rots"][SEG]
        c32, s32, _, rtok32 = cons["rots"][32]
        _, _, rho4, rtok4 = cons["rots"][4]
        iotc = ar.alloc([128, NC])
        h0r = ar.alloc([128, NPAIR, 2])
        h0i = ar.alloc([128, NPAIR, 2])
        ge_re = ar.alloc([128, NPAIR, 3])
        ge_im = ar.alloc([128, NPAIR, 3])
        hi_re = ar.alloc([128, NPAIR, 3])
        hi_im = ar.alloc([128, NPAIR, 3])
        hfin = ar.alloc([128, NPAIR, 3, 2])
        hin_re, hin_im, anr, ani, w1_, w2_, kap4 = [ar.alloc([128, NPAIR]) for _ in range(7)]
        p3_top = ar.top
        c4s = k.dsem()
        k.dma("sp", iotc, iotc_d, c4s, tmax(cons["tok"]))
        k.dma("sp", h0r.rearrange("p a b -> p (a b)"), h0r_d, c4s)
        c4t = k.dma("sp", h0i.rearrange("p a b -> p (a b)"), h0i_d, c4s)
        v = k.op("dve", lambda e: e.tensor_scalar(out=kap4, in0=cons["kap"], scalar1=4.0, scalar2=None, op0=ALU.mult), tmax(cons["tok"]))
        v = k.op("dve", lambda e: e.tensor_tensor(out=anr, in0=rN, in1=cN, op=ALU.mult), tmax(rtokN, rtok32, rtok4, c2t, cons["tok"]))
        v = k.op("dve", lambda e: e.tensor_tensor(out=ani, in0=rN, in1=sN, op=ALU.mult))
        v = k.op("dve", lambda e: e.tensor_copy(out=hin_re, in_=f0p[:, 6, :, 0]))
        v = k.op("dve", lambda e: e.tensor_copy(out=hin_im, in_=f0p[:, 6, :, 1]))
        for j in range(5, -1, -1):
            k.op("dve", lambda e: e.tensor_tensor(out=w1_, in0=anr, in1=hin_re, op=ALU.mult))
            k.op("dve", lambda e: e.tensor_tensor(out=w2_, in0=ani, in1=hin_im, op=ALU.mult))
            k.op("dve", lambda e: e.tensor_tensor(out=w1_, in0=w1_, in1=w2_, op=ALU.subtract))
            k.op("dve", lambda e: e.tensor_tensor(out=w2_, in0=ani, in1=hin_re, op=ALU.mult))
            k.op("dve", lambda e: e.tensor_tensor(out=hin_re, in0=w1_, in1=f0p[:, j, :, 0], op=ALU.add))
            k.op("dve", lambda e: e.tensor_tensor(out=w1_, in0=anr, in1=hin_im, op=ALU.mult))
            k.op("dve", lambda e: e.tensor_tensor(out=w1_, in0=w1_, in1=w2_, op=ALU.add))
            v = k.op("dve", lambda e: e.tensor_tensor(out=hin_im, in0=w1_, in1=f0p[:, j, :, 1], op=ALU.add))
        k.op("dve", lambda e: e.tensor_copy(out=hi_re[:, :, 0:2], in_=h0r), [c4t])
        k.op("dve", lambda e: e.tensor_copy(out=hi_im[:, :, 0:2], in_=h0i))
        k.op("dve", lambda e: e.tensor_copy(out=hi_re[:, :, 2], in_=hin_re))
        carry_tok = k.op("dve", lambda e: e.tensor_copy(out=hi_im[:, :, 2], in_=hin_im))
        pws = [(None, None)]
        for kk in range(1, 5):
            pr_, pi_2 = ar.alloc([128, NPAIR]), ar.alloc([128, NPAIR])
            if kk == 1:
                k.op("dve", lambda e: e.tensor_copy(out=pr_, in_=cons["ab_re"]))
                k.op("dve", lambda e: e.tensor_copy(out=pi_2, in_=cons["ab_im"]))
            else:
                qr, qi = pws[kk - 1]
                k.op("dve", lambda e: e.tensor_tensor(out=w1_, in0=qr, in1=cons["ab_re"], op=ALU.mult))
                k.op("dve", lambda e: e.tensor_tensor(out=w2_, in0=qi, in1=cons["ab_im"], op=ALU.mult))
                k.op("dve", lambda e: e.tensor_tensor(out=pr_, in0=w1_, in1=w2_, op=ALU.subtract))
                k.op("dve", lambda e: e.tensor_tensor(out=w1_, in0=qr, in1=cons["ab_im"], op=ALU.mult))
                k.op("dve", lambda e: e.tensor_tensor(out=w2_, in0=qi, in1=cons["ab_re"], op=ALU.mult))
                k.op("dve", lambda e: e.tensor_tensor(out=pi_2, in0=w1_, in1=w2_, op=ALU.add))
            pws.append((pr_, pi_2))
        HP = NPAIR // 2
        csm_re = ar.alloc([128, HP, 32])
        csm_im = ar.alloc([128, HP, 32])
        ct1 = ar.alloc([128, HP, 32])
        ct2 = ar.alloc([128, HP, 32])
        cko = [ar.alloc([128, HP, 32], BF16) for _ in range(2)]
        assert ar.top <= tr_base, (ar.top, tr_base)
        c5 = k.dsem()
        cks = [k.dsem(), k.dsem()]
        ck_tok = [None, None]
        v = None
        for half in range(2):
            hs_ = slice(half * HP * 32, (half + 1) * HP * 32)
            k.dma("sp", csm_re.rearrange("p a b -> p (a b)"), cbr_d[:, hs_], c5, tmax(cons["tok"], v))
            c5t = k.dma("sp", csm_im.rearrange("p a b -> p (a b)"), cbi_d[:, hs_], c5)
            for kk in range(5):
                if kk == 0:
                    k.op("dve", lambda e: e.tensor_copy(out=cko[0], in_=csm_re), tmax(c5t, ck_tok[0]))
                    v = k.op("dve", lambda e: e.tensor_copy(out=cko[1], in_=csm_im), tmax(ck_tok[1]))
                else:
                    pr_, pi_2 = pws[kk]
                    prb = pr_[:, half * HP:(half + 1) * HP].unsqueeze(2).to_broadcast([128, HP, 32])
                    pib = pi_2[:, half * HP:(half + 1) * HP].unsqueeze(2).to_broadcast([128, HP, 32])
                    k.op("dve", lambda e: e.tensor_tensor(out=ct1, in0=csm_re, in1=prb, op=ALU.mult), [c5t])
                    k.op("dve", lambda e: e.tensor_tensor(out=ct2, in0=csm_im, in1=pib, op=ALU.mult))
                    k.op("dve", lambda e: e.tensor_tensor(out=cko[0], in0=ct1, in1=ct2, op=ALU.subtract), tmax(ck_tok[0]))
                    k.op("dve", lambda e: e.tensor_tensor(out=ct1, in0=csm_re, in1=pib, op=ALU.mult))
                    k.op("dve", lambda e: e.tensor_tensor(out=ct2, in0=csm_im, in1=prb, op=ALU.mult))
                    v = k.op("dve", lambda e: e.tensor_tensor(out=cko[1], in0=ct1, in1=ct2, op=ALU.add), tmax(ck_tok[1]))
                ck_tok[0] = k.dma("sp", scr_ck[2 * kk][:, hs_], cko[0].rearrange("p a b -> p (a b)"), cks[0], [v])
                ck_tok[1] = k.dma("sp", scr_ck[2 * kk + 1][:, hs_], cko[1].rearrange("p a b -> p (a b)"), cks[1])
        scr_done = tmax(ck_tok[0], ck_tok[1], cons["tok"], v)

        ar.top = p3_top
        wus = [ar.alloc([128, 16, 128], BF16) for _ in range(2)]
        uTs = [ar.alloc([128, T], BF16) for _ in range(2)]
        bfb = [ar.alloc([128, 8, 128], BF16) for _ in range(2)]
        ckb = [ar.alloc([128, 10, 128], BF16) for _ in range(2)]
        A_ = lambda: ar.alloc([128, NC])
        all_tok = tmax(scr_done, carry_tok, c4t)

        class PB:
            pass
        lsets = []
        for _ in range(4):
            q = PB()
            q.Lb_re = [ar.alloc([128, NC], BF16) for _ in range(4)]
            q.Lb_n = [ar.alloc([128, NC], BF16) for _ in range(4)]
            q.bu_re, q.bu_im = A_(), A_()
            q.He_re = ar.alloc([128, NC + 1], BF16)
            q.He_n = ar.alloc([128, NC + 1], BF16)
            q.t_yread = all_tok
            q.t_dve = all_tok
            q.t_gcopy = all_tok
            lsets.append(q)
        ssets = []
        for _ in range(2):
            q = PB()
            q.ph, q.rn, q.fr, q.sn, q.cs, q.rrow = [A_() for _ in range(6)]
            q.t_dve = all_tok
            q.t_gcopy = all_tok
            ssets.append(q)
        tmpys = [ar.alloc([128, NC]) for _ in range(2)]
        wu_free = [all_tok, all_tok]
        uT_free = [all_tok, all_tok]
        blk_sem = [k.dsem(), k.dsem(), k.dsem(), k.dsem()]
        blk_free = [all_tok, all_tok]
        yb_free = [bank_free[0], bank_free[1], bank_free[2], bank_free[3]]
        lset = [(5, 6), (7, 4)]
        lstep = 0
        g_done = None
        BS = {}
        LST = {"lstep": 0}
        EPI = {"a": [None, None]}

        def uproj(b):
            sb_ = b % 2
            st = {}
            st["lb1"] = k.dma("sp", bfb[sb_], scr_bft[:, :, b * 128:(b + 1) * 128].rearrange("k p c -> p k c"), blk_sem[sb_], tmax(blk_free[sb_]))
            st["lb2"] = k.dma("sp", ckb[sb_], scr_ck[:, :, b * 128:(b + 1) * 128].rearrange("k p c -> p k c"), blk_sem[2 + sb_])
            wu = wus[sb_]
            uT = uTs[sb_]
            wl = k.dma("pool", wu.rearrange("p a b -> p (a b)"), wsu[b], wsem[sb_], tmax(wu_free[sb_]), max_dma_last_dim=8192)
            p = None
            a_u = None
            for ri, (c0, c1) in enumerate(RANGES):
                n = c1 - c0
                bk = 5 + ri
                for kc in range(16):
                    p = k.op("pe", lambda e: e.matmul(ps[:, bk, 0:n], lhsT=wu[:, kc, :], rhs=xn1[:, kc, c0:c1],
                                                      start=(kc == 0), stop=(kc == 15)),
                             tmax(wl, xn1_done, bank_free[bk]), inc=(kc == 15))
                a_u = k.op("act", lambda e: e.copy(out=uT[:, c0:c1], in_=ps[:, bk, 0:n]), tmax(p, uT_free[sb_]))
                bank_free[bk] = a_u
            wu_free[sb_] = p
            st["a_u"] = a_u
            st["last_y"] = None
            st["a_e"] = {}
            st["tab"] = {}
            BS[b] = st

        LBK = [5, 6, 7, 4]
        SEGS2 = [(0, 16), (16, 257)]

        def seg_rhs(uT, rows, si, rp):
            if si == 0:
                return uT[rows, T - 68:T].rearrange("p (a b) -> p a b", b=34)[:, :, 2 + rp:2 + rp + 29:4]
            return uT[rows, C_P0 + rp:C_P0 + rp + 4 * 256 + 1:4]

        def seg_out(bank, si):
            if si == 0:
                return ps[:, bank, 0:16].rearrange("p (a b) -> p a b", a=2)
            return ps[:, bank, 16:NC]

        def lphase(b):
            sb_ = b % 2
            st = BS[b]
            uT = uTs[sb_]
            for r in range(4):
                for ri_ in (0, 1):
                    pl = None
                    for si in range(2):
                        for rp in range(r + 1):
                            for j4 in range(4):
                                rows = slice(32 * j4, 32 * j4 + 32)
                                pl = k.op("pe", lambda e: e.matmul(seg_out(LBK[j4], si), lhsT=bfb[sb_][rows, 2 * (r - rp) + ri_, :],
                                                                   rhs=seg_rhs(uT, rows, si, rp), start=(rp == 0), stop=(rp == r),
                                                                   tile_position=(32 * j4, 0)),
                                          tmax(st["a_u"], st["lb1"], bank_free[LBK[j4]]), inc=(si == 1 and rp == r and j4 == 3))
                    for j4 in range(4):
                        q = lsets[j4]
                        if ri_ == 0:
                            a = k.op("act", lambda e: e.copy(out=q.Lb_re[r], in_=ps[:, LBK[j4], 0:NC]), tmax(pl, q.t_yread))
                            if r == 3:
                                a = k.op("act", lambda e: e.copy(out=q.bu_re, in_=ps[:, LBK[j4], 0:NC]), tmax(q.t_dve))
                        else:
                            a = k.op("act", lambda e: e.activation(out=q.Lb_n[r], in_=ps[:, LBK[j4], 0:NC], func=ACT.Copy, scale=-1.0),
                                     tmax(pl, q.t_yread))
                            if r == 3:
                                a = k.op("act", lambda e: e.copy(out=q.bu_im, in_=ps[:, LBK[j4], 0:NC]), tmax(q.t_dve, q.t_gcopy))
                        bank_free[LBK[j4]] = a
                        st["a_e"][j4] = a

        def tables(b, j4):
            st = BS[b]
            pair = 4 * b + j4
            s_ = ssets[j4 % 2]
            ph, rn, fr, sn, cs, rrow = s_.ph, s_.rn, s_.fr, s_.sn, s_.cs, s_.rrow
            kcol = kap4[:, pair:pair + 1]
            v = k.op("dve", lambda e: e.tensor_scalar(out=ph, in0=iotc, scalar1=kcol, scalar2=None, op0=ALU.mult), tmax(s_.t_gcopy))
            v = k.op("dve", lambda e: e.tensor_scalar(out=rn, in0=ph, scalar1=MAGIC, scalar2=MAGIC, op0=ALU.add, op1=ALU.subtract))
            v = k.op("dve", lambda e: e.tensor_tensor(out=fr, in0=ph, in1=rn, op=ALU.subtract))
            a = k.op("act", lambda e: e.activation(out=sn, in_=fr, func=ACT.Sin, scale=TWO_PI), tmax(v, s_.t_dve))
            a = k.op("act", lambda e: e.activation(out=rn, in_=fr, func=ACT.Abs))
            a_t = k.op("act", lambda e: e.activation(out=cs, in_=rn, func=ACT.Sin, scale=-TWO_PI, bias=float(np.pi / 2)))
            a_r = k.op("act", lambda e: e.activation(out=rrow, in_=iotc, func=ACT.Identity, scale=0.0, bias=rho4[:, pair:pair + 1]))
            st["tab"][j4] = tmax(a_t, a_r)

        def chain(b, j4):
            st = BS[b]
            pair = 4 * b + j4
            q = lsets[j4]
            s_ = ssets[j4 % 2]
            bu_re, bu_im, He_re, He_n = q.bu_re, q.bu_im, q.He_re, q.He_n
            ph, rn, fr, sn, cs, rrow = s_.ph, s_.rn, s_.fr, s_.sn, s_.cs, s_.rrow
            a_e = st["a_e"][j4]
            v = k.op("dve", lambda e: e.tensor_tensor(out=ph, in0=cs, in1=bu_re, op=ALU.mult), tmax(st["tab"][j4], a_e))
            v = k.op("dve", lambda e: e.tensor_tensor(out=rn, in0=sn, in1=bu_im, op=ALU.mult))
            v = k.op("dve", lambda e: e.tensor_tensor(out=fr, in0=ph, in1=rn, op=ALU.add))
            v = k.op("dve", lambda e: e.tensor_tensor(out=ph, in0=cs, in1=bu_im, op=ALU.mult))
            v = k.op("dve", lambda e: e.tensor_tensor(out=rn, in0=sn, in1=bu_re, op=ALU.mult))
            v = k.op("dve", lambda e: e.tensor_tensor(out=bu_re, in0=ph, in1=rn, op=ALU.subtract))
            for si, (cc0, ln, tk0) in enumerate(SEGS):
                k.op("dve", lambda e: e.tensor_tensor_scan(out=bu_im[:, cc0:cc0 + ln], data0=rrow[:, cc0:cc0 + ln], data1=fr[:, cc0:cc0 + ln],
                                                           initial=hi_re[:, pair, si:si + 1], op0=ALU.mult, op1=ALU.add))
                v = k.op("dve", lambda e: e.tensor_tensor_scan(out=ph[:, cc0:cc0 + ln], data0=rrow[:, cc0:cc0 + ln], data1=bu_re[:, cc0:cc0 + ln],
                                                               initial=hi_im[:, pair, si:si + 1], op0=ALU.mult, op1=ALU.add))
            gre, gim = bu_im, ph
            k.op("act", lambda e: e.copy(out=ge_re[:, pair, 0:2], in_=gre[:, 7:16:8]), [v])
            k.op("act", lambda e: e.copy(out=ge_re[:, pair, 2:3], in_=gre[:, NC - 1:NC]))
            k.op("act", lambda e: e.copy(out=ge_im[:, pair, 0:2], in_=gim[:, 7:16:8]))
            gc = k.op("act", lambda e: e.copy(out=ge_im[:, pair, 2:3], in_=gim[:, NC - 1:NC]))
            q.t_gcopy = gc
            s_.t_gcopy = gc
            v = k.op("dve", lambda e: e.tensor_tensor(out=rn, in0=cs, in1=gre, op=ALU.mult))
            v = k.op("dve", lambda e: e.tensor_tensor(out=fr, in0=sn, in1=gim, op=ALU.mult))
            v = k.op("dve", lambda e: e.tensor_tensor(out=He_re[:, 1:NC + 1], in0=rn, in1=fr, op=ALU.subtract), tmax(q.t_yread))
            v = k.op("dve", lambda e: e.tensor_tensor(out=rn, in0=sn, in1=gre, op=ALU.mult))
            v = k.op("dve", lambda e: e.tensor_tensor(out=fr, in0=cs, in1=gim, op=ALU.mult))
            v = k.op("dve", lambda e: e.scalar_tensor_tensor(out=He_n[:, 1:NC + 1], in0=rn, scalar=-1.0, in1=fr, op0=ALU.mult, op1=ALU.subtract))
            v = k.op("dve", lambda e: e.tensor_copy(out=He_re[:, 0:17:8], in_=hi_re[:, pair, :]))
            v = k.op("dve", lambda e: e.tensor_scalar(out=He_n[:, 0:17:8], in0=hi_im[:, pair, :], scalar1=-1.0, scalar2=None, op0=ALU.mult))
            q.t_dve = v
            s_.t_dve = v
            return v

        def yphase(b, chain_toks):
            sb_ = b % 2
            st = BS[b]
            py = None
            for r in range(4):
                for step in range(4):
                    for j4 in range(4):
                        rows = slice(32 * j4, 32 * j4 + 32)
                        q = lsets[j4]
                        yo = ps[rows, r, 0:NC]
                        lhs_k = (0, 1, 2 * (r + 1), 2 * (r + 1) + 1)[step]
                        rhs = (q.Lb_re[r], q.Lb_n[r], q.He_re[:, 0:NC], q.He_n[:, 0:NC])[step]
                        py = k.op("pe", lambda e: e.matmul(yo, lhsT=ckb[sb_][:, lhs_k, rows], rhs=rhs, start=(step == 0), stop=(step == 3),
                                                           tile_position=(0, 32 * j4)),
                                  tmax(chain_toks, st["a_e"][j4], st["lb2"], yb_free[r]), inc=(r == 3 and step == 3 and j4 == 3))
            for j4 in range(4):
                lsets[j4].t_yread = py
            st["last_y"] = py

        def epilogue(b):
            sb_ = b % 2
            st = BS[b]
            uT = uTs[sb_]
            last_y = st["last_y"]
            blk_free[sb_] = last_y
            v = None
            a = None
            for r in range(4):
                tmpy = tmpys[r % 2]
                for (cc0, ln, tk0) in SEGS:
                    tsl = slice(tk0 + r, tk0 + r + 4 * (ln - 1) + 1, 4)
                    v = k.op("dve", lambda e: e.scalar_tensor_tensor(out=tmpy[:, cc0:cc0 + ln], in0=uT[:, tsl], scalar=dvec[:, b:b + 1],
                                                                    in1=ps[:, r, cc0:cc0 + ln], op0=ALU.mult, op1=ALU.add),
                             tmax(last_y, EPI["a"][r % 2]))
                    a = k.op("act", lambda e: e.activation(out=g_all[:, b, tsl], in_=tmpy[:, cc0:cc0 + ln], func=ACT.Gelu_apprx_tanh), [v])
                EPI["a"][r % 2] = a
                yb_free[r] = v
            uT_free[sb_] = tmax(last_y, v)
            return a

        uproj(0)
        for b_ in range(_DBG_NBLK):
            tables(b_, 0)
            tables(b_, 1)
            lphase(b_)
            if b_ > 0:
                g_done = epilogue(b_ - 1)
            if b_ + 1 < _DBG_NBLK:
                uproj(b_ + 1)
            toks = [chain(b_, 0)]
            tables(b_, 2)
            toks.append(chain(b_, 1))
            tables(b_, 3)
            toks.append(chain(b_, 2))
            toks.append(chain(b_, 3))
            yphase(b_, tmax(*toks))
        g_done = epilogue(_DBG_NBLK - 1)
        for hc in (0, C_A0 - 2, C_B0 - 2):
            g_done = k.op("act", lambda e: e.activation(out=g_all[:, :, hc:hc + 2], in_=g_all[:, :, 2:4], func=ACT.Copy, scale=0.0), [g_done])
        v = None
        for si, (cc_, ss_, gi) in enumerate(((cN, sN, 2), (c32, s32, 0), (c32, s32, 1))):
            k.op("dve", lambda e: e.tensor_tensor(out=w1_, in0=cc_, in1=ge_re[:, :, gi], op=ALU.mult), tmax(ssets[0].t_gcopy, ssets[1].t_gcopy, g_done))
            k.op("dve", lambda e: e.tensor_tensor(out=w2_, in0=ss_, in1=ge_im[:, :, gi], op=ALU.mult))
            k.op("dve", lambda e: e.tensor_tensor(out=hfin[:, :, si, 0], in0=w1_, in1=w2_, op=ALU.subtract))
            k.op("dve", lambda e: e.tensor_tensor(out=w1_, in0=ss_, in1=ge_re[:, :, gi], op=ALU.mult))
            k.op("dve", lambda e: e.tensor_tensor(out=w2_, in0=cc_, in1=ge_im[:, :, gi], op=ALU.mult))
            v = k.op("dve", lambda e: e.tensor_tensor(out=hfin[:, :, si, 1], in0=w1_, in1=w2_, op=ALU.add))
        fsem = k.dsem()
        hf_tok = k.dma("sp", hf_o, hfin.rearrange("p a b c -> p (a b c)"), fsem, [v])

        ar.top = BASE
        y2 = ar.alloc([128, 32, T], BF16)
        wg = [ar.alloc([128, 32, 128], BF16) for _ in range(2)]
        wz = [ar.alloc([128, 16, 128], BF16) for _ in range(2)]
        sg = [ar.alloc([128, 366]) for _ in range(2)]
        zs = [ar.alloc([128, 366]) for _ in range(2)]
        guard = tmax(hf_tok, g_done, lsets[0].t_yread, v)
        wg_free = [guard, guard]
        tmp_free = [guard, guard]
        step = 0
        y2_done = None
        for ob in range(32):
            s = ob % 2
            wlg = k.dma("pool", wg[s].rearrange("p a b -> p (a b)"), wgl[ob], wsem[s], tmax(wg_free[s]), max_dma_last_dim=8192)
            wlz = k.dma("pool", wz[s].rearrange("p a b -> p (a b)"), wsz[ob], wsem[2 + s], max_dma_last_dim=8192)
            p = None
            for (c0, c1) in RANGES:
                n = c1 - c0
                st_ = step % 2
                step += 1
                bg_, bz_ = st_ * 2, st_ * 2 + 1
                for kb in range(32):
                    p = k.op("pe", lambda e: e.matmul(ps[:, bg_, 0:n], lhsT=wg[s][:, kb, :], rhs=g_all[:, kb, c0:c1],
                                                      start=(kb == 0), stop=(kb == 31)), tmax(wlg, g_done, bank_free[bg_]), inc=False)
                for kc in range(16):
                    p = k.op("pe", lambda e: e.matmul(ps[:, bz_, 0:n], lhsT=wz[s][:, kc, :], rhs=xn1[:, kc, c0:c1],
                                                      start=(kc == 0), stop=(kc == 15)), tmax(wlz, bank_free[bz_]), inc=(kc == 15))
                a1 = k.op("act", lambda e: e.activation(out=sg[st_][:, 0:n], in_=ps[:, bg_, 0:n], func=ACT.Sigmoid,
                                                        bias=bgl[:, ob:ob + 1]), tmax(p, tmp_free[st_]))
                a2 = k.op("act", lambda e: e.activation(out=zs[st_][:, 0:n], in_=ps[:, bz_, 0:n], func=ACT.Silu))
                bank_free[bg_] = a2
                bank_free[bz_] = a2
                v = k.op("dve", lambda e: e.tensor_tensor(out=sg[st_][:, 0:n], in0=sg[st_][:, 0:n], in1=g_all[:, ob, c0:c1], op=ALU.mult), [a2])
                v = k.op("dve", lambda e: e.tensor_tensor(out=y2[:, ob, c0:c1], in0=sg[st_][:, 0:n], in1=zs[st_][:, 0:n], op=ALU.mult))
                tmp_free[st_] = v
                y2_done = v
            wg_free[s] = p
        glu_pe_done = p

        ar.top = 0
        h2T = ar.alloc([128, 16, T])
        wo = [ar.alloc([128, 32, 128], BF16) for _ in range(2)]
        h1c = [ar.alloc([128, T]) for _ in range(2)]
        assert ar.top <= BASE
        ar.top = BASE + 17568
        xt = [ar.alloc([128, D]) for _ in range(2)]
        sq2 = [ar.alloc([128, T]) for _ in range(2)]
        rb2 = ar.alloc([128, T])
        ones2 = ar.alloc([128, 128])
        guard = tmax(glu_pe_done, y2_done)
        vm2 = k.op("dve", lambda e: e.memset(ones2, 1.0 / D), guard)
        wo_free = [guard, guard]
        h1c_free = [guard, guard]
        h1sem = [k.dsem(), k.dsem()]
        last_h = None
        for db in range(16):
            s = db % 2
            wl = k.dma("pool", wo[s].rearrange("p a b -> p (a b)"), wso[db], wsem[s], tmax(wo_free[s]), max_dma_last_dim=8192)
            hl = k.dma("sp", h1c[s], h1T_i[:, db * T:(db + 1) * T], h1sem[s], tmax(h1c_free[s]))
            p = None
            for ri, (c0, c1) in enumerate(RANGES):
                n = c1 - c0
                bk = 2 + (db % 2) * 3 + ri
                for eb in range(32):
                    p = k.op("pe", lambda e: e.matmul(ps[:, bk, 0:n], lhsT=wo[s][:, eb, :], rhs=y2[:, eb, c0:c1],
                                                      start=(eb == 0), stop=(eb == 31)), tmax(wl, y2_done, bank_free[bk]), inc=(eb == 31))
                bank_free[bk] = k.op("dve", lambda e: e.tensor_tensor(out=h2T[:, db, c0:c1], in0=ps[:, bk, 0:n], in1=h1c[s][:, c0:c1],
                                                                     op=ALU.add), tmax(p, hl, guard))
                last_h = bank_free[bk]
            wo_free[s] = p
            h1c_free[s] = last_h
        sq_free = [guard, guard]
        p = None
        for db in range(16):
            s = db % 2
            a = k.op("act", lambda e: e.activation(out=sq2[s], in_=h2T[:, db, :], func=ACT.Square), tmax(last_h, sq_free[s]))
            for ri, (c0, c1) in enumerate(RANGES):
                n = c1 - c0
                p = k.op("pe", lambda e: e.matmul(ps[:, 2 + ri, 0:n], lhsT=ones2, rhs=sq2[s][:, c0:c1], start=(db == 0), stop=(db == 15)),
                         tmax(a, vm2, bank_free[2 + ri], last_h), inc=(ri == 2))
            sq_free[s] = p
        v = None
        for ri, (c0, c1) in enumerate(RANGES):
            n = c1 - c0
            v = k.op("dve", lambda e: e.tensor_scalar(out=rb2[:, c0:c1], in0=ps[:, 2 + ri, 0:n], scalar1=EPS, scalar2=None, op0=ALU.add), [p])
            bank_free[2 + ri] = v
        a = k.op("act", lambda e: e.activation(out=rb2, in_=rb2, func=ACT.Sqrt), [v])
        v = k.op("dve", lambda e: e.reciprocal(out=rb2, in_=rb2), [a])
        for db in range(16):
            v = k.op("dve", lambda e: e.scalar_tensor_tensor(out=h2T[:, db, :], in0=h2T[:, db, :], scalar=nw[:, 32 + db:33 + db],
                                                            in1=rb2, op0=ALU.mult, op1=ALU.mult))
        yn_done = v
        osem = [k.dsem(), k.dsem()]
        xt_free = [None, None]
        outs = []
        for i in range(NTT):
            r0 = 128 * i
            nr = min(128, T - r0)
            s = i % 2
            p = None
            a3 = None
            for g in range(4):
                bk = g % 2
                for q in range(4):
                    kc = 4 * g + q
                    p = k.op("pe", lambda e: e.transpose(out=ps[0:nr, bk, q * 128:(q + 1) * 128], in_=h2T[:, kc, r0:r0 + nr],
                                                         identity=ident), tmax(yn_done, bank_free[bk]), inc=(q == 3))
                a3 = k.op("act", lambda e: e.copy(out=xt[s][0:nr, 512 * g:512 * (g + 1)], in_=ps[0:nr, bk, :]), tmax(p, xt_free[s]))
                bank_free[bk] = a3
            od = k.dma("sp", y_o[r0:r0 + nr, :], xt[s][0:nr, :], osem[s], [a3])
            xt_free[s] = od
            outs.append(od)
        k.wait("sp", tmax(hf_tok, outs[-1], outs[-2]))
    return nc

def host_layout_a(inp):
    f = np.float32
    xp = inp["x_prompt"][0]
    P = np.concatenate([np.zeros((16, D), f), inp["meta_tokens"].astype(f), xp], axis=0)
    Ppad = np.concatenate([np.zeros((2, D), f), P], axis=0)
    xs = inp["x_sample"]
    z2 = np.zeros((2, D), f)
    xin = []
    for c in range(NCORE):
        rows = [Ppad[SEG * c:SEG * c + SEG + 2], z2, xs[2 * c], z2, xs[2 * c + 1]]
        xin.append(np.ascontiguousarray(np.concatenate(rows, axis=0)))
    nw = np.concatenate([inp["norm_w"][0].reshape(16, 128).T, inp["norm_w"][1].reshape(16, 128).T], axis=1)
    wci = inp["conv_w_in"][0].reshape(16, 128, 4, 32, 128)
    wci = np.ascontiguousarray(wci.transpose(3, 1, 0, 2, 4)).reshape(32, 128, 16 * 512)
    cwv = np.concatenate([inp["conv_w"][0], inp["conv_b"]], axis=0)
    cw = np.ascontiguousarray(cwv.reshape(4, 32, 128).transpose(2, 1, 0)).reshape(128, 128)
    cache = inp["cache_conv"][0]
    ccs = []
    for c in range(NCORE):
        a = cache[2 * c:2 * c + 2].reshape(2, 2, 32, 128)
        ccs.append(np.ascontiguousarray(a.transpose(3, 2, 0, 1)).reshape(128, 128))
    wco = inp["conv_w_out"][0].reshape(32, 128, 16, 128)
    wco = np.ascontiguousarray(wco.transpose(2, 1, 0, 3)).reshape(16, 128, 32 * 128)
    sm = lambda a: np.ascontiguousarray(a.reshape(NPAIR, 2, 64).transpose(1, 2, 0)).reshape(128, NPAIR)
    are = sm(inp["ssm_a_re"][0])
    aim = sm(inp["ssm_a_im"][0])
    ldt = sm(np.repeat(inp["ssm_log_dt"][0][:, None], 64, axis=1))
    smb = lambda a: np.ascontiguousarray(a.reshape(NPAIR, 2, 64, 16).transpose(1, 2, 0, 3)).reshape(128, NPAIR * 16)
    bre = smb(inp["ssm_b_re"][0])
    bim = smb(inp["ssm_b_im"][0])
    wsu = inp["ssm_w_in"][0][:, :E].reshape(16, 128, 32, 128)
    wsu = np.ascontiguousarray(wsu.transpose(2, 1, 0, 3)).reshape(32, 128, 16 * 128)
    io = np.zeros((T,), f)
    io[C_P0:C_P0 + SEG] = np.arange(1, SEG + 1)
    io[C_A0:C_A0 + 32] = np.arange(1, 33)
    io[C_B0:C_B0 + 32] = np.arange(1, 33)
    iota = np.ascontiguousarray(np.broadcast_to(io[None, :], (128, T)))
    shared = dict(ident=np.eye(128, dtype=f), nw=np.ascontiguousarray(nw.astype(f)), wci=wci, cw=cw, wco=wco, are=are, aim=aim,
                  ldt=ldt, bre=bre, bim=bim, wsu=wsu, iota=iota)
    return [dict(shared, xin=xin[c], cc=ccs[c]) for c in range(NCORE)]


def host_layout_b(inp, res_a):
    f = np.float32
    base = host_layout_a.cache
    f0 = [np.asarray(r["f0"]).reshape(128, NPAIR, 2) for r in res_a]
    nw48 = np.ascontiguousarray(np.concatenate([base["nw"], inp["final_norm_w"].astype(f).reshape(16, 128).T], axis=1))

    def cbd(cm):
        c4 = cm.reshape(NPAIR, 2, 16, 64)
        out = np.zeros((2, 64, NPAIR, 2, 16), f)
        for g2 in range(2):
            out[g2, :, :, g2, :] = c4[:, g2].transpose(2, 0, 1)
        return out.reshape(128, NPAIR * 32)
    cbr = cbd(inp["ssm_c_re"][0])
    cbi = cbd(inp["ssm_c_im"][0])
    dvec = np.ascontiguousarray(inp["ssm_d"][0].reshape(32, 128).T)
    bgl = np.ascontiguousarray(inp["ssm_b_glu"][0].reshape(32, 128).T)
    wsz = inp["ssm_w_in"][0][:, E:].reshape(16, 128, 32, 128)
    wsz = np.ascontiguousarray(wsz.transpose(2, 1, 0, 3)).reshape(32, 128, 16 * 128)
    wgl = inp["ssm_w_glu"][0].reshape(32, 128, 32, 128)
    wgl = np.ascontiguousarray(wgl.transpose(2, 1, 0, 3)).reshape(32, 128, 32 * 128)
    wso = inp["ssm_w_out"][0].reshape(32, 128, 16, 128)
    wso = np.ascontiguousarray(wso.transpose(2, 1, 0, 3)).reshape(16, 128, 32 * 128)
    ioc = np.concatenate([np.arange(1, 9), np.arange(1, 9), np.arange(1, 258)]).astype(f)
    iotc = np.ascontiguousarray(np.broadcast_to(ioc[None, :], (128, 273)))
    shared = dict(ident=base["ident"], nw=nw48, iotc=iotc, are=base["are"], aim=base["aim"], ldt=base["ldt"], bre=base["bre"],
                  bim=base["bim"], cbr=cbr, cbi=cbi, dvec=dvec, bgl=bgl, wsu=base["wsu"], wsz=wsz, wgl=wgl, wso=wso)
    maps = []
    for c in range(NCORE):
        fp = np.zeros((128, 7, NPAIR, 2), f)
        for j in range(7):
            if c - 1 - j >= 0:
                fp[:, j] = f0[c - 1 - j]
        hs = []
        for st in (inp["state_ssm_re"][0], inp["state_ssm_im"][0]):
            a = st[2 * c:2 * c + 2].reshape(2, NPAIR, 2, 64).transpose(2, 3, 1, 0)
            hs.append(np.ascontiguousarray(a).reshape(128, NPAIR * 2))
        maps.append(dict(shared, h1T_i=np.asarray(res_a[c]["h1T"]), f0prev=fp.reshape(128, 7 * NPAIR * 2), h0r=hs[0], h0i=hs[1]))
    return maps


def assemble(inp, res_a, res_b):
    f = np.float32
    y_prompt = np.zeros((1, 8192, D), f)
    y_sample = np.zeros((16, 32, D), f)
    ncp = np.zeros((1, 1, 2, E), f)
    ncs = np.zeros((1, 16, 2, E), f)
    hpr = np.zeros((1, 1, 256, 64), f)
    hpi = np.zeros((1, 1, 256, 64), f)
    hsr = np.zeros((1, 16, 256, 64), f)
    hsi = np.zeros((1, 16, 256, 64), f)
    for c in range(NCORE):
        yo = np.asarray(res_b[c]["yout"])
        lo = SEG * c - 32
        c0 = C_P0 + max(0, -lo)
        y_prompt[0, max(lo, 0):lo + SEG] = yo[c0:C_P0 + SEG]
        y_sample[2 * c] = yo[C_A0:C_A0 + 32]
        y_sample[2 * c + 1] = yo[C_B0:C_B0 + 32]
        ncv = np.asarray(res_a[c]["ncv"]).reshape(128, 32, 3, 2)
        hf = np.asarray(res_b[c]["hfin"]).reshape(2, 64, NPAIR, 3, 2)
        hf = hf.transpose(3, 4, 2, 0, 1).reshape(3, 2, 256, 64)
        if c == NCORE - 1:
            ncp[0, 0] = ncv[:, :, 0, :].transpose(2, 1, 0).reshape(2, E)
            hpr[0, 0] = hf[0, 0]
            hpi[0, 0] = hf[0, 1]
        for s in range(2):
            ncs[0, 2 * c + s] = ncv[:, :, 1 + s, :].transpose(2, 1, 0).reshape(2, E)
            hsr[0, 2 * c + s] = hf[1 + s, 0]
            hsi[0, 2 * c + s] = hf[1 + s, 1]
    return (y_prompt, y_sample, ncp, ncs, hpr, hpi, hsr, hsi)


_PROGS = {}


def kernel(**inp):
    inp = {k_: np.asarray(v) for k_, v in inp.items()}
    maps_a = host_layout_a(inp)
    host_layout_a.cache = maps_a[0]
    if "a" not in _PROGS:
        _PROGS["a"] = build_a()
    res_a = run_bass_kernel_spmd(_PROGS["a"], maps_a, core_ids=list(range(NCORE))).results
    maps_b = host_layout_b(inp, res_a)
    if "b" not in _PROGS:
        _PROGS["b"] = build_b()
    res_b = run_bass_kernel_spmd(_PROGS["b"], maps_b, core_ids=list(range(NCORE))).results
    return assemble(inp, res_a, res_b)
```
